# Optimizing a Trainium2 kernel written in Bass

```python
import jax, jax.numpy as jnp
from jax import lax
import numpy as np

D_MODEL = 1024
BATCH = 16
SEQ = 4096
DEPTH = 2
DEC_BATCH = 16
DEC_SEQ = 2048
PAST_LEN = 128

A_GROUPS = ((128, 1), (512, 4), (2048, 16))
A_HEADS = 8
A_HEAD_DIM = 64
A_WIDTH = A_HEADS * A_HEAD_DIM
A_QKV_W = len(A_GROUPS) * A_WIDTH
B_HEADS = 4
B_DK = 128
B_DV = 256
B_KW = B_HEADS * B_DK
B_VW = B_HEADS * B_DV
B_GATE_RANK = 16
B_GATE_TAU = 16.0
B_CHUNK = 64
D_FF = 4 * D_MODEL
EPS = 1e-6
NEG = -1e30

IN_SPLITS = [A_QKV_W, A_QKV_W, A_QKV_W,
             B_KW, B_KW, B_VW, B_VW,
             B_GATE_RANK, B_GATE_RANK,
             D_MODEL, D_MODEL]
IN_COLS = int(sum(IN_SPLITS))
IN_SPLIT_POINTS = [int(c) for c in np.cumsum(IN_SPLITS)[:-1]]

kernel_name = "hybrid_dilated_gla_encoder"


def rmsnorm(x, g):
    xf = x.astype(jnp.float32)
    y = xf * lax.rsqrt(jnp.mean(jnp.square(xf), axis=-1, keepdims=True) + EPS)
    return (y * g.astype(jnp.float32)).astype(x.dtype)


def alibi_slopes(n):
    return jnp.exp2(-8.0 * jnp.arange(1, n + 1, dtype=jnp.float32) / n)


def dilated_window_attention(q, k, v, slopes, window, dilation):
    bsz, seq, nh, hd = q.shape
    half = window // (2 * dilation)
    blk = half
    sub = seq // dilation
    nb = -(-sub // blk)
    subp = nb * blk
    grp = bsz * dilation

    def to_sub(t):
        return t.reshape(bsz, sub, dilation, nh, hd).transpose(0, 2, 1, 3, 4).reshape(grp, sub, nh, hd)

    qs = jnp.pad(to_sub(q), ((0, 0), (0, subp - sub), (0, 0), (0, 0))).reshape(grp, nb, blk, nh, hd)

    def key_windows(t):
        tp = jnp.pad(to_sub(t), ((0, 0), (blk, subp - sub + blk), (0, 0), (0, 0)))
        tp = tp.reshape(grp, nb + 2, blk, nh, hd)
        return jnp.concatenate([tp[:, :-2], tp[:, 1:-1], tp[:, 2:]], axis=2)

    kw = key_windows(k)
    vw = key_windows(v).astype(jnp.float32)
    qi = jnp.arange(blk)[:, None]
    ci = jnp.arange(3 * blk)[None, :]
    delta = ci - blk - qi
    jk = (jnp.arange(nb)[:, None, None] - 1) * blk + ci[None]
    valid = (jnp.abs(delta) <= half)[None] & (jk >= 0) & (jk < sub)
    dist = (jnp.abs(delta) * dilation).astype(jnp.float32)
    bias = -slopes[:, None, None] * dist[None]
    s = jnp.einsum('gnqhe,gnkhe->gnhqk', qs, kw, preferred_element_type=jnp.float32) * (hd ** -0.5)
    s = jnp.where(valid[None, :, None], s + bias[None, None], NEG)
    m = jnp.max(s, axis=-1, keepdims=True)
    p = jnp.exp(s - m)
    l = jnp.sum(p, axis=-1, keepdims=True)
    o = jnp.einsum('gnhqk,gnkhe->gnqhe', p, vw)
    o = o / jnp.moveaxis(l[..., 0], 2, 3)[..., None]
    lse = jnp.moveaxis((m + jnp.log(l))[..., 0], 2, 3)
    o = o.reshape(grp, subp, nh, hd)[:, :sub].reshape(bsz, dilation, sub, nh, hd)
    o = o.transpose(0, 2, 1, 3, 4).reshape(bsz, seq, nh, hd)
    lse = lse.reshape(grp, subp, nh)[:, :sub].reshape(bsz, dilation, sub, nh)
    lse = lse.transpose(0, 2, 1, 3).reshape(bsz, seq, nh)
    return o, lse


def mixer_a(qa, ka, va):
    bsz, seq, _ = qa.shape
    ng = len(A_GROUPS)
    slopes = alibi_slopes(ng * A_HEADS)
    shp = (bsz, seq, ng, A_HEADS, A_HEAD_DIM)
    qa, ka, va = qa.reshape(shp), ka.reshape(shp), va.reshape(shp)
    outs, lses = [], []
    for g, (win, dil) in enumerate(A_GROUPS):
        o, lse = dilated_window_attention(qa[:, :, g], ka[:, :, g], va[:, :, g],
                                          slopes[g * A_HEADS:(g + 1) * A_HEADS], win, dil)
        outs.append(o)
        lses.append(lse)
    wgt = jax.nn.softmax(jnp.stack(lses, axis=0), axis=0)
    o = jnp.sum(jnp.stack(outs, axis=0) * wgt[..., None], axis=0)
    return o.reshape(bsz, seq, A_WIDTH).astype(qa.dtype)


def gla_direction(q, k, v, log_a, strict):
    bsz, seq, nh, dk = q.shape
    dv = v.shape[-1]
    c = B_CHUNK
    n = seq // c
    q = q.reshape(bsz, n, c, nh, dk)
    k = k.reshape(bsz, n, c, nh, dk)
    v = v.reshape(bsz, n, c, nh, dv)
    b = jnp.cumsum(log_a.reshape(bsz, n, c, nh, dk), axis=2)
    b_last = b[:, :, -1:]
    qe = q * jnp.exp(b)
    ke = k * jnp.exp(-b)
    kd = k * jnp.exp(b_last - b)
    att = jnp.einsum('bnchk,bnshk->bnhcs', qe, ke)
    mask = jnp.tril(jnp.ones((c, c), dtype=bool), k=-1 if strict else 0)
    att = jnp.where(mask, att, 0.0)
    o_intra = jnp.einsum('bnhcs,bnshv->bnchv', att, v)

    def step(state, xs):
        qe_n, kd_n, v_n, dec_n = xs
        o_n = jnp.einsum('bchk,bhkv->bchv', qe_n, state)
        state = state * dec_n[..., None] + jnp.einsum('bchk,bchv->bhkv', kd_n, v_n)
        return state, o_n

    xs = (jnp.moveaxis(qe, 1, 0), jnp.moveaxis(kd, 1, 0), jnp.moveaxis(v, 1, 0),
          jnp.moveaxis(jnp.exp(b_last[:, :, 0]), 1, 0))
    _, o_inter = lax.scan(step, jnp.zeros((bsz, nh, dk, dv), jnp.float32), xs)
    o = o_intra + jnp.moveaxis(o_inter, 0, 1)
    return o.reshape(bsz, seq, nh, dv)


def mixer_b(qb, kb, vb, rb, glf, glb, w_gate_f, b_gate_f, w_gate_b, b_gate_b, norm_g):
    bsz, seq, _ = qb.shape
    f32 = jnp.float32
    q = qb.astype(f32).reshape(bsz, seq, B_HEADS, B_DK) * (B_DK ** -0.5)
    k = kb.astype(f32).reshape(bsz, seq, B_HEADS, B_DK)
    v = vb.astype(f32).reshape(bsz, seq, B_HEADS, B_DV)

    def log_decay(lr, w, bias):
        z = jnp.einsum('bsr,rk->bsk', lr.astype(f32), w.astype(f32)) + bias.astype(f32)
        return (jax.nn.log_sigmoid(z) / B_GATE_TAU).reshape(bsz, seq, B_HEADS, B_DK)

    def flip(t):
        return jnp.flip(t, axis=1)

    o_f = gla_direction(q, k, v, log_decay(glf, w_gate_f, b_gate_f), strict=False)
    o_b = flip(gla_direction(flip(q), flip(k), flip(v), flip(log_decay(glb, w_gate_b, b_gate_b)), strict=True))
    o = o_f + o_b
    o = o * lax.rsqrt(jnp.mean(jnp.square(o), axis=-1, keepdims=True) + EPS) * norm_g.astype(f32)
    o = o.reshape(bsz, seq, B_VW) * jax.nn.silu(rb.astype(f32))
    return o.astype(qb.dtype)


def encoder_layer(x, norm_mix_g, w_in, w_gate_f, b_gate_f, w_gate_b, b_gate_b, gla_norm_g,
                  w_branch_a, w_branch_b, w_out, norm_ffn_g, w_ff1, w_ff2):
    h = rmsnorm(x, norm_mix_g)
    proj = jnp.einsum('bsd,dc->bsc', h, w_in)
    qa, ka, va, qb, kb, vb, rb, glf, glb, ga, gb = jnp.split(proj, IN_SPLIT_POINTS, axis=-1)
    ya = jnp.einsum('bsc,cd->bsd', mixer_a(qa, ka, va), w_branch_a)
    yb = jnp.einsum('bsc,cd->bsd', mixer_b(qb, kb, vb, rb, glf, glb, w_gate_f, b_gate_f,
                                           w_gate_b, b_gate_b, gla_norm_g), w_branch_b)
    merged = jax.nn.sigmoid(ga) * ya + jax.nn.sigmoid(gb) * yb
    x = x + jnp.einsum('bsd,de->bse', merged, w_out)
    h = rmsnorm(x, norm_ffn_g)
    hid = jnp.square(jax.nn.relu(jnp.einsum('bsd,df->bsf', h, w_ff1)))
    return x + jnp.einsum('bsf,fd->bsd', hid, w_ff2)


def trunk(x, norm_mix_g, w_in, gla_w_gate_fwd, gla_b_gate_fwd, gla_w_gate_bwd, gla_b_gate_bwd,
          gla_norm_g, w_branch_a, w_branch_b, w_out, norm_ffn_g, w_ff1, w_ff2, final_norm_g):
    for l in range(DEPTH):
        x = encoder_layer(x, norm_mix_g[l], w_in[l], gla_w_gate_fwd[l], gla_b_gate_fwd[l],
                          gla_w_gate_bwd[l], gla_b_gate_bwd[l], gla_norm_g[l], w_branch_a[l],
                          w_branch_b[l], w_out[l], norm_ffn_g[l], w_ff1[l], w_ff2[l])
    return rmsnorm(x, final_norm_g)


def setup_inputs(seed: int = 0) -> dict:
    key = jax.random.key(seed)
    ks = jax.random.split(key, 16)
    f32 = jnp.float32

    def nrm(k, shape, scale):
        return jax.random.normal(k, shape, f32) * scale

    return {
        "x_prompt": nrm(ks[0], (BATCH, SEQ, D_MODEL), 1.0),
        "x_sample": nrm(ks[1], (DEC_BATCH, DEC_SEQ, D_MODEL), 1.0),
        "norm_mix_g": 1.0 + nrm(ks[2], (DEPTH, D_MODEL), 0.02),
        "w_in": nrm(ks[3], (DEPTH, D_MODEL, IN_COLS), D_MODEL ** -0.5),
        "gla_w_gate_fwd": nrm(ks[4], (DEPTH, B_GATE_RANK, B_KW), B_GATE_RANK ** -0.5),
        "gla_b_gate_fwd": nrm(ks[5], (DEPTH, B_KW), 0.1),
        "gla_w_gate_bwd": nrm(ks[6], (DEPTH, B_GATE_RANK, B_KW), B_GATE_RANK ** -0.5),
        "gla_b_gate_bwd": nrm(ks[7], (DEPTH, B_KW), 0.1),
        "gla_norm_g": 1.0 + nrm(ks[8], (DEPTH, B_DV), 0.02),
        "w_branch_a": nrm(ks[9], (DEPTH, A_WIDTH, D_MODEL), A_WIDTH ** -0.5),
        "w_branch_b": nrm(ks[10], (DEPTH, B_VW, D_MODEL), B_VW ** -0.5),
        "w_out": nrm(ks[11], (DEPTH, D_MODEL, D_MODEL), D_MODEL ** -0.5),
        "norm_ffn_g": 1.0 + nrm(ks[12], (DEPTH, D_MODEL), 0.02),
        "w_ff1": nrm(ks[13], (DEPTH, D_MODEL, D_FF), D_MODEL ** -0.5),
        "w_ff2": nrm(ks[14], (DEPTH, D_FF, D_MODEL), D_FF ** -0.5),
        "final_norm_g": 1.0 + nrm(ks[15], (D_MODEL,), 0.02),
    }


def reference(x_prompt, x_sample, norm_mix_g, w_in, gla_w_gate_fwd, gla_b_gate_fwd,
              gla_w_gate_bwd, gla_b_gate_bwd, gla_norm_g, w_branch_a, w_branch_b, w_out,
              norm_ffn_g, w_ff1, w_ff2, final_norm_g):
    y_prompt = trunk(x_prompt, norm_mix_g, w_in, gla_w_gate_fwd, gla_b_gate_fwd, gla_w_gate_bwd,
                     gla_b_gate_bwd, gla_norm_g, w_branch_a, w_branch_b, w_out, norm_ffn_g,
                     w_ff1, w_ff2, final_norm_g)
    y_sample = trunk(x_sample, norm_mix_g, w_in, gla_w_gate_fwd, gla_b_gate_fwd, gla_w_gate_bwd,
                     gla_b_gate_bwd, gla_norm_g, w_branch_a, w_branch_b, w_out, norm_ffn_g,
                     w_ff1, w_ff2, final_norm_g)
    return (y_prompt, y_sample)
```

```python
import numpy as np
from contextlib import ExitStack
import concourse.bass as bass
import concourse.mybir as mybir
from concourse.bass_utils import run_bass_kernel_spmd

F32 = mybir.dt.float32
F32R = mybir.dt.float32r
AF = mybir.ActivationFunctionType
ALU = mybir.AluOpType
AX = mybir.AxisListType

D = 1024
DEPTH = 2
A_GROUPS = ((128, 1), (512, 4), (2048, 16))
IN_COLS = 9760
C_QA, C_KA, C_VA, C_QB, C_KB, C_VB, C_RB, C_GL, C_GA, C_GB = 0, 1536, 3072, 4608, 5120, 5632, 6656, 7680, 7712, 8736
D_FF = 4096
EPS = 1e-6
NEG = -1e30
PAD = 1024
OAW = 66
N_CORES = 8
CUT = 99


class Buf:
    __slots__ = ("name", "w", "r", "excl")

    def __init__(self, name, excl=False):
        self.name = name
        self.w = None
        self.r = {}
        self.excl = excl


class Ctx:
    WINDOW = 4
    NRING = 12

    def __init__(self, nc, es):
        self.nc = nc
        self.es = es
        self.eng = {"pe": nc.tensor, "dve": nc.vector, "act": nc.scalar,
                    "pool": nc.gpsimd, "sp": nc.sync}
        self.sem = {k: es.enter_context(nc.semaphore("s_" + k)) for k in self.eng}
        self.cnt = {k: 0 for k in self.eng}
        self.seen = {k: {} for k in self.eng}
        self.pend = {k: ([], []) for k in self.eng}
        self.ring = {q: [es.enter_context(nc.semaphore("r_%s%d" % (q, i))) for i in range(self.NRING)]
                     for q in ("sp", "pool")}
        self.ringcnt = {"sp": 0, "pool": 0}
        self.uid = 0

    def sb(self, name, shape, dtype=F32):
        self.uid += 1
        t = self.es.enter_context(self.nc.sbuf_tensor("%s_%d" % (name, self.uid), list(shape), dtype))
        return t, Buf(name)

    def _semof(self, key):
        if isinstance(key, str):
            return self.sem[key]
        return self.ring[key[1]][key[2]]

    def _wait(self, e, key, val):
        if self.seen[e].get(key, 0) >= val:
            return
        if key == e:
            if e == "pe":
                return
            if val <= self.cnt[e] - self.WINDOW:
                return
        self.eng[e].wait_ge(self._semof(key), val)
        self.seen[e][key] = val

    def _deps(self, e, reads, writes):
        for b in reads:
            if b.w is not None:
                self._wait(e, b.w[0], b.w[1])
            if b.excl:
                for k, v in b.r.items():
                    if k != e:
                        self._wait(e, k, v)
        for b in writes:
            if b.w is not None:
                self._wait(e, b.w[0], b.w[1])
            for k, v in b.r.items():
                self._wait(e, k, v)

    def _mark(self, ev, reads, writes):
        for b in reads:
            if b.r.get(ev[0], 0) < ev[1]:
                b.r[ev[0]] = ev[1]
        for b in writes:
            b.w = ev
            b.r = {}

    def op(self, e, fn, reads=(), writes=(), last=True):
        self._deps(e, reads, writes)
        ins = fn(self.eng[e])
        pr, pw = self.pend[e]
        pr.extend(reads)
        pw.extend(writes)
        if last:
            self.cnt[e] += 1
            ins.then_inc(self.sem[e], 1)
            self._mark((e, self.cnt[e]), pr, pw)
            self.pend[e] = ([], [])
        return ins

    def dma(self, q, out, in_, reads=(), writes=()):
        self._deps(q, reads, writes)
        i = self.ringcnt[q]
        slot, rnd = i % self.NRING, i // self.NRING
        key = ("q", q, slot)
        if rnd > 0:
            self._wait(q, key, 16 * rnd)
        self.eng[q].dma_start(out=out, in_=in_).then_inc(self.ring[q][slot], 16)
        self.ringcnt[q] += 1
        self._mark((key, 16 * (rnd + 1)), reads, writes)

    def barrier(self, final=False):
        for e in self.eng:
            assert not self.pend[e][0] and not self.pend[e][1]
        evs = [(k, self.cnt[k]) for k in self.eng if self.cnt[k] > 0]
        for q in ("sp", "pool"):
            n = self.ringcnt[q]
            for slot in range(min(n, self.NRING)):
                rounds = (n - slot + self.NRING - 1) // self.NRING
                evs.append((("q", q, slot), 16 * rounds))
        engs = ["sp"] if final else list(self.eng)
        for e in engs:
            for k, v in evs:
                if k == e or self.seen[e].get(k, 0) >= v:
                    continue
                self.eng[e].wait_ge(self._semof(k), v)
                self.seen[e][k] = v


def ssl(start, count, step):
    return slice(start, start + (count - 1) * step + 1, step)


class Rot:
    def __init__(self, items):
        self.items = items
        self.i = 0

    def next(self):
        it = self.items[self.i % len(self.items)]
        self.i += 1
        return it


def host_consts():
    q = np.arange(128)[:, None]
    kk = np.arange(256)[None, :]
    delta = kk - 64 - q
    ad = np.abs(delta).astype(np.float64)
    valid = ad <= 64
    nh = 24
    slopes = np.exp2(-8.0 * np.arange(1, nh + 1, dtype=np.float64) / nh)
    abias = np.zeros((nh, 128, 256), np.float32)
    for g, (win, dil) in enumerate(A_GROUPS):
        for h in range(8):
            b = -slopes[g * 8 + h] * ad * dil
            abias[g * 8 + h] = np.where(valid, b, NEG).astype(np.float32)
    edge = np.zeros((2, 128, 256), np.float32)
    edge[0][:, :64] = NEG
    edge[1][:, 192:] = NEG
    s = np.arange(128)[:, None]
    t = np.arange(128)[None, :]
    tri = np.zeros((2, 128, 128), np.float32)
    tri[0] = np.where(s <= t, -1.0 / 16.0, 0.0)
    tri[1] = np.where(s >= t, -1.0 / 16.0, 0.0)
    msk = np.zeros((2, 128, 128), np.float32)
    msk[0] = (s <= t)
    msk[1] = (s > t)
    return {"c_ident": np.eye(128, dtype=np.float32), "c_abias": abias, "c_edge": edge,
            "c_tri": tri, "c_msk": msk, "c_ones": np.ones((1, 512), np.float32)}


def build_program(seqs, depth=DEPTH, dbg=False, phases="ABCcDE"):
    nc = bass.Bass("TRN2", target_bir_lowering=False)
    TOK = sum(seqs)
    offs = [sum(seqs[:i]) for i in range(len(seqs))]
    SMAX = max(seqs)

    def din(name, shape):
        return nc.dram_tensor(name, list(shape), F32, kind="ExternalInput").ap()

    def dscr(name, shape):
        return nc.dram_tensor(name, list(shape), F32, kind=("ExternalOutput" if dbg else "Internal")).ap()

    x_in = din("x", [TOK, D])
    W = {
        "norm_mix_g": din("norm_mix_g", [depth, D]),
        "w_in": din("w_in", [depth, D, IN_COLS]),
        "gwf": din("gla_w_gate_fwd", [depth, 16, 512]),
        "gbf": din("gla_b_gate_fwd", [depth, 512]),
        "gwb": din("gla_w_gate_bwd", [depth, 16, 512]),
        "gbb": din("gla_b_gate_bwd", [depth, 512]),
        "gng": din("gla_norm_g", [depth, 256]),
        "wa": din("w_branch_a", [depth, 512, D]),
        "wb": din("w_branch_b", [depth, D, D]),
        "wo": din("w_out", [depth, D, D]),
        "nfg": din("norm_ffn_g", [depth, D]),
        "w1": din("w_ff1", [depth, D, D_FF]),
        "w2": din("w_ff2", [depth, D_FF, D]),
        "fng": din("final_norm_g", [D]),
    }
    c_ident = din("c_ident", [128, 128])
    c_abias = din("c_abias", [24, 128, 256])
    c_edge = din("c_edge", [2, 128, 256])
    c_tri = din("c_tri", [2, 128, 128])
    c_msk = din("c_msk", [2, 128, 128])
    c_ones = din("c_ones", [1, 512])
    y_out = nc.dram_tensor("y", [TOK, D], F32, kind="ExternalOutput").ap()

    S_ = {
        "qaT": dscr("s_qaT", [1536, TOK]), "kaT": dscr("s_kaT", [1536, TOK]), "va": dscr("s_va", [TOK, 1536]),
        "qbT": dscr("s_qbT", [512, TOK]), "kbT": dscr("s_kbT", [512, TOK]), "vb": dscr("s_vb", [TOK, 1024]),
        "rb": dscr("s_rb", [TOK, 1024]), "lrT": dscr("s_lrT", [32, TOK]),
        "gaT": dscr("s_gaT", [1024, TOK]), "gbT": dscr("s_gbT", [1024, TOK]),
        "oag": dscr("s_oag", [TOK, 24, OAW]), "obb": dscr("s_obb", [TOK, 1024]), "ob": dscr("s_ob", [TOK, 1024]),
        "xa": dscr("s_xa", [TOK, D]), "xb": dscr("s_xb", [TOK, D]),
    }

    with ExitStack() as ges:
        c = Ctx(nc, ges)
        ident, identb = c.sb("ident", [128, 128])
        c.dma("sp", ident[:], c_ident, writes=[identb])
        banks = []
        for i in range(8):
            t = ges.enter_context(nc.psum_tensor("psb%d" % i, [128, 512], F32))
            banks.append((t, Buf("psb%d" % i, excl=True)))
        evac_rr = [0]

        def evac(out, in_, reads, writes, eng=None):
            if eng is None:
                eng = ("act", "dve")[evac_rr[0] % 2]
                evac_rr[0] += 1
            if eng == "act":
                c.op("act", lambda e: e.activation(out=out, in_=in_, func=AF.Copy), reads=reads, writes=writes)
            else:
                c.op("dve", lambda e: e.tensor_copy(out, in_), reads=reads, writes=writes)

        def rms_stats(xt_ap, xb, junk, jb, st, stb, n):
            c.op("act", lambda e: e.activation(out=junk, in_=xt_ap, func=AF.Square, accum_out=st[:, 0:1]),
                 reads=[xb], writes=[jb, stb])
            c.op("act", lambda e: e.activation(out=st[:, 1:2], in_=st[:, 0:1], func=AF.Ln, scale=1.0 / n, bias=EPS),
                 reads=[stb], writes=[stb])
            c.op("act", lambda e: e.activation(out=st[:, 2:3], in_=st[:, 1:2], func=AF.Exp, scale=-0.5),
                 reads=[stb], writes=[stb])

        def transpose_to(src, srcb, nchunks, dstT, dstb, tsl, psrot):
            for c0 in range(0, nchunks, 4):
                nn = min(4, nchunks - c0)
                ps, psb = psrot.next()
                for j in range(nn):
                    c.op("pe", lambda e, j=j: e.transpose(ps[:, j * 128:(j + 1) * 128],
                                                          src[:, (c0 + j) * 128:(c0 + j + 1) * 128], ident[:]),
                         reads=[srcb, identb], writes=[psb], last=(j == nn - 1))
                evac(dstT[:, c0:c0 + nn, tsl], ps[:, 0:nn * 128].rearrange("p (a b) -> p a b", a=nn),
                     reads=[psb], writes=[dstb])

        def phase_A(l, xsrc):
            TA = 1024 if TOK % 1024 == 0 else 512
            panels = []
            for (c0, n, form, dst) in ((C_QA, 1536, "F", "qaT"), (C_KA, 1536, "F", "kaT"), (C_VA, 1536, "T", "va"),
                                       (C_QB, 512, "F", "qbT"), (C_KB, 512, "F", "kbT"), (C_VB, 1024, "T", "vb"),
                                       (C_RB, 1024, "T", "rb"), (C_GL, 32, "F", "lrT"), (C_GA, 1024, "F", "gaT"),
                                       (C_GB, 1024, "F", "gbT")):
                for p0 in range(0, n, 512):
                    panels.append((c0 + p0, min(512, n - p0), form, dst, p0))
            with ExitStack() as es:
                c.es = es
                gbc, gbcb = c.sb("gbc", [128, D])
                c.dma("sp", gbc[:], W["norm_mix_g"][l].partition_broadcast(128), writes=[gbcb])
                hTh, hThb = c.sb("hTh", [128, 8, TA], F32R)
                hTl, hTlb = c.sb("hTl", [128, 8, TA], F32R)
                xts = Rot([c.sb("xt", [128, D]) for _ in range(2)])
                hts = Rot([c.sb("ht", [128, D]) for _ in range(2)])
                sts = Rot([c.sb("st", [128, 3]) for _ in range(2)])
                junk, junkb = c.sb("junk", [128, D])
                wp = Rot([c.sb("wpan", [128, 8, 512]) for _ in range(2)])
                wph = Rot([c.sb("wph", [128, 8, 512], F32R) for _ in range(2)])
                wpl = Rot([c.sb("wpl", [128, 8, 512], F32R) for _ in range(2)])
                stg = Rot([c.sb("stg", [128, 512]) for _ in range(4)])
                psrot = Rot(banks)
                jobs = [(blk, p) for blk in range(TOK // TA) for p in panels]
                slots = {}

                rawsA = {}

                def dmaA(n):
                    if n >= len(jobs):
                        return
                    blk, (c0, ncols, form, dst, p0) = jobs[n]
                    sl, slb = wp.next()
                    c.dma("sp", sl[:, :, 0:ncols], W["w_in"][l][:, c0:c0 + ncols].rearrange("(kc p) c -> p kc c", p=128),
                          writes=[slb])
                    rawsA[n] = (sl, slb)

                def load(n):
                    blk, (c0, ncols, form, dst, p0) = jobs[n]
                    sl, slb = rawsA.pop(n)
                    wh, whb = wph.next()
                    wl, wlb = wpl.next()
                    c.op("act", lambda e: e.activation(out=wh[:, :, 0:ncols], in_=sl[:, :, 0:ncols], func=AF.Copy),
                         reads=[slb], writes=[whb])
                    c.op("dve", lambda e: e.tensor_tensor(out=wl[:, :, 0:ncols], in0=sl[:, :, 0:ncols],
                                                           in1=wh[:, :, 0:ncols].bitcast(F32), op=ALU.subtract),
                         reads=[slb, whb], writes=[wlb])
                    slots[n] = (wh, whb, wl, wlb)

                def mm3(ps_ap, psb, A, B, first, lastk):
                    (ah, ahb), (al, alb) = A
                    (bh, bhb), (bl_, blb_) = B
                    c.op("pe", lambda e: e.matmul(ps_ap, ah, bh, start=first, stop=False), reads=[ahb, bhb], writes=[psb], last=False)
                    c.op("pe", lambda e: e.matmul(ps_ap, ah, bl_, start=False, stop=False), reads=[ahb, blb_], writes=[psb], last=False)
                    c.op("pe", lambda e: e.matmul(ps_ap, al, bh, start=False, stop=lastk), reads=[alb, bhb], writes=[psb], last=lastk)

                dmaA(0)
                load(0)
                dmaA(1)
                for n, (blk, (c0, ncols, form, dst, p0)) in enumerate(jobs):
                    t0 = blk * TA
                    if n % len(panels) == 0:
                        for ti in range(TA // 128):
                            xt, xb = xts.next()
                            ht, hb = hts.next()
                            st, stb = sts.next()
                            c.dma("sp", xt[:], xsrc[t0 + ti * 128:t0 + (ti + 1) * 128, :], writes=[xb])
                            rms_stats(xt[:], xb, junk[:], junkb, st, stb, D)
                            c.op("dve", lambda e: e.scalar_tensor_tensor(out=ht[:], in0=xt[:], scalar=st[:, 2:3], in1=gbc[:],
                                                                         op0=ALU.mult, op1=ALU.mult),
                                 reads=[xb, stb, gbcb], writes=[hb])
                            tsl = slice(ti * 128, (ti + 1) * 128)
                            for c0_ in range(0, 8, 4):
                                ps, psb = psrot.next()
                                for j in range(4):
                                    c.op("pe", lambda e, j=j: e.transpose(ps[:, j * 128:(j + 1) * 128],
                                                                          ht[:, (c0_ + j) * 128:(c0_ + j + 1) * 128], ident[:]),
                                         reads=[hb, identb], writes=[psb], last=(j == 3))
                                ps3 = ps[:, :].rearrange("p (a b) -> p a b", a=4)
                                c.op("act", lambda e: e.activation(out=hTh[:, c0_:c0_ + 4, tsl], in_=ps3, func=AF.Copy),
                                     reads=[psb], writes=[hThb])
                                c.op("dve", lambda e: e.tensor_tensor(out=hTl[:, c0_:c0_ + 4, tsl], in0=ps3,
                                                                      in1=hTh[:, c0_:c0_ + 4, tsl].bitcast(F32), op=ALU.subtract),
                                     reads=[psb, hThb], writes=[hTlb])
                    if n + 1 < len(jobs):
                        load(n + 1)
                        dmaA(n + 2)
                    wh, whb, wl, wlb = slots.pop(n)
                    if form == "F":
                        for cc in range((ncols + 127) // 128):
                            m = min(128, ncols - cc * 128)
                            csl = slice(cc * 128, cc * 128 + m)
                            for tb in range(TA // 512):
                                ps, psb = psrot.next()
                                tsl = slice(tb * 512, (tb + 1) * 512)
                                for kc in range(8):
                                    mm3(ps[0:m, :], psb, ((wh[:, kc, csl], whb), (wl[:, kc, csl], wlb)),
                                        ((hTh[:, kc, tsl], hThb), (hTl[:, kc, tsl], hTlb)), kc == 0, kc == 7)
                                sg, sgb = stg.next()
                                evac(sg[0:m, :], ps[0:m, :], reads=[psb], writes=[sgb])
                                c.dma("pool", S_[dst][p0 + cc * 128:p0 + cc * 128 + m, t0 + tb * 512:t0 + (tb + 1) * 512],
                                      sg[0:m, :], reads=[sgb])
                    else:
                        for ti in range(TA // 128):
                            ps, psb = psrot.next()
                            tsl = slice(ti * 128, (ti + 1) * 128)
                            for kc in range(8):
                                mm3(ps[:, 0:ncols], psb, ((hTh[:, kc, tsl], hThb), (hTl[:, kc, tsl], hTlb)),
                                    ((wh[:, kc, 0:ncols], whb), (wl[:, kc, 0:ncols], wlb)), kc == 0, kc == 7)
                            sg, sgb = stg.next()
                            evac(sg[:, 0:ncols], ps[:, 0:ncols], reads=[psb], writes=[sgb])
                            c.dma("pool", S_[dst][t0 + ti * 128:t0 + (ti + 1) * 128, p0:p0 + ncols], sg[:, 0:ncols],
                                  reads=[sgb])
                c.barrier()
            c.es = ges

        def phase_B(l):
            with ExitStack() as es:
                c.es = es
                NT = max(d * (S // (d * 128) + 1) for S in seqs for (_, d) in A_GROUPS)
                QT = Rot([c.sb("QT", [64, SMAX]) for _ in range(2)])
                KT = Rot([c.sb("KT", [64, SMAX + 2 * PAD]) for _ in range(2)])
                VP = Rot([c.sb("VP", [128, NT, 128]) for _ in range(2)])
                BI = Rot([c.sb("bias", [128, 256]) for _ in range(3)])
                edge, edgeb = c.sb("edge", [128, 2, 256])
                c.dma("sp", edge[:], c_edge.rearrange("a p k -> p a k"), writes=[edgeb])
                for (kt, ktb) in KT.items:
                    c.op("pool", lambda e: e.memset(kt[:, 0:PAD], 0.0), writes=[ktb])
                Tt = Rot([c.sb("T", [128, 256]) for _ in range(2)])
                Pt = Rot([c.sb("P", [128, 256]) for _ in range(3)])
                PTt = Rot([c.sb("PT", [128, 256]) for _ in range(3)])
                sm = Rot([c.sb("sm", [128, 4]) for _ in range(5)])
                ot = Rot([c.sb("ot", [128, OAW]) for _ in range(3)])
                qhs = Rot([c.sb("qh", [64, 128], F32R) for _ in range(3)])
                qls = Rot([c.sb("ql", [64, 128], F32R) for _ in range(3)])
                khs = Rot([c.sb("kh", [64, 256], F32R) for _ in range(3)])
                kls = Rot([c.sb("kl", [64, 256], F32R) for _ in range(3)])
                psS = Rot(banks[0:2])
                psT = Rot(banks[2:4])
                psO = Rot(banks[4:6])
                jobs = []
                for si, S in enumerate(seqs):
                    for g, (win, d) in enumerate(A_GROUPS):
                        for h in range(8):
                            jobs.append((si, S, g, d, h))
                st = {}

                def load(n):
                    si, S, g, d, h = jobs[n]
                    o = offs[si]
                    sub = S // d
                    nt = sub // 128
                    if h % 2 == 0:
                        vp, vpb = VP.next()
                        st["vp"] = (vp, vpb)
                        vcol = C_QA + g * 512 + (h // 2) * 128
                        vv = vp[:, 0:d * (nt + 1), :].rearrange("p (r a) c -> p r a c", r=d)
                        c.op("pool", lambda e: e.memset(vv[0:64, :, 0, :], 0.0), writes=[vpb])
                        c.op("pool", lambda e: e.memset(vv[64:128, :, nt, :], 0.0), writes=[vpb])
                        va = S_["va"]
                        c.dma("sp", vv[64:128, :, 0, :],
                              va[o:o + 64 * d, vcol:vcol + 128].rearrange("(k r) c -> k r c", r=d), writes=[vpb])
                        c.dma("sp", vv[0:64, :, nt, :],
                              va[o + S - 64 * d:o + S, vcol:vcol + 128].rearrange("(k r) c -> k r c", r=d), writes=[vpb])
                        if nt > 1:
                            for r in range(d):
                                c.dma("sp", vv[:, r, 1:nt, :],
                                      va[ssl(o + 64 * d + r, (nt - 1) * 128, d), vcol:vcol + 128]
                                      .rearrange("(a p) c -> p a c", p=128), writes=[vpb])
                    qt, qtb = QT.next()
                    kt, ktb = KT.next()
                    bi, bib = BI.next()
                    col = g * 512 + h * 64
                    c.dma("sp", qt[:, 0:S], S_["qaT"][col:col + 64, o:o + S], writes=[qtb])
                    c.op("pool", lambda e: e.memset(kt[:, PAD + S:PAD + S + PAD], 0.0), writes=[ktb])
                    c.dma("sp", kt[:, PAD:PAD + S], S_["kaT"][col:col + 64, o:o + S], writes=[ktb])
                    c.dma("sp", bi[:], c_abias[g * 8 + h], writes=[bib])
                    st[n] = (qt, qtb, kt, ktb, bi, bib) + st["vp"]

                tiles = []
                for n, (si, S, g, d, h) in enumerate(jobs):
                    nt = (S // d) // 128
                    first = True
                    for r in range(d):
                        for j in range(nt):
                            tiles.append((n, r, j, first))
                            first = False
                NTL = len(tiles)
                res = {}
                TS = {}

                def stage0(k):
                    n, r, j, first = tiles[k]
                    if first:
                        if n == 0:
                            load(0)
                        if n + 1 < len(jobs):
                            load(n + 1)
                        res[n] = st.pop(n)
                        res.pop(n - 2, None)
                    si, S, g, d, h = jobs[n]
                    qt, qtb, kt, ktb, bi, bib, vp, vpb = res[n]
                    q0 = r + d * 128 * j
                    k0 = PAD + r + d * (128 * j - 64)
                    qh, qhb = qhs.next()
                    ql, qlb = qls.next()
                    kh, khb = khs.next()
                    kl, klb = kls.next()
                    c.op("act", lambda e: e.activation(out=qh[:], in_=qt[:, ssl(q0, 128, d)], func=AF.Copy), reads=[qtb], writes=[qhb])
                    c.op("act", lambda e: e.activation(out=kh[:], in_=kt[:, ssl(k0, 256, d)], func=AF.Copy), reads=[ktb], writes=[khb])
                    c.op("dve", lambda e: e.tensor_tensor(out=ql[:], in0=qt[:, ssl(q0, 128, d)], in1=qh[:].bitcast(F32), op=ALU.subtract),
                         reads=[qtb, qhb], writes=[qlb])
                    c.op("dve", lambda e: e.tensor_tensor(out=kl[:], in0=kt[:, ssl(k0, 256, d)], in1=kh[:].bitcast(F32), op=ALU.subtract),
                         reads=[ktb, khb], writes=[klb])
                    TS[k] = {"qk": (qh, qhb, ql, qlb, kh, khb, kl, klb)}

                def stage1a(k):
                    n, r, j, first = tiles[k]
                    si, S, g, d, h = jobs[n]
                    qt, qtb, kt, ktb, bi, bib, vp, vpb = res[n]
                    qh, qhb, ql, qlb, kh, khb, kl, klb = TS[k]["qk"]
                    ps, psb = psS.next()
                    c.op("pe", lambda e: e.matmul(ps[:, 0:256], qh[:], kh[:], start=True, stop=False), reads=[qhb, khb], writes=[psb], last=False)
                    c.op("pe", lambda e: e.matmul(ps[:, 0:256], qh[:], kl[:], start=False, stop=False), reads=[qhb, klb], writes=[psb], last=False)
                    c.op("pe", lambda e: e.matmul(ps[:, 0:256], ql[:], kh[:], start=False, stop=True), reads=[qlb, khb], writes=[psb])
                    T, Tb = Tt.next()
                    c.op("dve", lambda e: e.scalar_tensor_tensor(out=T[:], in0=ps[:, 0:256], scalar=0.125, in1=bi[:],
                                                                 op0=ALU.mult, op1=ALU.add),
                         reads=[psb, bib], writes=[Tb])
                    nt = (S // d) // 128
                    if j == 0:
                        c.op("dve", lambda e: e.tensor_tensor(out=T[:], in0=T[:], in1=edge[:, 0, :], op=ALU.add),
                             reads=[Tb, edgeb], writes=[Tb])
                    if j == nt - 1:
                        c.op("dve", lambda e: e.tensor_tensor(out=T[:], in0=T[:], in1=edge[:, 1, :], op=ALU.add),
                             reads=[Tb, edgeb], writes=[Tb])
                    TS[k].update({"T": (T, Tb), "s4": sm.next(), "P": Pt.next()})

                def stage1b(k):
                    T, Tb = TS[k]["T"]
                    s4, s4b = TS[k]["s4"]
                    P, Pb = TS[k]["P"]
                    c.op("dve", lambda e: e.reduce_max(out=s4[:, 0:1], in_=T[:], axis=AX.X, negate=True),
                         reads=[Tb], writes=[s4b])
                    c.op("act", lambda e: e.activation(out=P[:], in_=T[:], func=AF.Exp, bias=s4[:, 0:1], scale=1.0,
                                                       accum_out=s4[:, 1:2]),
                         reads=[Tb, s4b], writes=[Pb, s4b])

                def stage2(k):
                    P, Pb = TS[k]["P"]
                    pt, ptb = psT.next()
                    for b in range(2):
                        c.op("pe", lambda e, b=b: e.transpose(pt[:, b * 128:(b + 1) * 128], P[:, b * 128:(b + 1) * 128], ident[:]),
                             reads=[Pb, identb], writes=[ptb], last=(b == 1))
                    PT, PTb = PTt.next()
                    c.op("act", lambda e: e.activation(out=PT[:], in_=pt[:, 0:256], func=AF.Copy), reads=[ptb], writes=[PTb])
                    TS[k]["PT"] = (PT, PTb)

                def stage3a(k):
                    n, r, j, first = tiles[k]
                    si, S, g, d, h = jobs[n]
                    nt = (S // d) // 128
                    hh = h % 2
                    vp, vpb = res[n][6], res[n][7]
                    PT, PTb = TS[k]["PT"]
                    s4, s4b = TS[k]["s4"]
                    po, pob = psO.next()
                    for b in range(2):
                        c.op("pe", lambda e, b=b: e.matmul(po[:, 0:64], PT[:, b * 128:(b + 1) * 128],
                                                           vp[:, r * (nt + 1) + j + b, hh * 64:(hh + 1) * 64],
                                                           start=(b == 0), stop=(b == 1)),
                             reads=[PTb, vpb], writes=[pob], last=(b == 1))
                    c.op("dve", lambda e: e.reciprocal(s4[:, 2:3], s4[:, 1:2]), reads=[s4b], writes=[s4b])
                    c.op("act", lambda e: e.activation(out=s4[:, 3:4], in_=s4[:, 1:2], func=AF.Ln), reads=[s4b], writes=[s4b])
                    TS[k]["po"] = (po, pob)

                def stage3b(k):
                    n, r, j, first = tiles[k]
                    si, S, g, d, h = jobs[n]
                    o = offs[si]
                    s4, s4b = TS[k]["s4"]
                    po, pob = TS[k]["po"]
                    o_t, o_tb = ot.next()
                    c.op("dve", lambda e: e.tensor_scalar(o_t[:, 0:64], po[:, 0:64], s4[:, 2:3], None, ALU.mult),
                         reads=[pob, s4b], writes=[o_tb])
                    c.op("dve", lambda e: e.tensor_tensor(out=o_t[:, 64:65], in0=s4[:, 3:4], in1=s4[:, 0:1], op=ALU.subtract),
                         reads=[s4b], writes=[o_tb])
                    tq = o + r + d * 128 * j
                    c.dma("pool", S_["oag"][ssl(tq, 128, d), g * 8 + h, 0:65], o_t[:, 0:65], reads=[o_tb])
                    del TS[k]

                stage0(0)
                for k in range(NTL + 2):
                    if k + 1 < NTL:
                        stage0(k + 1)
                    if k < NTL:
                        stage1a(k)
                    if 0 <= k - 2 < NTL:
                        stage3a(k - 2)
                    if k < NTL:
                        stage1b(k)
                    if 0 <= k - 1 < NTL:
                        stage2(k - 1)
                    if 0 <= k - 2 < NTL:
                        stage3b(k - 2)
                c.barrier()
            c.es = ges

        def phase_C(l, direction):
            fwd = (direction == 0)
            LNS = float(np.log(128.0 ** -0.5))
            with ExitStack() as es:
                c.es = es
                waug, waugb = c.sb("waug", [32, 512])
                c.op("pool", lambda e: e.memset(waug[:], 0.0), writes=[waugb])
                c.dma("sp", waug[0:16, :], W["gwf" if fwd else "gwb"][l], writes=[waugb])
                c.dma("sp", waug[16:17, :], W["gbf" if fwd else "gbb"][l].unsqueeze(0), writes=[waugb])
                tri, trib = c.sb("tri", [128, 128])
                msk, mskb = c.sb("msk", [128, 128])
                c.dma("sp", tri[:], c_tri[direction], writes=[trib])
                c.dma("sp", msk[:], c_msk[direction], writes=[mskb])
                gnb, gnbb = c.sb("gnb", [128, 256])
                if fwd:
                    c.dma("sp", gnb[:], W["gng"][l].partition_broadcast(128), writes=[gnbb])
                BT = 512
                qbl = Rot([c.sb("qbl", [128, 4, BT]) for _ in range(2)])
                kbl = Rot([c.sb("kbl", [128, 4, BT]) for _ in range(2)])
                lra = Rot([c.sb("lra", [32, BT]) for _ in range(2)])
                for (t_, b_) in lra.items:
                    c.op("pool", lambda e: e.memset(t_[:], 0.0), writes=[b_])
                    c.dma("sp", t_[16:17, :], c_ones, writes=[b_])
                vts = Rot([c.sb("vt", [128, 1024]) for _ in range(3)])
                rbs = Rot([c.sb("rbt", [128, 1024]) for _ in range(3)]) if fwd else None
                obs = Rot([c.sb("obt", [128, 1024]) for _ in range(3)]) if fwd else None
                e1s = Rot([c.sb("e1", [128, 512]) for _ in range(2)])
                Ls = Rot([c.sb("L", [128, 512]) for _ in range(3)])
                eqs = Rot([c.sb("eq", [128, 512]) for _ in range(2)])
                eks = Rot([c.sb("ek", [128, 512]) for _ in range(2)])
                bls = Rot([c.sb("bl", [128, 8]) for _ in range(5)])
                qes = Rot([c.sb("qe", [128, 512]) for _ in range(4)])
                kes = Rot([c.sb("ke", [128, 512]) for _ in range(4)])
                kdTs = Rot([c.sb("kdT", [128, 512]) for _ in range(3)])
                kds = Rot([c.sb("kd", [128, 512]) for _ in range(3)])
                atts = Rot([c.sb("att", [128, 512]) for _ in range(2)])
                osb = Rot([c.sb("osb", [128, 1024]) for _ in range(2)])
                sil = Rot([c.sb("sil", [128, 1024]) for _ in range(2)]) if fwd else None
                junk, junkb = c.sb("junkc", [128, 256])
                nst = Rot([c.sb("nst", [128, 12]) for _ in range(2)])
                states = [c.sb("state", [128, 4, 256]) for _ in range(2)]
                cur = [0]
                psZ, psB_, psKD, psA = banks[0], banks[1], banks[2], banks[3]
                psO = (banks[4], banks[5])
                psSt = (banks[6], banks[7])
                lastcol = 127 if fwd else 0

                for si, S in enumerate(seqs):
                    o = offs[si]
                    nblk = S // BT
                    tiles = []
                    for bi_ in (range(nblk) if fwd else range(nblk - 1, -1, -1)):
                        tl = range(BT // 128)
                        for ti in (tl if fwd else reversed(tl)):
                            tiles.append((bi_, ti))
                    c.op("dve", lambda e: e.memset(states[cur[0]][0][:], 0.0), writes=[states[cur[0]][1]])
                    G = {}
                    blkbuf = {}

                    GS = {}

                    def g1(idx):
                        bi_, ti = tiles[idx]
                        t0 = o + bi_ * BT
                        if bi_ not in blkbuf:
                            qb_, qbb = qbl.next()
                            kb_, kbb = kbl.next()
                            lr_, lrb = lra.next()
                            c.dma("sp", qb_[:], S_["qbT"][:, t0:t0 + BT].rearrange("(h p) t -> p h t", p=128), writes=[qbb])
                            c.dma("sp", kb_[:], S_["kbT"][:, t0:t0 + BT].rearrange("(h p) t -> p h t", p=128), writes=[kbb])
                            c.dma("sp", lr_[0:16, :], S_["lrT"][direction * 16:direction * 16 + 16, t0:t0 + BT], writes=[lrb])
                            blkbuf.clear()
                            blkbuf[bi_] = (qb_, qbb, kb_, kbb, lr_, lrb)
                        qb_, qbb, kb_, kbb, lr_, lrb = blkbuf[bi_]
                        tsl = slice(ti * 128, (ti + 1) * 128)
                        pz, pzb = psZ
                        c.op("pe", lambda e: e.matmul(pz[:, :], lr_[:, tsl], waug[:], start=True, stop=True),
                             reads=[lrb, waugb], writes=[pzb])
                        e1, e1b = e1s.next()
                        Lt, Lb = Ls.next()
                        c.op("act", lambda e: e.activation(out=e1[:], in_=pz[:, :], func=AF.Exp, scale=-1.0),
                             reads=[pzb], writes=[e1b])
                        c.op("act", lambda e: e.activation(out=Lt[:], in_=e1[:], func=AF.Ln, bias=1.0),
                             reads=[e1b], writes=[Lb])
                        GS[idx] = {"blk": (qb_, qbb, kb_, kbb), "tsl": tsl, "L": (Lt, Lb), "tok": t0 + ti * 128}

                    def g2(idx):
                        d_ = GS[idx]
                        qb_, qbb, kb_, kbb = d_["blk"]
                        tsl = d_["tsl"]
                        Lt, Lb = d_["L"]
                        pb, pbb = psB_
                        for h in range(4):
                            c.op("pe", lambda e, h=h: e.matmul(pb[:, h * 128:(h + 1) * 128], Lt[:, h * 128:(h + 1) * 128], tri[:],
                                                               start=True, stop=True),
                                 reads=[Lb, trib], writes=[pbb], last=(h == 3))
                        bl, blb = bls.next()
                        pb3 = pb[:, :].rearrange("p (h t) -> p h t", h=4)
                        c.op("dve", lambda e: e.tensor_copy(bl[:, 0:4], pb3[:, :, lastcol]), reads=[pbb], writes=[blb])
                        eq, eqb = eqs.next()
                        ek, ekb = eks.next()
                        c.op("act", lambda e: e.activation(out=eq[:], in_=pb[:, :], func=AF.Exp, bias=LNS, scale=1.0),
                             reads=[pbb], writes=[eqb])
                        c.op("act", lambda e: e.activation(out=ek[:], in_=pb[:, :], func=AF.Exp, scale=-1.0),
                             reads=[pbb], writes=[ekb])
                        c.op("act", lambda e: e.activation(out=bl[:, 4:8], in_=bl[:, 0:4], func=AF.Exp),
                             reads=[blb], writes=[blb])
                        qe, qeb = qes.next()
                        ke, keb = kes.next()
                        kdT, kdTb = kdTs.next()
                        c.op("dve", lambda e: e.tensor_tensor(out=qe[:].rearrange("p (h t) -> p h t", h=4), in0=qb_[:, :, tsl],
                                                              in1=eq[:].rearrange("p (h t) -> p h t", h=4), op=ALU.mult),
                             reads=[qbb, eqb], writes=[qeb])
                        c.op("dve", lambda e: e.tensor_tensor(out=ke[:].rearrange("p (h t) -> p h t", h=4), in0=kb_[:, :, tsl],
                                                              in1=ek[:].rearrange("p (h t) -> p h t", h=4), op=ALU.mult),
                             reads=[kbb, ekb], writes=[keb])
                        for h in range(4):
                            c.op("dve", lambda e, h=h: e.tensor_scalar(kdT[:, h * 128:(h + 1) * 128], ke[:, h * 128:(h + 1) * 128],
                                                                       bl[:, 4 + h:5 + h], None, ALU.mult),
                                 reads=[keb, blb], writes=[kdTb])
                        d_.update({"bl": (bl, blb), "qe": (qe, qeb), "ke": (ke, keb), "kdT": (kdT, kdTb)})

                    def g3(idx):
                        d_ = GS.pop(idx)
                        tok = d_["tok"]
                        vt, vtb = vts.next()
                        c.dma("sp", vt[:], S_["vb"][tok:tok + 128, :], writes=[vtb])
                        ext = None
                        if fwd:
                            rbt, rbb = rbs.next()
                            obt, obb_ = obs.next()
                            c.dma("sp", rbt[:], S_["rb"][tok:tok + 128, :], writes=[rbb])
                            c.dma("sp", obt[:], S_["obb"][tok:tok + 128, :], writes=[obb_])
                            ext = (rbt, rbb, obt, obb_)
                        kdT, kdTb = d_["kdT"]
                        pk, pkb = psKD
                        for h in range(4):
                            c.op("pe", lambda e, h=h: e.transpose(pk[:, h * 128:(h + 1) * 128], kdT[:, h * 128:(h + 1) * 128], ident[:]),
                                 reads=[kdTb, identb], writes=[pkb], last=(h == 3))
                        kd, kdb = kds.next()
                        c.op("act", lambda e: e.activation(out=kd[:], in_=pk[:, :], func=AF.Copy), reads=[pkb], writes=[kdb])
                        bl, blb = d_["bl"]
                        qe, qeb = d_["qe"]
                        ke, keb = d_["ke"]
                        G[idx] = (tok, vt, vtb, ext, bl, blb, qe, qeb, ke, keb, kd, kdb)

                    def statepart(idx):
                        if CUT < 5:
                            return
                        tok, vt, vtb, ext, bl, blb, qe, qeb, ke, keb, kd, kdb = G.pop(idx)
                        state, stateb = states[cur[0]]
                        snew, snewb = states[1 - cur[0]]
                        cur[0] = 1 - cur[0]
                        pa, pab = psA
                        for h in range(4):
                            c.op("pe", lambda e, h=h: e.matmul(pa[:, h * 128:(h + 1) * 128], ke[:, h * 128:(h + 1) * 128],
                                                               qe[:, h * 128:(h + 1) * 128], start=True, stop=True),
                                 reads=[keb, qeb], writes=[pab], last=(h == 3))
                        for h in range(4):
                            pst, pstb = psSt[h // 2]
                            osl = slice((h % 2) * 256, (h % 2 + 1) * 256)
                            c.op("pe", lambda e, h=h: e.matmul(pst[:, osl], kd[:, h * 128:(h + 1) * 128], vt[:, h * 256:(h + 1) * 256],
                                                               start=True, stop=True),
                                 reads=[kdb, vtb], writes=[pstb], last=(h % 2 == 1))
                        att, attb = atts.next()
                        c.op("dve", lambda e: e.tensor_tensor(out=att[:].rearrange("p (h t) -> p h t", h=4),
                                                              in0=pa[:, :].rearrange("p (h t) -> p h t", h=4),
                                                              in1=msk[:].unsqueeze(1).to_broadcast([128, 4, 128]), op=ALU.mult),
                             reads=[pab, mskb], writes=[attb])
                        for h in range(4):
                            pst, pstb = psSt[h // 2]
                            osl = slice((h % 2) * 256, (h % 2 + 1) * 256)
                            c.op("dve", lambda e, h=h: e.scalar_tensor_tensor(out=snew[:, h, :], in0=state[:, h, :],
                                                                              scalar=bl[:, 4 + h:5 + h], in1=pst[:, osl],
                                                                              op0=ALU.mult, op1=ALU.add),
                                 reads=[stateb, blb, pstb], writes=[snewb])
                        for h in range(4):
                            po, pob = psO[h // 2]
                            osl = slice((h % 2) * 256, (h % 2 + 1) * 256)
                            c.op("pe", lambda e, h=h: e.matmul(po[:, osl], att[:, h * 128:(h + 1) * 128], vt[:, h * 256:(h + 1) * 256],
                                                               start=True, stop=False),
                                 reads=[attb, vtb], writes=[pob], last=False)
                            c.op("pe", lambda e, h=h: e.matmul(po[:, osl], qe[:, h * 128:(h + 1) * 128], state[:, h, :],
                                                               start=False, stop=True),
                                 reads=[qeb, stateb], writes=[pob], last=(h % 2 == 1))
                        if CUT < 8:
                            return
                        ob_, obb2 = osb.next()
                        if not fwd:
                            evac(ob_[:, 0:512], psO[0][0][:, :], reads=[psO[0][1]], writes=[obb2], eng="act")
                            evac(ob_[:, 512:1024], psO[1][0][:, :], reads=[psO[1][1]], writes=[obb2], eng="dve")
                            c.dma("pool", S_["obb"][tok:tok + 128, :], ob_[:], reads=[obb2])
                        else:
                            rbt, rbb, obt, obb_ = ext
                            for k in range(2):
                                c.op("dve", lambda e, k=k: e.tensor_tensor(out=ob_[:, k * 512:(k + 1) * 512], in0=psO[k][0][:, :],
                                                                           in1=obt[:, k * 512:(k + 1) * 512], op=ALU.add),
                                     reads=[psO[k][1], obb_], writes=[obb2])
                            ns, nsb = nst.next()
                            for h in range(4):
                                c.op("act", lambda e, h=h: e.activation(out=junk[:], in_=ob_[:, h * 256:(h + 1) * 256], func=AF.Square,
                                                                        accum_out=ns[:, h:h + 1]),
                                     reads=[obb2], writes=[junkb, nsb])
                            c.op("act", lambda e: e.activation(out=ns[:, 4:8], in_=ns[:, 0:4], func=AF.Ln, scale=1.0 / 256, bias=EPS),
                                 reads=[nsb], writes=[nsb])
                            c.op("act", lambda e: e.activation(out=ns[:, 8:12], in_=ns[:, 4:8], func=AF.Exp, scale=-0.5),
                                 reads=[nsb], writes=[nsb])
                            sl_, slb_ = sil.next()
                            c.op("act", lambda e: e.activation(out=sl_[:], in_=rbt[:], func=AF.Silu), reads=[rbb], writes=[slb_])
                            c.op("pool", lambda e: e.tensor_tensor(out=sl_[:].rearrange("p (h v) -> p h v", h=4),
                                                                   in0=sl_[:].rearrange("p (h v) -> p h v", h=4),
                                                                   in1=gnb[:].unsqueeze(1).to_broadcast([128, 4, 256]), op=ALU.mult),
                                 reads=[slb_, gnbb], writes=[slb_])
                            c.op("dve", lambda e: e.tensor_tensor(out=ob_[:].rearrange("p (h v) -> p h v", h=4),
                                                                  in0=ob_[:].rearrange("p (h v) -> p h v", h=4),
                                                                  in1=ns[:, 8:12].unsqueeze(2).to_broadcast([128, 4, 256]), op=ALU.mult),
                                 reads=[obb2, nsb], writes=[obb2])
                            c.op("dve", lambda e: e.tensor_tensor(out=ob_[:], in0=ob_[:], in1=sl_[:], op=ALU.mult),
                                 reads=[obb2, slb_], writes=[obb2])
                            c.dma("pool", S_["ob"][tok:tok + 128, :], ob_[:], reads=[obb2])

                    NTI = len(tiles)
                    for it in range(-3, NTI):
                        if 0 <= it + 3 < NTI:
                            g1(it + 3)
                        if 0 <= it + 2 < NTI:
                            g2(it + 2)
                        if 0 <= it + 1 < NTI:
                            g3(it + 1)
                        if 0 <= it < NTI:
                            statepart(it)
                c.barrier()
            c.es = ges

        def phase_D(l, xsrc):
            BT = 512
            with ExitStack() as es:
                c.es = es
                wo, wob = c.sb("wo", [128, 8, D])
                c.dma("sp", wo[:], W["wo"][l].rearrange("(kc p) c -> p kc c", p=128), writes=[wob])
                oags = Rot([c.sb("oag", [128, 24, OAW]) for _ in range(2)])
                obts = Rot([c.sb("obt", [128, 1024]) for _ in range(2)])
                oas = Rot([c.sb("oa", [128, 512]) for _ in range(2)])
                tm1 = Rot([c.sb("tm1", [128, 512]) for _ in range(2)])
                tm2 = Rot([c.sb("tm2", [128, 512]) for _ in range(2)])
                cs = Rot([c.sb("cs", [128, 64]) for _ in range(2)])
                oaT, oaTb = c.sb("oaT", [128, 4, BT])
                obT, obTb = c.sb("obT", [128, 8, BT])
                gaT, gaTb = c.sb("gaT", [128, 8, BT])
                gbT, gbTb = c.sb("gbT", [128, 8, BT])
                mT, mTb = c.sb("mT", [128, 8, BT])
                xt, xtb = c.sb("xtD", [128, 4, D])
                wap = Rot([c.sb("wap", [128, 4, 256]) for _ in range(2)])
                wbp = Rot([c.sb("wbp", [128, 8, 256]) for _ in range(2)])
                tmpm = Rot([c.sb("tmpm", [128, 512]) for _ in range(2)])
                psrot = Rot(banks)
                nblk = TOK // BT
                jobs = [(blk, pc) for blk in range(nblk) for pc in range(4)]
                slots = {}

                def loadw(n):
                    blk, pc = jobs[n]
                    a_, ab = wap.next()
                    b_, bb = wbp.next()
                    c.dma("sp", a_[:], W["wa"][l][:, pc * 256:(pc + 1) * 256].rearrange("(kc p) c -> p kc c", p=128), writes=[ab])
                    c.dma("sp", b_[:], W["wb"][l][:, pc * 256:(pc + 1) * 256].rearrange("(kc p) c -> p kc c", p=128), writes=[bb])
                    slots[n] = (a_, ab, b_, bb)

                def gates(blk):
                    t0 = blk * BT
                    c.dma("sp", gaT[:], S_["gaT"][:, t0:t0 + BT].rearrange("(kc p) t -> p kc t", p=128), writes=[gaTb])
                    c.dma("sp", gbT[:], S_["gbT"][:, t0:t0 + BT].rearrange("(kc p) t -> p kc t", p=128), writes=[gbTb])
                    c.op("act", lambda e: e.activation(out=gaT[:], in_=gaT[:], func=AF.Sigmoid), reads=[gaTb], writes=[gaTb])
                    c.op("act", lambda e: e.activation(out=gbT[:], in_=gbT[:], func=AF.Sigmoid), reads=[gbTb], writes=[gbTb])

                def combine(blk, ti):
                    tok = blk * BT + ti * 128
                    og, ogb = oags.next()
                    ob_, obb_ = obts.next()
                    c.dma("sp", og[:], S_["oag"][tok:tok + 128, :, :], writes=[ogb])
                    c.dma("sp", ob_[:], S_["ob"][tok:tok + 128, :], writes=[obb_])
                    cs_, csb = cs.next()
                    lse = og[:, :, 64].rearrange("p (g h) -> p g h", g=3)
                    c.op("dve", lambda e: e.tensor_tensor(out=cs_[:, 0:8], in0=lse[:, 0, :], in1=lse[:, 1, :], op=ALU.max),
                         reads=[ogb], writes=[csb])
                    c.op("dve", lambda e: e.tensor_tensor(out=cs_[:, 0:8], in0=cs_[:, 0:8], in1=lse[:, 2, :], op=ALU.max),
                         reads=[ogb, csb], writes=[csb])
                    e3 = cs_[:, 8:32].rearrange("p (g h) -> p g h", g=3)
                    c.op("dve", lambda e: e.tensor_tensor(out=e3, in0=lse, in1=cs_[:, 0:8].unsqueeze(1).to_broadcast([128, 3, 8]),
                                                          op=ALU.subtract), reads=[ogb, csb], writes=[csb])
                    c.op("act", lambda e: e.activation(out=cs_[:, 8:32], in_=cs_[:, 8:32], func=AF.Exp), reads=[csb], writes=[csb])
                    c.op("dve", lambda e: e.tensor_tensor(out=cs_[:, 32:40], in0=cs_[:, 8:16], in1=cs_[:, 16:24], op=ALU.add),
                         reads=[csb], writes=[csb])
                    c.op("dve", lambda e: e.tensor_tensor(out=cs_[:, 32:40], in0=cs_[:, 32:40], in1=cs_[:, 24:32], op=ALU.add),
                         reads=[csb], writes=[csb])
                    c.op("dve", lambda e: e.reciprocal(cs_[:, 40:48], cs_[:, 32:40]), reads=[csb], writes=[csb])
                    c.op("dve", lambda e: e.tensor_tensor(out=e3, in0=e3, in1=cs_[:, 40:48].unsqueeze(1).to_broadcast([128, 3, 8]),
                                                          op=ALU.mult), reads=[csb], writes=[csb])
                    oa, oab = oas.next()
                    t1, t1b = tm1.next()
                    t2, t2b = tm2.next()

                    def wmul(eng, dst, dstb, g):
                        c.op(eng, lambda e: e.tensor_tensor(out=dst[:].rearrange("p (h x) -> p h x", h=8),
                                                            in0=og[:, g * 8:(g + 1) * 8, 0:64],
                                                            in1=cs_[:, 8 + g * 8:16 + g * 8].unsqueeze(2).to_broadcast([128, 8, 64]),
                                                            op=ALU.mult), reads=[ogb, csb], writes=[dstb])
                    wmul("dve", oa, oab, 0)
                    wmul("pool", t1, t1b, 1)
                    wmul("pool", t2, t2b, 2)
                    c.op("dve", lambda e: e.tensor_tensor(out=oa[:], in0=oa[:], in1=t1[:], op=ALU.add), reads=[oab, t1b], writes=[oab])
                    c.op("dve", lambda e: e.tensor_tensor(out=oa[:], in0=oa[:], in1=t2[:], op=ALU.add), reads=[oab, t2b], writes=[oab])
                    return (oa, oab, ob_, obb_)

                def trans(ti, bufs):
                    oa, oab, ob_, obb_ = bufs
                    tsl = slice(ti * 128, (ti + 1) * 128)
                    transpose_to(oa, oab, 4, oaT, oaTb, tsl, psrot)
                    transpose_to(ob_, obb_, 8, obT, obTb, tsl, psrot)

                def branch(blk):
                    for pc in range(4):
                        n = blk * 4 + pc
                        if n + 1 < len(jobs):
                            loadw(n + 1)
                        a_, ab, b_, bb = slots.pop(n)
                        for c2 in range(2):
                            cc = pc * 2 + c2
                            pA, pAb = psrot.next()
                            for kc in range(4):
                                c.op("pe", lambda e, kc=kc: e.matmul(pA[:, :], a_[:, kc, c2 * 128:(c2 + 1) * 128], oaT[:, kc, :],
                                                                     start=(kc == 0), stop=(kc == 3)),
                                     reads=[ab, oaTb], writes=[pAb], last=(kc == 3))
                            pB, pBb = psrot.next()
                            for kc in range(8):
                                c.op("pe", lambda e, kc=kc: e.matmul(pB[:, :], b_[:, kc, c2 * 128:(c2 + 1) * 128], obT[:, kc, :],
                                                                     start=(kc == 0), stop=(kc == 7)),
                                     reads=[bb, obTb], writes=[pBb], last=(kc == 7))
                            tm, tmb = tmpm.next()
                            c.op("dve", lambda e: e.tensor_tensor(out=tm[:], in0=pA[:, :], in1=gaT[:, cc, :], op=ALU.mult),
                                 reads=[pAb, gaTb], writes=[tmb])
                            c.op("dve", lambda e: e.tensor_tensor(out=mT[:, cc, :], in0=pB[:, :], in1=gbT[:, cc, :], op=ALU.mult),
                                 reads=[pBb, gbTb], writes=[mTb])
                            c.op("pool", lambda e: e.tensor_tensor(out=mT[:, cc, :], in0=mT[:, cc, :], in1=tm[:], op=ALU.add),
                                 reads=[mTb, tmb], writes=[mTb])

                def outproj(ti):
                    for half in range(2):
                        pX, pXb = psrot.next()
                        for cc in range(8):
                            c.op("pe", lambda e, cc=cc: e.matmul(pX[:, :], mT[:, cc, ti * 128:(ti + 1) * 128],
                                                                 wo[:, cc, half * 512:(half + 1) * 512],
                                                                 start=(cc == 0), stop=(cc == 7)),
                                 reads=[mTb, wob], writes=[pXb], last=(cc == 7))
                        c.op("dve", lambda e: e.tensor_tensor(out=xt[:, ti, half * 512:(half + 1) * 512], in0=pX[:, :],
                                                              in1=xt[:, ti, half * 512:(half + 1) * 512], op=ALU.add),
                             reads=[pXb, xtb], writes=[xtb])

                loadw(0)
                gates(0)
                for ti in range(BT // 128):
                    trans(ti, combine(0, ti))
                for blk in range(nblk):
                    t0 = blk * BT
                    c.dma("sp", xt[:], xsrc[t0:t0 + BT, :].rearrange("(a p) c -> p a c", p=128), writes=[xtb])
                    branch(blk)
                    nxt = blk + 1 < nblk
                    if nxt:
                        gates(blk + 1)
                    for ti in range(BT // 128):
                        bufs = combine(blk + 1, ti) if nxt else None
                        outproj(ti)
                        if nxt:
                            trans(ti, bufs)
                    c.dma("pool", S_["xb"][t0:t0 + BT, :].rearrange("(a p) c -> p a c", p=128), xt[:], reads=[xtb])
                c.barrier()
            c.es = ges

        def phase_E(l, xdst, final):
            BT = 384
            blocks = []
            t = 0
            while t < TOK:
                n_ = min(BT, TOK - t)
                blocks.append((t, n_ // 128))
                t += n_
            with ExitStack() as es:
                c.es = es
                gbc, gbcb = c.sb("gbcE", [128, D])
                c.dma("sp", gbc[:], W["nfg"][l].partition_broadcast(128), writes=[gbcb])
                gfn, gfnb = c.sb("gfn", [128, D])
                if final:
                    c.dma("sp", gfn[:], W["fng"].partition_broadcast(128), writes=[gfnb])
                xts = Rot([c.sb("xtE", [128, 3, D]) for _ in range(2)])
                h2h = Rot([c.sb("h2h", [128, 8, BT], F32R) for _ in range(2)])
                h2l = Rot([c.sb("h2l", [128, 8, BT], F32R) for _ in range(2)])
                hts = Rot([c.sb("htE", [128, D]) for _ in range(2)])
                sts = Rot([c.sb("stE", [128, 3]) for _ in range(2)])
                w1r = Rot([c.sb("w1r", [128, 8, 256]) for _ in range(2)])
                w1h = Rot([c.sb("w1h", [128, 8, 256], F32R) for _ in range(2)])
                w1l = Rot([c.sb("w1l", [128, 8, 256], F32R) for _ in range(2)])
                w2r = Rot([c.sb("w2r", [128, 2, D]) for _ in range(2)])
                w2h = Rot([c.sb("w2h", [128, 2, D], F32R) for _ in range(2)])
                w2l = Rot([c.sb("w2l", [128, 2, D], F32R) for _ in range(2)])
                rls = Rot([c.sb("rl", [128, BT]) for _ in range(2)])
                sqs = Rot([c.sb("sq", [128, BT]) for _ in range(2)])
                hhs = Rot([c.sb("hh", [128, BT], F32R) for _ in range(3)])
                hls = Rot([c.sb("hl", [128, BT], F32R) for _ in range(3)])
                acc = banks[0:6]
                ps67 = Rot(banks[6:8])

                def mm3(ps_ap, psb, A, B, first, lastk, inc):
                    (ah, ahb), (al, alb) = A
                    (bh, bhb), (bl_, blb_) = B
                    c.op("pe", lambda e: e.matmul(ps_ap, ah, bh, start=first, stop=False), reads=[ahb, bhb], writes=[psb], last=False)
                    c.op("pe", lambda e: e.matmul(ps_ap, ah, bl_, start=False, stop=False), reads=[ahb, blb_], writes=[psb], last=False)
                    c.op("pe", lambda e: e.matmul(ps_ap, al, bh, start=False, stop=lastk), reads=[alb, bhb], writes=[psb], last=inc)

                jobs = [(bi_, pj) for bi_ in range(len(blocks)) for pj in range(16)]
                wsl = {}

                raws = {}

                def dmaw(n):
                    if n >= len(jobs):
                        return
                    bi_, pj = jobs[n]
                    r1, r1b = w1r.next()
                    r2, r2b = w2r.next()
                    c.dma("sp", r1[:], W["w1"][l][:, pj * 256:(pj + 1) * 256].rearrange("(kc p) c -> p kc c", p=128), writes=[r1b])
                    c.dma("sp", r2[:], W["w2"][l][pj * 256:(pj + 1) * 256, :].rearrange("(fc p) c -> p fc c", p=128), writes=[r2b])
                    raws[n] = (r1, r1b, r2, r2b)

                def loadw(n):
                    r1, r1b, r2, r2b = raws.pop(n)
                    a1, a1b = w1h.next()
                    l1, l1b = w1l.next()
                    a2, a2b = w2h.next()
                    l2, l2b = w2l.next()
                    c.op("act", lambda e: e.activation(out=a1[:], in_=r1[:], func=AF.Copy), reads=[r1b], writes=[a1b])
                    c.op("dve", lambda e: e.tensor_tensor(out=l1[:], in0=r1[:], in1=a1[:].bitcast(F32), op=ALU.subtract),
                         reads=[r1b, a1b], writes=[l1b])
                    c.op("act", lambda e: e.activation(out=a2[:], in_=r2[:], func=AF.Copy), reads=[r2b], writes=[a2b])
                    c.op("dve", lambda e: e.tensor_tensor(out=l2[:], in0=r2[:], in1=a2[:].bitcast(F32), op=ALU.subtract),
                         reads=[r2b, a2b], writes=[l2b])
                    wsl[n] = (a1, a1b, l1, l1b, a2, a2b, l2, l2b)

                BS = {}

                def pro_load(bi_):
                    t0, nt = blocks[bi_]
                    xt, xtb = xts.next()
                    hh_, hhb_ = h2h.next()
                    hl_, hlb_ = h2l.next()
                    c.dma("sp", xt[:, 0:nt, :], S_["xb"][t0:t0 + nt * 128, :].rearrange("(a p) c -> p a c", p=128), writes=[xtb])
                    BS[bi_] = (xt, xtb, hh_, hhb_, hl_, hlb_)

                def pro_tile(bi_, ti):
                    xt, xtb, hh_, hhb_, hl_, hlb_ = BS[bi_]
                    ht, hb = hts.next()
                    st, stb = sts.next()
                    rms_stats(xt[:, ti, :], xtb, ht[:], hb, st, stb, D)
                    c.op("dve", lambda e: e.scalar_tensor_tensor(out=ht[:], in0=xt[:, ti, :], scalar=st[:, 2:3], in1=gbc[:],
                                                                 op0=ALU.mult, op1=ALU.mult),
                         reads=[xtb, stb, gbcb], writes=[hb])
                    tsl = slice(ti * 128, (ti + 1) * 128)
                    for c0_ in range(0, 8, 4):
                        ps, psb = ps67.next()
                        for j in range(4):
                            c.op("pe", lambda e, j=j: e.transpose(ps[:, j * 128:(j + 1) * 128],
                                                                  ht[:, (c0_ + j) * 128:(c0_ + j + 1) * 128], ident[:]),
                                 reads=[hb, identb], writes=[psb], last=(j == 3))
                        ps3 = ps[:, :].rearrange("p (a b) -> p a b", a=4)
                        c.op("act", lambda e: e.activation(out=hh_[:, c0_:c0_ + 4, tsl], in_=ps3, func=AF.Copy),
                             reads=[psb], writes=[hhb_])
                        c.op("dve", lambda e: e.tensor_tensor(out=hl_[:, c0_:c0_ + 4, tsl], in0=ps3,
                                                              in1=hh_[:, c0_:c0_ + 4, tsl].bitcast(F32), op=ALU.subtract),
                             reads=[psb, hhb_], writes=[hlb_])

                dmaw(0)
                loadw(0)
                dmaw(1)
                pro_load(0)
                for ti in range(blocks[0][1]):
                    pro_tile(0, ti)
                n = 0
                for bi_, (t0, nt) in enumerate(blocks):
                    ntok = nt * 128
                    xt, xtb, hh_, hhb_, hl_, hlb_ = BS[bi_]
                    nxt = bi_ + 1 < len(blocks)
                    pend_ff2 = None

                    def ff2(fc, c2, hbuf, wts):
                        a1, a1b, l1, l1b, a2, a2b, l2, l2b = wts
                        hh, hhb, hl, hlb = hbuf
                        for ti in range(nt):
                            for half in range(2):
                                pa_, pab_ = acc[ti * 2 + half]
                                lastone = (ti == nt - 1 and half == 1)
                                mm3(pa_[:, :], pab_,
                                    ((hh[:, ti * 128:(ti + 1) * 128], hhb), (hl[:, ti * 128:(ti + 1) * 128], hlb)),
                                    ((a2[:, c2, half * 512:(half + 1) * 512], a2b), (l2[:, c2, half * 512:(half + 1) * 512], l2b)),
                                    fc == 0, fc == 31, lastone)

                    for pj in range(16):
                        wts = wsl.pop(n)
                        n += 1
                        a1, a1b, l1, l1b = wts[0:4]
                        if nxt and pj == 0:
                            pro_load(bi_ + 1)
                        for c2 in range(2):
                            fc = pj * 2 + c2
                            ps, psb = ps67.next()
                            for kc in range(8):
                                mm3(ps[:, 0:ntok], psb,
                                    ((a1[:, kc, c2 * 128:(c2 + 1) * 128], a1b), (l1[:, kc, c2 * 128:(c2 + 1) * 128], l1b)),
                                    ((hh_[:, kc, 0:ntok], hhb_), (hl_[:, kc, 0:ntok], hlb_)), kc == 0, kc == 7, kc == 7)
                            if pend_ff2 is not None:
                                ff2(*pend_ff2)
                                pend_ff2 = None
                            if c2 == 0 and n < len(jobs):
                                loadw(n)
                                dmaw(n + 1)
                            r_, rb_ = rls.next()
                            q_, qb_ = sqs.next()
                            hh, hhb = hhs.next()
                            hl, hlb = hls.next()
                            c.op("act", lambda e: e.activation(out=r_[:, 0:ntok], in_=ps[:, 0:ntok], func=AF.Relu), reads=[psb], writes=[rb_])
                            c.op("dve", lambda e: e.tensor_tensor(out=q_[:, 0:ntok], in0=r_[:, 0:ntok], in1=r_[:, 0:ntok], op=ALU.mult),
                                 reads=[rb_], writes=[qb_])
                            c.op("act", lambda e: e.activation(out=hh[:, 0:ntok], in_=q_[:, 0:ntok], func=AF.Copy), reads=[qb_], writes=[hhb])
                            c.op("dve", lambda e: e.tensor_tensor(out=hl[:, 0:ntok], in0=q_[:, 0:ntok], in1=hh[:, 0:ntok].bitcast(F32),
                                                                   op=ALU.subtract), reads=[qb_, hhb], writes=[hlb])
                            pend_ff2 = (fc, c2, (hh, hhb, hl, hlb), wts)
                        if nxt and 9 <= pj < 9 + blocks[bi_ + 1][1]:
                            pro_tile(bi_ + 1, pj - 9)
                    ff2(*pend_ff2)
                    pend_ff2 = None
                    for ti in range(nt):
                        for half in range(2):
                            pa_, pab_ = acc[ti * 2 + half]
                            c.op("dve", lambda e: e.tensor_tensor(out=xt[:, ti, half * 512:(half + 1) * 512], in0=pa_[:, :],
                                                                  in1=xt[:, ti, half * 512:(half + 1) * 512], op=ALU.add),
                                 reads=[pab_, xtb], writes=[xtb])
                    if final:
                        for ti in range(nt):
                            ht, hb = hts.next()
                            st, stb = sts.next()
                            rms_stats(xt[:, ti, :], xtb, ht[:], hb, st, stb, D)
                            c.op("dve", lambda e: e.scalar_tensor_tensor(out=ht[:], in0=xt[:, ti, :], scalar=st[:, 2:3], in1=gfn[:],
                                                                         op0=ALU.mult, op1=ALU.mult),
                                 reads=[xtb, stb, gfnb], writes=[hb])
                            c.dma("pool", xdst[t0 + ti * 128:t0 + (ti + 1) * 128, :], ht[:], reads=[hb])
                    else:
                        c.dma("pool", xdst[t0:t0 + ntok, :].rearrange("(a p) c -> p a c", p=128), xt[:, 0:nt, :], reads=[xtb])
                    del BS[bi_]
                c.barrier()
            c.es = ges

        for l in range(depth):
            xsrc = x_in if l == 0 else S_["xa"]
            final = (l == depth - 1)
            if "A" in phases:
                phase_A(l, xsrc)
            if "B" in phases:
                phase_B(l)
            if "C" in phases:
                phase_C(l, 1)
            if "c" in phases:
                phase_C(l, 0)
            if "D" in phases:
                phase_D(l, xsrc)
            if "E" in phases:
                phase_E(l, y_out if final else S_["xa"], final)
        c.barrier(final=True)
    return nc


_PROG_CACHE = {}


def kernel(x_prompt, x_sample, norm_mix_g, w_in, gla_w_gate_fwd, gla_b_gate_fwd, gla_w_gate_bwd, gla_b_gate_bwd,
           gla_norm_g, w_branch_a, w_branch_b, w_out, norm_ffn_g, w_ff1, w_ff2, final_norm_g):
    f32 = np.float32
    xp = np.asarray(x_prompt, f32)
    xs = np.asarray(x_sample, f32)
    nb_p, sp = xp.shape[0], xp.shape[1]
    nb_s, ss = xs.shape[0], xs.shape[1]
    pp, ps_ = nb_p // N_CORES, nb_s // N_CORES
    seqs = [sp] * pp + [ss] * ps_
    key = tuple(seqs)
    if key not in _PROG_CACHE:
        _PROG_CACHE[key] = build_program(seqs)
    nc = _PROG_CACHE[key]
    shared = {
        "norm_mix_g": norm_mix_g, "w_in": w_in, "gla_w_gate_fwd": gla_w_gate_fwd, "gla_b_gate_fwd": gla_b_gate_fwd,
        "gla_w_gate_bwd": gla_w_gate_bwd, "gla_b_gate_bwd": gla_b_gate_bwd, "gla_norm_g": gla_norm_g,
        "w_branch_a": w_branch_a, "w_branch_b": w_branch_b, "w_out": w_out, "norm_ffn_g": norm_ffn_g,
        "w_ff1": w_ff1, "w_ff2": w_ff2, "final_norm_g": final_norm_g,
    }
    shared = {k: np.ascontiguousarray(np.asarray(v, f32)) for k, v in shared.items()}
    shared.update(host_consts())
    in_maps = []
    for i in range(N_CORES):
        xcore = np.concatenate([xp[i * pp:(i + 1) * pp].reshape(pp * sp, D), xs[i * ps_:(i + 1) * ps_].reshape(ps_ * ss, D)], axis=0)
        m = dict(shared)
        m["x"] = np.ascontiguousarray(xcore)
        in_maps.append(m)
    res = run_bass_kernel_spmd(nc, in_maps, core_ids=list(range(N_CORES)))
    yp = np.empty((nb_p, sp, D), f32)
    ys = np.empty((nb_s, ss, D), f32)
    for i in range(N_CORES):
        y = np.asarray(res.results[i]["y"])
        yp[i * pp:(i + 1) * pp] = y[:pp * sp].reshape(pp, sp, D)
        ys[i * ps_:(i + 1) * ps_] = y[pp * sp:].reshape(ps_, ss, D)
    return (yp, ys)
```

```python
import numpy as np
from contextlib import ExitStack
import concourse.bass as bass
import concourse.mybir as mybir
from concourse.bass_utils import run_bass_kernel_spmd

F32 = mybir.dt.float32
F32R = mybir.dt.float32r
AF = mybir.ActivationFunctionType
ALU = mybir.AluOpType
AX = mybir.AxisListType

D = 1024
DEPTH = 2
A_GROUPS = ((128, 1), (512, 4), (2048, 16))
IN_COLS = 9760
C_QA, C_KA, C_VA, C_QB, C_KB, C_VB, C_RB, C_GL, C_GA, C_GB = 0, 1536, 3072, 4608, 5120, 5632, 6656, 7680, 7712, 8736
D_FF = 4096
EPS = 1e-6
NEG = -1e30
PAD = 1024
OAW = 66
N_CORES = 8
CUT = 99


class Buf:
    __slots__ = ("name", "w", "r", "excl")

    def __init__(self, name, excl=False):
        self.name = name
        self.w = None
        self.r = {}
        self.excl = excl


class Ctx:
    WINDOW = 4
    NRING = 12

    def __init__(self, nc, es):
        self.nc = nc
        self.es = es
        self.eng = {"pe": nc.tensor, "dve": nc.vector, "act": nc.scalar,
                    "pool": nc.gpsimd, "sp": nc.sync}
        self.sem = {k: es.enter_context(nc.semaphore("s_" + k)) for k in self.eng}
        self.cnt = {k: 0 for k in self.eng}
        self.seen = {k: {} for k in self.eng}
        self.pend = {k: ([], []) for k in self.eng}
        self.ring = {q: [es.enter_context(nc.semaphore("r_%s%d" % (q, i))) for i in range(self.NRING)]
                     for q in ("sp", "pool")}
        self.ringcnt = {"sp": 0, "pool": 0}
        self.uid = 0

    def sb(self, name, shape, dtype=F32):
        self.uid += 1
        t = self.es.enter_context(self.nc.sbuf_tensor("%s_%d" % (name, self.uid), list(shape), dtype))
        return t, Buf(name)

    def _semof(self, key):
        if isinstance(key, str):
            return self.sem[key]
        return self.ring[key[1]][key[2]]

    def _wait(self, e, key, val):
        if self.seen[e].get(key, 0) >= val:
            return
        if key == e:
            if e == "pe":
                return
            if val <= self.cnt[e] - self.WINDOW:
                return
        self.eng[e].wait_ge(self._semof(key), val)
        self.seen[e][key] = val

    def _deps(self, e, reads, writes):
        for b in reads:
            if b.w is not None:
                self._wait(e, b.w[0], b.w[1])
            if b.excl:
                for k, v in b.r.items():
                    if k != e:
                        self._wait(e, k, v)
        for b in writes:
            if b.w is not None:
                self._wait(e, b.w[0], b.w[1])
            for k, v in b.r.items():
                self._wait(e, k, v)

    def _mark(self, ev, reads, writes):
        for b in reads:
            if b.r.get(ev[0], 0) < ev[1]:
                b.r[ev[0]] = ev[1]
        for b in writes:
            b.w = ev
            b.r = {}

    def op(self, e, fn, reads=(), writes=(), last=True):
        self._deps(e, reads, writes)
        ins = fn(self.eng[e])
        pr, pw = self.pend[e]
        pr.extend(reads)
        pw.extend(writes)
        if last:
            self.cnt[e] += 1
            ins.then_inc(self.sem[e], 1)
            self._mark((e, self.cnt[e]), pr, pw)
            self.pend[e] = ([], [])
        return ins

    def dma(self, q, out, in_, reads=(), writes=()):
        self._deps(q, reads, writes)
        i = self.ringcnt[q]
        slot, rnd = i % self.NRING, i // self.NRING
        key = ("q", q, slot)
        if rnd > 0:
            self._wait(q, key, 16 * rnd)
        self.eng[q].dma_start(out=out, in_=in_).then_inc(self.ring[q][slot], 16)
        self.ringcnt[q] += 1
        self._mark((key, 16 * (rnd + 1)), reads, writes)

    def barrier(self, final=False):
        for e in self.eng:
            assert not self.pend[e][0] and not self.pend[e][1]
        evs = [(k, self.cnt[k]) for k in self.eng if self.cnt[k] > 0]
        for q in ("sp", "pool"):
            n = self.ringcnt[q]
            for slot in range(min(n, self.NRING)):
                rounds = (n - slot + self.NRING - 1) // self.NRING
                evs.append((("q", q, slot), 16 * rounds))
        engs = ["sp"] if final else list(self.eng)
        for e in engs:
            for k, v in evs:
                if k == e or self.seen[e].get(k, 0) >= v:
                    continue
                self.eng[e].wait_ge(self._semof(k), v)
                self.seen[e][k] = v


def ssl(start, count, step):
    return slice(start, start + (count - 1) * step + 1, step)


class Rot:
    def __init__(self, items):
        self.items = items
        self.i = 0

    def next(self):
        it = self.items[self.i % len(self.items)]
        self.i += 1
        return it


def host_consts():
    q = np.arange(128)[:, None]
    kk = np.arange(256)[None, :]
    delta = kk - 64 - q
    ad = np.abs(delta).astype(np.float64)
    valid = ad <= 64
    nh = 24
    slopes = np.exp2(-8.0 * np.arange(1, nh + 1, dtype=np.float64) / nh)
    abias = np.zeros((nh, 128, 256), np.float32)
    for g, (win, dil) in enumerate(A_GROUPS):
        for h in range(8):
            b = -slopes[g * 8 + h] * ad * dil
            abias[g * 8 + h] = np.where(valid, b, NEG).astype(np.float32)
    edge = np.zeros((2, 128, 256), np.float32)
    edge[0][:, :64] = NEG
    edge[1][:, 192:] = NEG
    s = np.arange(128)[:, None]
    t = np.arange(128)[None, :]
    tri = np.zeros((2, 128, 128), np.float32)
    tri[0] = np.where(s <= t, -1.0 / 16.0, 0.0)
    tri[1] = np.where(s >= t, -1.0 / 16.0, 0.0)
    msk = np.zeros((2, 128, 128), np.float32)
    msk[0] = (s <= t)
    msk[1] = (s > t)
    return {"c_ident": np.eye(128, dtype=np.float32), "c_abias": abias, "c_edge": edge,
            "c_tri": tri, "c_msk": msk, "c_ones": np.ones((1, 512), np.float32)}


def build_program(seqs, depth=DEPTH, dbg=False, phases="ABCcDE"):
    nc = bass.Bass("TRN2", target_bir_lowering=False)
    TOK = sum(seqs)
    offs = [sum(seqs[:i]) for i in range(len(seqs))]
    SMAX = max(seqs)

    def din(name, shape):
        return nc.dram_tensor(name, list(shape), F32, kind="ExternalInput").ap()

    def dscr(name, shape):
        return nc.dram_tensor(name, list(shape), F32, kind=("ExternalOutput" if dbg else "Internal")).ap()

    x_in = din("x", [TOK, D])
    W = {
        "norm_mix_g": din("norm_mix_g", [depth, D]),
        "w_in": din("w_in", [depth, D, IN_COLS]),
        "gwf": din("gla_w_gate_fwd", [depth, 16, 512]),
        "gbf": din("gla_b_gate_fwd", [depth, 512]),
        "gwb": din("gla_w_gate_bwd", [depth, 16, 512]),
        "gbb": din("gla_b_gate_bwd", [depth, 512]),
        "gng": din("gla_norm_g", [depth, 256]),
        "wa": din("w_branch_a", [depth, 512, D]),
        "wb": din("w_branch_b", [depth, D, D]),
        "wo": din("w_out", [depth, D, D]),
        "nfg": din("norm_ffn_g", [depth, D]),
        "w1": din("w_ff1", [depth, D, D_FF]),
        "w2": din("w_ff2", [depth, D_FF, D]),
        "fng": din("final_norm_g", [D]),
    }
    c_ident = din("c_ident", [128, 128])
    c_abias = din("c_abias", [24, 128, 256])
    c_edge = din("c_edge", [2, 128, 256])
    c_tri = din("c_tri", [2, 128, 128])
    c_msk = din("c_msk", [2, 128, 128])
    c_ones = din("c_ones", [1, 512])
    y_out = nc.dram_tensor("y", [TOK, D], F32, kind="ExternalOutput").ap()

    S_ = {
        "qaT": dscr("s_qaT", [1536, TOK]), "kaT": dscr("s_kaT", [1536, TOK]), "va": dscr("s_va", [TOK, 1536]),
        "qbT": dscr("s_qbT", [512, TOK]), "kbT": dscr("s_kbT", [512, TOK]), "vb": dscr("s_vb", [TOK, 1024]),
        "rb": dscr("s_rb", [TOK, 1024]), "lrT": dscr("s_lrT", [32, TOK]),
        "gaT": dscr("s_gaT", [1024, TOK]), "gbT": dscr("s_gbT", [1024, TOK]),
        "oag": dscr("s_oag", [TOK, 24, OAW]), "obb": dscr("s_obb", [TOK, 1024]), "ob": dscr("s_ob", [TOK, 1024]),
        "xa": dscr("s_xa", [TOK, D]), "xb": dscr("s_xb", [TOK, D]),
    }

    with ExitStack() as ges:
        c = Ctx(nc, ges)
        ident, identb = c.sb("ident", [128, 128])
        c.dma("sp", ident[:], c_ident, writes=[identb])
        banks = []
        for i in range(8):
            t = ges.enter_context(nc.psum_tensor("psb%d" % i, [128, 512], F32))
            banks.append((t, Buf("psb%d" % i, excl=True)))
        evac_rr = [0]

        def evac(out, in_, reads, writes, eng=None):
            if eng is None:
                eng = ("act", "dve")[evac_rr[0] % 2]
                evac_rr[0] += 1
            if eng == "act":
                c.op("act", lambda e: e.activation(out=out, in_=in_, func=AF.Copy), reads=reads, writes=writes)
            else:
                c.op("dve", lambda e: e.tensor_copy(out, in_), reads=reads, writes=writes)

        def rms_stats(xt_ap, xb, junk, jb, st, stb, n):
            c.op("act", lambda e: e.activation(out=junk, in_=xt_ap, func=AF.Square, accum_out=st[:, 0:1]),
                 reads=[xb], writes=[jb, stb])
            c.op("act", lambda e: e.activation(out=st[:, 1:2], in_=st[:, 0:1], func=AF.Ln, scale=1.0 / n, bias=EPS),
                 reads=[stb], writes=[stb])
            c.op("act", lambda e: e.activation(out=st[:, 2:3], in_=st[:, 1:2], func=AF.Exp, scale=-0.5),
                 reads=[stb], writes=[stb])

        def transpose_to(src, srcb, nchunks, dstT, dstb, tsl, psrot):
            for c0 in range(0, nchunks, 4):
                nn = min(4, nchunks - c0)
                ps, psb = psrot.next()
                for j in range(nn):
                    c.op("pe", lambda e, j=j: e.transpose(ps[:, j * 128:(j + 1) * 128],
                                                          src[:, (c0 + j) * 128:(c0 + j + 1) * 128], ident[:]),
                         reads=[srcb, identb], writes=[psb], last=(j == nn - 1))
                evac(dstT[:, c0:c0 + nn, tsl], ps[:, 0:nn * 128].rearrange("p (a b) -> p a b", a=nn),
                     reads=[psb], writes=[dstb])

        def phase_A(l, xsrc):
            TA = 1024 if TOK % 1024 == 0 else 512
            panels = []
            for (c0, n, form, dst) in ((C_QA, 1536, "F", "qaT"), (C_KA, 1536, "F", "kaT"), (C_VA, 1536, "T", "va"),
                                       (C_QB, 512, "F", "qbT"), (C_KB, 512, "F", "kbT"), (C_VB, 1024, "T", "vb"),
                                       (C_RB, 1024, "T", "rb"), (C_GL, 32, "F", "lrT"), (C_GA, 1024, "F", "gaT"),
                                       (C_GB, 1024, "F", "gbT")):
                for p0 in range(0, n, 512):
                    panels.append((c0 + p0, min(512, n - p0), form, dst, p0))
            with ExitStack() as es:
                c.es = es
                gbc, gbcb = c.sb("gbc", [128, D])
                c.dma("sp", gbc[:], W["norm_mix_g"][l].partition_broadcast(128), writes=[gbcb])
                hTh, hThb = c.sb("hTh", [128, 8, TA], F32R)
                hTl, hTlb = c.sb("hTl", [128, 8, TA], F32R)
                xts = Rot([c.sb("xt", [128, D]) for _ in range(2)])
                hts = Rot([c.sb("ht", [128, D]) for _ in range(2)])
                sts = Rot([c.sb("st", [128, 3]) for _ in range(2)])
                junk, junkb = c.sb("junk", [128, D])
                wp = Rot([c.sb("wpan", [128, 8, 512]) for _ in range(2)])
                wph = Rot([c.sb("wph", [128, 8, 512], F32R) for _ in range(2)])
                wpl = Rot([c.sb("wpl", [128, 8, 512], F32R) for _ in range(2)])
                stg = Rot([c.sb("stg", [128, 512]) for _ in range(4)])
                psrot = Rot(banks)
                jobs = [(blk, p) for blk in range(TOK // TA) for p in panels]
                slots = {}

                rawsA = {}

                def dmaA(n):
                    if n >= len(jobs):
                        return
                    blk, (c0, ncols, form, dst, p0) = jobs[n]
                    sl, slb = wp.next()
                    c.dma("sp", sl[:, :, 0:ncols], W["w_in"][l][:, c0:c0 + ncols].rearrange("(kc p) c -> p kc c", p=128),
                          writes=[slb])
                    rawsA[n] = (sl, slb)

                def load(n):
                    blk, (c0, ncols, form, dst, p0) = jobs[n]
                    sl, slb = rawsA.pop(n)
                    wh, whb = wph.next()
                    wl, wlb = wpl.next()
                    c.op("act", lambda e: e.activation(out=wh[:, :, 0:ncols], in_=sl[:, :, 0:ncols], func=AF.Copy),
                         reads=[slb], writes=[whb])
                    c.op("dve", lambda e: e.tensor_tensor(out=wl[:, :, 0:ncols], in0=sl[:, :, 0:ncols],
                                                           in1=wh[:, :, 0:ncols].bitcast(F32), op=ALU.subtract),
                         reads=[slb, whb], writes=[wlb])
                    slots[n] = (wh, whb, wl, wlb)

                def mm3(ps_ap, psb, A, B, first, lastk):
                    (ah, ahb), (al, alb) = A
                    (bh, bhb), (bl_, blb_) = B
                    c.op("pe", lambda e: e.matmul(ps_ap, ah, bh, start=first, stop=False), reads=[ahb, bhb], writes=[psb], last=False)
                    c.op("pe", lambda e: e.matmul(ps_ap, ah, bl_, start=False, stop=False), reads=[ahb, blb_], writes=[psb], last=False)
                    c.op("pe", lambda e: e.matmul(ps_ap, al, bh, start=False, stop=lastk), reads=[alb, bhb], writes=[psb], last=lastk)

                dmaA(0)
                load(0)
                dmaA(1)
                for n, (blk, (c0, ncols, form, dst, p0)) in enumerate(jobs):
                    t0 = blk * TA
                    if n % len(panels) == 0:
                        for ti in range(TA // 128):
                            xt, xb = xts.next()
                            ht, hb = hts.next()
                            st, stb = sts.next()
                            c.dma("sp", xt[:], xsrc[t0 + ti * 128:t0 + (ti + 1) * 128, :], writes=[xb])
                            rms_stats(xt[:], xb, junk[:], junkb, st, stb, D)
                            c.op("dve", lambda e: e.scalar_tensor_tensor(out=ht[:], in0=xt[:], scalar=st[:, 2:3], in1=gbc[:],
                                                                         op0=ALU.mult, op1=ALU.mult),
                                 reads=[xb, stb, gbcb], writes=[hb])
                            tsl = slice(ti * 128, (ti + 1) * 128)
                            for c0_ in range(0, 8, 4):
                                ps, psb = psrot.next()
                                for j in range(4):
                                    c.op("pe", lambda e, j=j: e.transpose(ps[:, j * 128:(j + 1) * 128],
                                                                          ht[:, (c0_ + j) * 128:(c0_ + j + 1) * 128], ident[:]),
                                         reads=[hb, identb], writes=[psb], last=(j == 3))
                                ps3 = ps[:, :].rearrange("p (a b) -> p a b", a=4)
                                c.op("act", lambda e: e.activation(out=hTh[:, c0_:c0_ + 4, tsl], in_=ps3, func=AF.Copy),
                                     reads=[psb], writes=[hThb])
                                c.op("dve", lambda e: e.tensor_tensor(out=hTl[:, c0_:c0_ + 4, tsl], in0=ps3,
                                                                      in1=hTh[:, c0_:c0_ + 4, tsl].bitcast(F32), op=ALU.subtract),
                                     reads=[psb, hThb], writes=[hTlb])
                    if n + 1 < len(jobs):
                        load(n + 1)
                        dmaA(n + 2)
                    wh, whb, wl, wlb = slots.pop(n)
                    if form == "F":
                        for cc in range((ncols + 127) // 128):
                            m = min(128, ncols - cc * 128)
                            csl = slice(cc * 128, cc * 128 + m)
                            for tb in range(TA // 512):
                                ps, psb = psrot.next()
                                tsl = slice(tb * 512, (tb + 1) * 512)
                                for kc in range(8):
                                    mm3(ps[0:m, :], psb, ((wh[:, kc, csl], whb), (wl[:, kc, csl], wlb)),
                                        ((hTh[:, kc, tsl], hThb), (hTl[:, kc, tsl], hTlb)), kc == 0, kc == 7)
                                sg, sgb = stg.next()
                                evac(sg[0:m, :], ps[0:m, :], reads=[psb], writes=[sgb])
                                c.dma("pool", S_[dst][p0 + cc * 128:p0 + cc * 128 + m, t0 + tb * 512:t0 + (tb + 1) * 512],
                                      sg[0:m, :], reads=[sgb])
                    else:
                        for ti in range(TA // 128):
                            ps, psb = psrot.next()
                            tsl = slice(ti * 128, (ti + 1) * 128)
                            for kc in range(8):
                                mm3(ps[:, 0:ncols], psb, ((hTh[:, kc, tsl], hThb), (hTl[:, kc, tsl], hTlb)),
                                    ((wh[:, kc, 0:ncols], whb), (wl[:, kc, 0:ncols], wlb)), kc == 0, kc == 7)
                            sg, sgb = stg.next()
                            evac(sg[:, 0:ncols], ps[:, 0:ncols], reads=[psb], writes=[sgb])
                            c.dma("pool", S_[dst][t0 + ti * 128:t0 + (ti + 1) * 128, p0:p0 + ncols], sg[:, 0:ncols],
                                  reads=[sgb])
                c.barrier()
            c.es = ges

        def phase_B(l):
            with ExitStack() as es:
                c.es = es
                NT = max(d * (S // (d * 128) + 1) for S in seqs for (_, d) in A_GROUPS)
                QT = Rot([c.sb("QT", [64, SMAX]) for _ in range(2)])
                KT = Rot([c.sb("KT", [64, SMAX + 2 * PAD]) for _ in range(2)])
                VP = Rot([c.sb("VP", [128, NT, 128]) for _ in range(2)])
                BI = Rot([c.sb("bias", [128, 256]) for _ in range(2)])
                edge, edgeb = c.sb("edge", [128, 2, 256])
                c.dma("sp", edge[:], c_edge.rearrange("a p k -> p a k"), writes=[edgeb])
                for (kt, ktb) in KT.items:
                    c.op("pool", lambda e: e.memset(kt[:, 0:PAD], 0.0), writes=[ktb])
                Tt = Rot([c.sb("T", [128, 256]) for _ in range(2)])
                Pt = Rot([c.sb("P", [128, 256]) for _ in range(3)])
                PTt = Rot([c.sb("PT", [128, 256]) for _ in range(3)])
                sm = Rot([c.sb("sm", [128, 4]) for _ in range(5)])
                ot = Rot([c.sb("ot", [128, OAW]) for _ in range(3)])
                psS = Rot(banks[0:2])
                psT = Rot(banks[2:4])
                psO = Rot(banks[4:6])
                jobs = []
                for si, S in enumerate(seqs):
                    for g, (win, d) in enumerate(A_GROUPS):
                        for h in range(8):
                            jobs.append((si, S, g, d, h))
                st = {}

                def load(n):
                    si, S, g, d, h = jobs[n]
                    o = offs[si]
                    sub = S // d
                    nt = sub // 128
                    if h % 2 == 0:
                        vp, vpb = VP.next()
                        st["vp"] = (vp, vpb)
                        vcol = C_QA + g * 512 + (h // 2) * 128
                        vv = vp[:, 0:d * (nt + 1), :].rearrange("p (r a) c -> p r a c", r=d)
                        c.op("pool", lambda e: e.memset(vv[0:64, :, 0, :], 0.0), writes=[vpb])
                        c.op("pool", lambda e: e.memset(vv[64:128, :, nt, :], 0.0), writes=[vpb])
                        va = S_["va"]
                        c.dma("sp", vv[64:128, :, 0, :],
                              va[o:o + 64 * d, vcol:vcol + 128].rearrange("(k r) c -> k r c", r=d), writes=[vpb])
                        c.dma("sp", vv[0:64, :, nt, :],
                              va[o + S - 64 * d:o + S, vcol:vcol + 128].rearrange("(k r) c -> k r c", r=d), writes=[vpb])
                        if nt > 1:
                            for r in range(d):
                                c.dma("sp", vv[:, r, 1:nt, :],
                                      va[ssl(o + 64 * d + r, (nt - 1) * 128, d), vcol:vcol + 128]
                                      .rearrange("(a p) c -> p a c", p=128), writes=[vpb])
                    qt, qtb = QT.next()
                    kt, ktb = KT.next()
                    bi, bib = BI.next()
                    col = g * 512 + h * 64
                    c.dma("sp", qt[:, 0:S], S_["qaT"][col:col + 64, o:o + S], writes=[qtb])
                    c.op("pool", lambda e: e.memset(kt[:, PAD + S:PAD + S + PAD], 0.0), writes=[ktb])
                    c.dma("sp", kt[:, PAD:PAD + S], S_["kaT"][col:col + 64, o:o + S], writes=[ktb])
                    c.dma("sp", bi[:], c_abias[g * 8 + h], writes=[bib])
                    st[n] = (qt, qtb, kt, ktb, bi, bib) + st["vp"]

                tiles = []
                for n, (si, S, g, d, h) in enumerate(jobs):
                    nt = (S // d) // 128
                    first = True
                    for r in range(d):
                        for j in range(nt):
                            tiles.append((n, r, j, first))
                            first = False
                NTL = len(tiles)
                res = {}
                TS = {}

                def stage1a(k):
                    n, r, j, first = tiles[k]
                    if first:
                        if n == 0:
                            load(0)
                        if n + 1 < len(jobs):
                            load(n + 1)
                        res[n] = st.pop(n)
                        res.pop(n - 2, None)
                    si, S, g, d, h = jobs[n]
                    qt, qtb, kt, ktb, bi, bib, vp, vpb = res[n]
                    q0 = r + d * 128 * j
                    k0 = PAD + r + d * (128 * j - 64)
                    ps, psb = psS.next()
                    c.op("pe", lambda e: e.matmul(ps[:, 0:256], qt[:, ssl(q0, 128, d)], kt[:, ssl(k0, 256, d)],
                                                  start=True, stop=True), reads=[qtb, ktb], writes=[psb])
                    T, Tb = Tt.next()
                    c.op("dve", lambda e: e.scalar_tensor_tensor(out=T[:], in0=ps[:, 0:256], scalar=0.125, in1=bi[:],
                                                                 op0=ALU.mult, op1=ALU.add),
                         reads=[psb, bib], writes=[Tb])
                    nt = (S // d) // 128
                    if j == 0:
                        c.op("dve", lambda e: e.tensor_tensor(out=T[:], in0=T[:], in1=edge[:, 0, :], op=ALU.add),
                             reads=[Tb, edgeb], writes=[Tb])
                    if j == nt - 1:
                        c.op("dve", lambda e: e.tensor_tensor(out=T[:], in0=T[:], in1=edge[:, 1, :], op=ALU.add),
                             reads=[Tb, edgeb], writes=[Tb])
                    TS[k] = {"T": (T, Tb), "s4": sm.next(), "P": Pt.next()}

                def stage1b(k):
                    T, Tb = TS[k]["T"]
                    s4, s4b = TS[k]["s4"]
                    P, Pb = TS[k]["P"]
                    c.op("dve", lambda e: e.reduce_max(out=s4[:, 0:1], in_=T[:], axis=AX.X, negate=True),
                         reads=[Tb], writes=[s4b])
                    c.op("act", lambda e: e.activation(out=P[:], in_=T[:], func=AF.Exp, bias=s4[:, 0:1], scale=1.0,
                                                       accum_out=s4[:, 1:2]),
                         reads=[Tb, s4b], writes=[Pb, s4b])

                def stage2(k):
                    P, Pb = TS[k]["P"]
                    pt, ptb = psT.next()
                    for b in range(2):
                        c.op("pe", lambda e, b=b: e.transpose(pt[:, b * 128:(b + 1) * 128], P[:, b * 128:(b + 1) * 128], ident[:]),
                             reads=[Pb, identb], writes=[ptb], last=(b == 1))
                    PT, PTb = PTt.next()
                    c.op("act", lambda e: e.activation(out=PT[:], in_=pt[:, 0:256], func=AF.Copy), reads=[ptb], writes=[PTb])
                    TS[k]["PT"] = (PT, PTb)

                def stage3a(k):
                    n, r, j, first = tiles[k]
                    si, S, g, d, h = jobs[n]
                    nt = (S // d) // 128
                    hh = h % 2
                    vp, vpb = res[n][6], res[n][7]
                    PT, PTb = TS[k]["PT"]
                    s4, s4b = TS[k]["s4"]
                    po, pob = psO.next()
                    for b in range(2):
                        c.op("pe", lambda e, b=b: e.matmul(po[:, 0:64], PT[:, b * 128:(b + 1) * 128],
                                                           vp[:, r * (nt + 1) + j + b, hh * 64:(hh + 1) * 64],
                                                           start=(b == 0), stop=(b == 1)),
                             reads=[PTb, vpb], writes=[pob], last=(b == 1))
                    c.op("dve", lambda e: e.reciprocal(s4[:, 2:3], s4[:, 1:2]), reads=[s4b], writes=[s4b])
                    c.op("act", lambda e: e.activation(out=s4[:, 3:4], in_=s4[:, 1:2], func=AF.Ln), reads=[s4b], writes=[s4b])
                    TS[k]["po"] = (po, pob)

                def stage3b(k):
                    n, r, j, first = tiles[k]
                    si, S, g, d, h = jobs[n]
                    o = offs[si]
                    s4, s4b = TS[k]["s4"]
                    po, pob = TS[k]["po"]
                    o_t, o_tb = ot.next()
                    c.op("dve", lambda e: e.tensor_scalar(o_t[:, 0:64], po[:, 0:64], s4[:, 2:3], None, ALU.mult),
                         reads=[pob, s4b], writes=[o_tb])
                    c.op("dve", lambda e: e.tensor_tensor(out=o_t[:, 64:65], in0=s4[:, 3:4], in1=s4[:, 0:1], op=ALU.subtract),
                         reads=[s4b], writes=[o_tb])
                    tq = o + r + d * 128 * j
                    c.dma("pool", S_["oag"][ssl(tq, 128, d), g * 8 + h, 0:65], o_t[:, 0:65], reads=[o_tb])
                    del TS[k]

                for k in range(NTL + 2):
                    if k < NTL:
                        stage1a(k)
                    if 0 <= k - 2 < NTL:
                        stage3a(k - 2)
                    if k < NTL:
                        stage1b(k)
                    if 0 <= k - 1 < NTL:
                        stage2(k - 1)
                    if 0 <= k - 2 < NTL:
                        stage3b(k - 2)
                c.barrier()
            c.es = ges

        def phase_C(l, direction):
            fwd = (direction == 0)
            LNS = float(np.log(128.0 ** -0.5))
            with ExitStack() as es:
                c.es = es
                waug, waugb = c.sb("waug", [32, 512])
                c.op("pool", lambda e: e.memset(waug[:], 0.0), writes=[waugb])
                c.dma("sp", waug[0:16, :], W["gwf" if fwd else "gwb"][l], writes=[waugb])
                c.dma("sp", waug[16:17, :], W["gbf" if fwd else "gbb"][l].unsqueeze(0), writes=[waugb])
                tri, trib = c.sb("tri", [128, 128])
                msk, mskb = c.sb("msk", [128, 128])
                c.dma("sp", tri[:], c_tri[direction], writes=[trib])
                c.dma("sp", msk[:], c_msk[direction], writes=[mskb])
                gnb, gnbb = c.sb("gnb", [128, 256])
                if fwd:
                    c.dma("sp", gnb[:], W["gng"][l].partition_broadcast(128), writes=[gnbb])
                BT = 512
                qbl = Rot([c.sb("qbl", [128, 4, BT]) for _ in range(2)])
                kbl = Rot([c.sb("kbl", [128, 4, BT]) for _ in range(2)])
                lra = Rot([c.sb("lra", [32, BT]) for _ in range(2)])
                for (t_, b_) in lra.items:
                    c.op("pool", lambda e: e.memset(t_[:], 0.0), writes=[b_])
                    c.dma("sp", t_[16:17, :], c_ones, writes=[b_])
                vts = Rot([c.sb("vt", [128, 1024]) for _ in range(3)])
                rbs = Rot([c.sb("rbt", [128, 1024]) for _ in range(3)]) if fwd else None
                obs = Rot([c.sb("obt", [128, 1024]) for _ in range(3)]) if fwd else None
                e1s = Rot([c.sb("e1", [128, 512]) for _ in range(2)])
                Ls = Rot([c.sb("L", [128, 512]) for _ in range(3)])
                eqs = Rot([c.sb("eq", [128, 512]) for _ in range(2)])
                eks = Rot([c.sb("ek", [128, 512]) for _ in range(2)])
                bls = Rot([c.sb("bl", [128, 8]) for _ in range(5)])
                qes = Rot([c.sb("qe", [128, 512]) for _ in range(4)])
                kes = Rot([c.sb("ke", [128, 512]) for _ in range(4)])
                kdTs = Rot([c.sb("kdT", [128, 512]) for _ in range(3)])
                kds = Rot([c.sb("kd", [128, 512]) for _ in range(3)])
                atts = Rot([c.sb("att", [128, 512]) for _ in range(2)])
                osb = Rot([c.sb("osb", [128, 1024]) for _ in range(2)])
                sil = Rot([c.sb("sil", [128, 1024]) for _ in range(2)]) if fwd else None
                junk, junkb = c.sb("junkc", [128, 256])
                nst = Rot([c.sb("nst", [128, 12]) for _ in range(2)])
                pend_e2 = [None]
                states = [c.sb("state", [128, 4, 256]) for _ in range(2)]
                cur = [0]
                psZ, psB_, psKD, psA = banks[0], banks[1], banks[2], banks[3]
                psO = (banks[4], banks[5])
                psSt = (banks[6], banks[7])
                lastcol = 127 if fwd else 0

                for si, S in enumerate(seqs):
                    o = offs[si]
                    nblk = S // BT
                    tiles = []
                    for bi_ in (range(nblk) if fwd else range(nblk - 1, -1, -1)):
                        tl = range(BT // 128)
                        for ti in (tl if fwd else reversed(tl)):
                            tiles.append((bi_, ti))
                    c.op("dve", lambda e: e.memset(states[cur[0]][0][:], 0.0), writes=[states[cur[0]][1]])
                    G = {}
                    blkbuf = {}

                    GS = {}

                    def g1(idx):
                        bi_, ti = tiles[idx]
                        t0 = o + bi_ * BT
                        if bi_ not in blkbuf:
                            qb_, qbb = qbl.next()
                            kb_, kbb = kbl.next()
                            lr_, lrb = lra.next()
                            c.dma("sp", qb_[:], S_["qbT"][:, t0:t0 + BT].rearrange("(h p) t -> p h t", p=128), writes=[qbb])
                            c.dma("sp", kb_[:], S_["kbT"][:, t0:t0 + BT].rearrange("(h p) t -> p h t", p=128), writes=[kbb])
                            c.dma("sp", lr_[0:16, :], S_["lrT"][direction * 16:direction * 16 + 16, t0:t0 + BT], writes=[lrb])
                            blkbuf.clear()
                            blkbuf[bi_] = (qb_, qbb, kb_, kbb, lr_, lrb)
                        qb_, qbb, kb_, kbb, lr_, lrb = blkbuf[bi_]
                        tsl = slice(ti * 128, (ti + 1) * 128)
                        pz, pzb = psZ
                        c.op("pe", lambda e: e.matmul(pz[:, :], lr_[:, tsl], waug[:], start=True, stop=True),
                             reads=[lrb, waugb], writes=[pzb])
                        e1, e1b = e1s.next()
                        Lt, Lb = Ls.next()
                        c.op("act", lambda e: e.activation(out=e1[:], in_=pz[:, :], func=AF.Exp, scale=-1.0),
                             reads=[pzb], writes=[e1b])
                        c.op("act", lambda e: e.activation(out=Lt[:], in_=e1[:], func=AF.Ln, bias=1.0),
                             reads=[e1b], writes=[Lb])
                        GS[idx] = {"blk": (qb_, qbb, kb_, kbb), "tsl": tsl, "L": (Lt, Lb), "tok": t0 + ti * 128}

                    def g2(idx):
                        d_ = GS[idx]
                        qb_, qbb, kb_, kbb = d_["blk"]
                        tsl = d_["tsl"]
                        Lt, Lb = d_["L"]
                        pb, pbb = psB_
                        for h in range(4):
                            c.op("pe", lambda e, h=h: e.matmul(pb[:, h * 128:(h + 1) * 128], Lt[:, h * 128:(h + 1) * 128], tri[:],
                                                               start=True, stop=True),
                                 reads=[Lb, trib], writes=[pbb], last=(h == 3))
                        bl, blb = bls.next()
                        pb3 = pb[:, :].rearrange("p (h t) -> p h t", h=4)
                        c.op("dve", lambda e: e.tensor_copy(bl[:, 0:4], pb3[:, :, lastcol]), reads=[pbb], writes=[blb])
                        eq, eqb = eqs.next()
                        ek, ekb = eks.next()
                        c.op("act", lambda e: e.activation(out=eq[:], in_=pb[:, :], func=AF.Exp, bias=LNS, scale=1.0),
                             reads=[pbb], writes=[eqb])
                        c.op("act", lambda e: e.activation(out=ek[:], in_=pb[:, :], func=AF.Exp, scale=-1.0),
                             reads=[pbb], writes=[ekb])
                        c.op("act", lambda e: e.activation(out=bl[:, 4:8], in_=bl[:, 0:4], func=AF.Exp),
                             reads=[blb], writes=[blb])
                        qe, qeb = qes.next()
                        ke, keb = kes.next()
                        kdT, kdTb = kdTs.next()
                        c.op("dve", lambda e: e.tensor_tensor(out=qe[:].rearrange("p (h t) -> p h t", h=4), in0=qb_[:, :, tsl],
                                                              in1=eq[:].rearrange("p (h t) -> p h t", h=4), op=ALU.mult),
                             reads=[qbb, eqb], writes=[qeb])
                        c.op("dve", lambda e: e.tensor_tensor(out=ke[:].rearrange("p (h t) -> p h t", h=4), in0=kb_[:, :, tsl],
                                                              in1=ek[:].rearrange("p (h t) -> p h t", h=4), op=ALU.mult),
                             reads=[kbb, ekb], writes=[keb])
                        for h in range(4):
                            c.op("dve", lambda e, h=h: e.tensor_scalar(kdT[:, h * 128:(h + 1) * 128], ke[:, h * 128:(h + 1) * 128],
                                                                       bl[:, 4 + h:5 + h], None, ALU.mult),
                                 reads=[keb, blb], writes=[kdTb])
                        d_.update({"bl": (bl, blb), "qe": (qe, qeb), "ke": (ke, keb), "kdT": (kdT, kdTb)})

                    def g3(idx):
                        d_ = GS.pop(idx)
                        tok = d_["tok"]
                        vt, vtb = vts.next()
                        c.dma("sp", vt[:], S_["vb"][tok:tok + 128, :], writes=[vtb])
                        ext = None
                        if fwd:
                            rbt, rbb = rbs.next()
                            obt, obb_ = obs.next()
                            c.dma("sp", rbt[:], S_["rb"][tok:tok + 128, :], writes=[rbb])
                            c.dma("sp", obt[:], S_["obb"][tok:tok + 128, :], writes=[obb_])
                            ext = (rbt, rbb, obt, obb_)
                        kdT, kdTb = d_["kdT"]
                        pk, pkb = psKD
                        for h in range(4):
                            c.op("pe", lambda e, h=h: e.transpose(pk[:, h * 128:(h + 1) * 128], kdT[:, h * 128:(h + 1) * 128], ident[:]),
                                 reads=[kdTb, identb], writes=[pkb], last=(h == 3))
                        kd, kdb = kds.next()
                        c.op("act", lambda e: e.activation(out=kd[:], in_=pk[:, :], func=AF.Copy), reads=[pkb], writes=[kdb])
                        bl, blb = d_["bl"]
                        qe, qeb = d_["qe"]
                        ke, keb = d_["ke"]
                        G[idx] = (tok, vt, vtb, ext, bl, blb, qe, qeb, ke, keb, kd, kdb)

                    def statepart(idx):
                        if CUT < 5:
                            return
                        tok, vt, vtb, ext, bl, blb, qe, qeb, ke, keb, kd, kdb = G.pop(idx)
                        state, stateb = states[cur[0]]
                        snew, snewb = states[1 - cur[0]]
                        cur[0] = 1 - cur[0]
                        pa, pab = psA
                        for h in range(4):
                            c.op("pe", lambda e, h=h: e.matmul(pa[:, h * 128:(h + 1) * 128], ke[:, h * 128:(h + 1) * 128],
                                                               qe[:, h * 128:(h + 1) * 128], start=True, stop=True),
                                 reads=[keb, qeb], writes=[pab], last=(h == 3))
                        for h in range(4):
                            pst, pstb = psSt[h // 2]
                            osl = slice((h % 2) * 256, (h % 2 + 1) * 256)
                            c.op("pe", lambda e, h=h: e.matmul(pst[:, osl], kd[:, h * 128:(h + 1) * 128], vt[:, h * 256:(h + 1) * 256],
                                                               start=True, stop=True),
                                 reads=[kdb, vtb], writes=[pstb], last=(h % 2 == 1))
                        att, attb = atts.next()
                        c.op("dve", lambda e: e.tensor_tensor(out=att[:].rearrange("p (h t) -> p h t", h=4),
                                                              in0=pa[:, :].rearrange("p (h t) -> p h t", h=4),
                                                              in1=msk[:].unsqueeze(1).to_broadcast([128, 4, 128]), op=ALU.mult),
                             reads=[pab, mskb], writes=[attb])
                        for h in range(4):
                            pst, pstb = psSt[h // 2]
                            osl = slice((h % 2) * 256, (h % 2 + 1) * 256)
                            c.op("dve", lambda e, h=h: e.scalar_tensor_tensor(out=snew[:, h, :], in0=state[:, h, :],
                                                                              scalar=bl[:, 4 + h:5 + h], in1=pst[:, osl],
                                                                              op0=ALU.mult, op1=ALU.add),
                                 reads=[stateb, blb, pstb], writes=[snewb])
                        if pend_e2[0] is not None:
                            pend_e2[0]()
                            pend_e2[0] = None
                        for h in range(4):
                            po, pob = psO[h // 2]
                            osl = slice((h % 2) * 256, (h % 2 + 1) * 256)
                            c.op("pe", lambda e, h=h: e.matmul(po[:, osl], att[:, h * 128:(h + 1) * 128], vt[:, h * 256:(h + 1) * 256],
                                                               start=True, stop=False),
                                 reads=[attb, vtb], writes=[pob], last=False)
                            c.op("pe", lambda e, h=h: e.matmul(po[:, osl], qe[:, h * 128:(h + 1) * 128], state[:, h, :],
                                                               start=False, stop=True),
                                 reads=[qeb, stateb], writes=[pob], last=(h % 2 == 1))
                        if CUT < 8:
                            return
                        ob_, obb2 = osb.next()
                        if not fwd:
                            evac(ob_[:, 0:512], psO[0][0][:, :], reads=[psO[0][1]], writes=[obb2], eng="act")
                            evac(ob_[:, 512:1024], psO[1][0][:, :], reads=[psO[1][1]], writes=[obb2], eng="dve")
                            c.dma("pool", S_["obb"][tok:tok + 128, :], ob_[:], reads=[obb2])
                        else:
                            rbt, rbb, obt, obb_ = ext
                            for k in range(2):
                                c.op("dve", lambda e, k=k: e.tensor_tensor(out=ob_[:, k * 512:(k + 1) * 512], in0=psO[k][0][:, :],
                                                                           in1=obt[:, k * 512:(k + 1) * 512], op=ALU.add),
                                     reads=[psO[k][1], obb_], writes=[obb2])
                            pend_e2[0] = lambda: epi2(tok, ob_, obb2, rbt, rbb)

                    def epi2(tok, ob_, obb2, rbt, rbb):
                        if True:
                            ns, nsb = nst.next()
                            for h in range(4):
                                c.op("act", lambda e, h=h: e.activation(out=junk[:], in_=ob_[:, h * 256:(h + 1) * 256], func=AF.Square,
                                                                        accum_out=ns[:, h:h + 1]),
                                     reads=[obb2], writes=[junkb, nsb])
                            c.op("act", lambda e: e.activation(out=ns[:, 4:8], in_=ns[:, 0:4], func=AF.Ln, scale=1.0 / 256, bias=EPS),
                                 reads=[nsb], writes=[nsb])
                            c.op("act", lambda e: e.activation(out=ns[:, 8:12], in_=ns[:, 4:8], func=AF.Exp, scale=-0.5),
                                 reads=[nsb], writes=[nsb])
                            sl_, slb_ = sil.next()
                            c.op("act", lambda e: e.activation(out=sl_[:], in_=rbt[:], func=AF.Silu), reads=[rbb], writes=[slb_])
                            c.op("pool", lambda e: e.tensor_tensor(out=sl_[:].rearrange("p (h v) -> p h v", h=4),
                                                                   in0=sl_[:].rearrange("p (h v) -> p h v", h=4),
                                                                   in1=gnb[:].unsqueeze(1).to_broadcast([128, 4, 256]), op=ALU.mult),
                                 reads=[slb_, gnbb], writes=[slb_])
                            c.op("dve", lambda e: e.tensor_tensor(out=ob_[:].rearrange("p (h v) -> p h v", h=4),
                                                                  in0=ob_[:].rearrange("p (h v) -> p h v", h=4),
                                                                  in1=ns[:, 8:12].unsqueeze(2).to_broadcast([128, 4, 256]), op=ALU.mult),
                                 reads=[obb2, nsb], writes=[obb2])
                            c.op("dve", lambda e: e.tensor_tensor(out=ob_[:], in0=ob_[:], in1=sl_[:], op=ALU.mult),
                                 reads=[obb2, slb_], writes=[obb2])
                            c.dma("pool", S_["ob"][tok:tok + 128, :], ob_[:], reads=[obb2])

                    NTI = len(tiles)
                    pend_e2[0] = None
                    for it in range(-3, NTI):
                        if 0 <= it + 3 < NTI:
                            g1(it + 3)
                        if 0 <= it + 2 < NTI:
                            g2(it + 2)
                        if 0 <= it + 1 < NTI:
                            g3(it + 1)
                        if 0 <= it < NTI:
                            statepart(it)
                    if pend_e2[0] is not None:
                        pend_e2[0]()
                        pend_e2[0] = None
                c.barrier()
            c.es = ges

        def phase_D(l, xsrc):
            BT = 512
            with ExitStack() as es:
                c.es = es
                wo, wob = c.sb("wo", [128, 8, D])
                c.dma("sp", wo[:], W["wo"][l].rearrange("(kc p) c -> p kc c", p=128), writes=[wob])
                oags = Rot([c.sb("oag", [128, 24, OAW]) for _ in range(2)])
                obts = Rot([c.sb("obt", [128, 1024]) for _ in range(2)])
                oas = Rot([c.sb("oa", [128, 512]) for _ in range(2)])
                tm1 = Rot([c.sb("tm1", [128, 512]) for _ in range(2)])
                tm2 = Rot([c.sb("tm2", [128, 512]) for _ in range(2)])
                cs = Rot([c.sb("cs", [128, 64]) for _ in range(2)])
                oaT, oaTb = c.sb("oaT", [128, 4, BT])
                obT, obTb = c.sb("obT", [128, 8, BT])
                gaT, gaTb = c.sb("gaT", [128, 8, BT])
                gbT, gbTb = c.sb("gbT", [128, 8, BT])
                mT, mTb = c.sb("mT", [128, 8, BT])
                xt, xtb = c.sb("xtD", [128, 4, D])
                wap = Rot([c.sb("wap", [128, 4, 256]) for _ in range(2)])
                wbp = Rot([c.sb("wbp", [128, 8, 256]) for _ in range(2)])
                tmpm = Rot([c.sb("tmpm", [128, 512]) for _ in range(2)])
                psrot = Rot(banks)
                nblk = TOK // BT
                jobs = [(blk, pc) for blk in range(nblk) for pc in range(4)]
                slots = {}

                def loadw(n):
                    blk, pc = jobs[n]
                    a_, ab = wap.next()
                    b_, bb = wbp.next()
                    c.dma("sp", a_[:], W["wa"][l][:, pc * 256:(pc + 1) * 256].rearrange("(kc p) c -> p kc c", p=128), writes=[ab])
                    c.dma("sp", b_[:], W["wb"][l][:, pc * 256:(pc + 1) * 256].rearrange("(kc p) c -> p kc c", p=128), writes=[bb])
                    slots[n] = (a_, ab, b_, bb)

                def gates(blk):
                    t0 = blk * BT
                    c.dma("sp", gaT[:], S_["gaT"][:, t0:t0 + BT].rearrange("(kc p) t -> p kc t", p=128), writes=[gaTb])
                    c.dma("sp", gbT[:], S_["gbT"][:, t0:t0 + BT].rearrange("(kc p) t -> p kc t", p=128), writes=[gbTb])
                    c.op("act", lambda e: e.activation(out=gaT[:], in_=gaT[:], func=AF.Sigmoid), reads=[gaTb], writes=[gaTb])
                    c.op("act", lambda e: e.activation(out=gbT[:], in_=gbT[:], func=AF.Sigmoid), reads=[gbTb], writes=[gbTb])

                def combine(blk, ti):
                    tok = blk * BT + ti * 128
                    og, ogb = oags.next()
                    ob_, obb_ = obts.next()
                    c.dma("sp", og[:], S_["oag"][tok:tok + 128, :, :], writes=[ogb])
                    c.dma("sp", ob_[:], S_["ob"][tok:tok + 128, :], writes=[obb_])
                    cs_, csb = cs.next()
                    lse = og[:, :, 64].rearrange("p (g h) -> p g h", g=3)
                    c.op("dve", lambda e: e.tensor_tensor(out=cs_[:, 0:8], in0=lse[:, 0, :], in1=lse[:, 1, :], op=ALU.max),
                         reads=[ogb], writes=[csb])
                    c.op("dve", lambda e: e.tensor_tensor(out=cs_[:, 0:8], in0=cs_[:, 0:8], in1=lse[:, 2, :], op=ALU.max),
                         reads=[ogb, csb], writes=[csb])
                    e3 = cs_[:, 8:32].rearrange("p (g h) -> p g h", g=3)
                    c.op("dve", lambda e: e.tensor_tensor(out=e3, in0=lse, in1=cs_[:, 0:8].unsqueeze(1).to_broadcast([128, 3, 8]),
                                                          op=ALU.subtract), reads=[ogb, csb], writes=[csb])
                    c.op("act", lambda e: e.activation(out=cs_[:, 8:32], in_=cs_[:, 8:32], func=AF.Exp), reads=[csb], writes=[csb])
                    c.op("dve", lambda e: e.tensor_tensor(out=cs_[:, 32:40], in0=cs_[:, 8:16], in1=cs_[:, 16:24], op=ALU.add),
                         reads=[csb], writes=[csb])
                    c.op("dve", lambda e: e.tensor_tensor(out=cs_[:, 32:40], in0=cs_[:, 32:40], in1=cs_[:, 24:32], op=ALU.add),
                         reads=[csb], writes=[csb])
                    c.op("dve", lambda e: e.reciprocal(cs_[:, 40:48], cs_[:, 32:40]), reads=[csb], writes=[csb])
                    c.op("dve", lambda e: e.tensor_tensor(out=e3, in0=e3, in1=cs_[:, 40:48].unsqueeze(1).to_broadcast([128, 3, 8]),
                                                          op=ALU.mult), reads=[csb], writes=[csb])
                    oa, oab = oas.next()
                    t1, t1b = tm1.next()
                    t2, t2b = tm2.next()

                    def wmul(eng, dst, dstb, g):
                        c.op(eng, lambda e: e.tensor_tensor(out=dst[:].rearrange("p (h x) -> p h x", h=8),
                                                            in0=og[:, g * 8:(g + 1) * 8, 0:64],
                                                            in1=cs_[:, 8 + g * 8:16 + g * 8].unsqueeze(2).to_broadcast([128, 8, 64]),
                                                            op=ALU.mult), reads=[ogb, csb], writes=[dstb])
                    wmul("dve", oa, oab, 0)
                    wmul("pool", t1, t1b, 1)
                    wmul("pool", t2, t2b, 2)
                    c.op("dve", lambda e: e.tensor_tensor(out=oa[:], in0=oa[:], in1=t1[:], op=ALU.add), reads=[oab, t1b], writes=[oab])
                    c.op("dve", lambda e: e.tensor_tensor(out=oa[:], in0=oa[:], in1=t2[:], op=ALU.add), reads=[oab, t2b], writes=[oab])
                    return (oa, oab, ob_, obb_)

                def trans(ti, bufs):
                    oa, oab, ob_, obb_ = bufs
                    tsl = slice(ti * 128, (ti + 1) * 128)
                    transpose_to(oa, oab, 4, oaT, oaTb, tsl, psrot)
                    transpose_to(ob_, obb_, 8, obT, obTb, tsl, psrot)

                def branch(blk):
                    for pc in range(4):
                        n = blk * 4 + pc
                        if n + 1 < len(jobs):
                            loadw(n + 1)
                        a_, ab, b_, bb = slots.pop(n)
                        for c2 in range(2):
                            cc = pc * 2 + c2
                            pA, pAb = psrot.next()
                            for kc in range(4):
                                c.op("pe", lambda e, kc=kc: e.matmul(pA[:, :], a_[:, kc, c2 * 128:(c2 + 1) * 128], oaT[:, kc, :],
                                                                     start=(kc == 0), stop=(kc == 3)),
                                     reads=[ab, oaTb], writes=[pAb], last=(kc == 3))
                            pB, pBb = psrot.next()
                            for kc in range(8):
                                c.op("pe", lambda e, kc=kc: e.matmul(pB[:, :], b_[:, kc, c2 * 128:(c2 + 1) * 128], obT[:, kc, :],
                                                                     start=(kc == 0), stop=(kc == 7)),
                                     reads=[bb, obTb], writes=[pBb], last=(kc == 7))
                            tm, tmb = tmpm.next()
                            c.op("dve", lambda e: e.tensor_tensor(out=tm[:], in0=pA[:, :], in1=gaT[:, cc, :], op=ALU.mult),
                                 reads=[pAb, gaTb], writes=[tmb])
                            c.op("dve", lambda e: e.tensor_tensor(out=mT[:, cc, :], in0=pB[:, :], in1=gbT[:, cc, :], op=ALU.mult),
                                 reads=[pBb, gbTb], writes=[mTb])
                            c.op("pool", lambda e: e.tensor_tensor(out=mT[:, cc, :], in0=mT[:, cc, :], in1=tm[:], op=ALU.add),
                                 reads=[mTb, tmb], writes=[mTb])

                def outproj(ti):
                    for half in range(2):
                        pX, pXb = psrot.next()
                        for cc in range(8):
                            c.op("pe", lambda e, cc=cc: e.matmul(pX[:, :], mT[:, cc, ti * 128:(ti + 1) * 128],
                                                                 wo[:, cc, half * 512:(half + 1) * 512],
                                                                 start=(cc == 0), stop=(cc == 7)),
                                 reads=[mTb, wob], writes=[pXb], last=(cc == 7))
                        c.op("dve", lambda e: e.tensor_tensor(out=xt[:, ti, half * 512:(half + 1) * 512], in0=pX[:, :],
                                                              in1=xt[:, ti, half * 512:(half + 1) * 512], op=ALU.add),
                             reads=[pXb, xtb], writes=[xtb])

                loadw(0)
                gates(0)
                for ti in range(BT // 128):
                    trans(ti, combine(0, ti))
                for blk in range(nblk):
                    t0 = blk * BT
                    c.dma("sp", xt[:], xsrc[t0:t0 + BT, :].rearrange("(a p) c -> p a c", p=128), writes=[xtb])
                    branch(blk)
                    nxt = blk + 1 < nblk
                    if nxt:
                        gates(blk + 1)
                    for ti in range(BT // 128):
                        bufs = combine(blk + 1, ti) if nxt else None
                        outproj(ti)
                        if nxt:
                            trans(ti, bufs)
                    c.dma("pool", S_["xb"][t0:t0 + BT, :].rearrange("(a p) c -> p a c", p=128), xt[:], reads=[xtb])
                c.barrier()
            c.es = ges

        def phase_E(l, xdst, final):
            BT = 384
            blocks = []
            t = 0
            while t < TOK:
                n_ = min(BT, TOK - t)
                blocks.append((t, n_ // 128))
                t += n_
            with ExitStack() as es:
                c.es = es
                gbc, gbcb = c.sb("gbcE", [128, D])
                c.dma("sp", gbc[:], W["nfg"][l].partition_broadcast(128), writes=[gbcb])
                gfn, gfnb = c.sb("gfn", [128, D])
                if final:
                    c.dma("sp", gfn[:], W["fng"].partition_broadcast(128), writes=[gfnb])
                xts = Rot([c.sb("xtE", [128, 3, D]) for _ in range(2)])
                h2h = Rot([c.sb("h2h", [128, 8, BT], F32R) for _ in range(2)])
                h2l = Rot([c.sb("h2l", [128, 8, BT], F32R) for _ in range(2)])
                hts = Rot([c.sb("htE", [128, D]) for _ in range(2)])
                sts = Rot([c.sb("stE", [128, 3]) for _ in range(2)])
                w1r = Rot([c.sb("w1r", [128, 8, 256]) for _ in range(2)])
                w1h = Rot([c.sb("w1h", [128, 8, 256], F32R) for _ in range(2)])
                w1l = Rot([c.sb("w1l", [128, 8, 256], F32R) for _ in range(2)])
                w2r = Rot([c.sb("w2r", [128, 2, D]) for _ in range(2)])
                w2h = Rot([c.sb("w2h", [128, 2, D], F32R) for _ in range(2)])
                w2l = Rot([c.sb("w2l", [128, 2, D], F32R) for _ in range(2)])
                rls = Rot([c.sb("rl", [128, BT]) for _ in range(2)])
                sqs = Rot([c.sb("sq", [128, BT]) for _ in range(2)])
                hhs = Rot([c.sb("hh", [128, BT], F32R) for _ in range(3)])
                hls = Rot([c.sb("hl", [128, BT], F32R) for _ in range(3)])
                acc = banks[0:6]
                ps67 = Rot(banks[6:8])

                def mm3(ps_ap, psb, A, B, first, lastk, inc):
                    (ah, ahb), (al, alb) = A
                    (bh, bhb), (bl_, blb_) = B
                    c.op("pe", lambda e: e.matmul(ps_ap, ah, bh, start=first, stop=False), reads=[ahb, bhb], writes=[psb], last=False)
                    c.op("pe", lambda e: e.matmul(ps_ap, ah, bl_, start=False, stop=False), reads=[ahb, blb_], writes=[psb], last=False)
                    c.op("pe", lambda e: e.matmul(ps_ap, al, bh, start=False, stop=lastk), reads=[alb, bhb], writes=[psb], last=inc)

                jobs = [(bi_, pj) for bi_ in range(len(blocks)) for pj in range(16)]
                wsl = {}

                raws = {}

                def dmaw(n):
                    if n >= len(jobs):
                        return
                    bi_, pj = jobs[n]
                    r1, r1b = w1r.next()
                    r2, r2b = w2r.next()
                    c.dma("sp", r1[:], W["w1"][l][:, pj * 256:(pj + 1) * 256].rearrange("(kc p) c -> p kc c", p=128), writes=[r1b])
                    c.dma("sp", r2[:], W["w2"][l][pj * 256:(pj + 1) * 256, :].rearrange("(fc p) c -> p fc c", p=128), writes=[r2b])
                    raws[n] = (r1, r1b, r2, r2b)

                def loadw(n):
                    r1, r1b, r2, r2b = raws.pop(n)
                    a1, a1b = w1h.next()
                    l1, l1b = w1l.next()
                    a2, a2b = w2h.next()
                    l2, l2b = w2l.next()
                    c.op("act", lambda e: e.activation(out=a1[:], in_=r1[:], func=AF.Copy), reads=[r1b], writes=[a1b])
                    c.op("dve", lambda e: e.tensor_tensor(out=l1[:], in0=r1[:], in1=a1[:].bitcast(F32), op=ALU.subtract),
                         reads=[r1b, a1b], writes=[l1b])
                    c.op("act", lambda e: e.activation(out=a2[:], in_=r2[:], func=AF.Copy), reads=[r2b], writes=[a2b])
                    c.op("dve", lambda e: e.tensor_tensor(out=l2[:], in0=r2[:], in1=a2[:].bitcast(F32), op=ALU.subtract),
                         reads=[r2b, a2b], writes=[l2b])
                    wsl[n] = (a1, a1b, l1, l1b, a2, a2b, l2, l2b)

                BS = {}

                def pro_load(bi_):
                    t0, nt = blocks[bi_]
                    xt, xtb = xts.next()
                    hh_, hhb_ = h2h.next()
                    hl_, hlb_ = h2l.next()
                    c.dma("sp", xt[:, 0:nt, :], S_["xb"][t0:t0 + nt * 128, :].rearrange("(a p) c -> p a c", p=128), writes=[xtb])
                    BS[bi_] = (xt, xtb, hh_, hhb_, hl_, hlb_)

                def pro_tile(bi_, ti):
                    xt, xtb, hh_, hhb_, hl_, hlb_ = BS[bi_]
                    ht, hb = hts.next()
                    st, stb = sts.next()
                    rms_stats(xt[:, ti, :], xtb, ht[:], hb, st, stb, D)
                    c.op("dve", lambda e: e.scalar_tensor_tensor(out=ht[:], in0=xt[:, ti, :], scalar=st[:, 2:3], in1=gbc[:],
                                                                 op0=ALU.mult, op1=ALU.mult),
                         reads=[xtb, stb, gbcb], writes=[hb])
                    tsl = slice(ti * 128, (ti + 1) * 128)
                    for c0_ in range(0, 8, 4):
                        ps, psb = ps67.next()
                        for j in range(4):
                            c.op("pe", lambda e, j=j: e.transpose(ps[:, j * 128:(j + 1) * 128],
                                                                  ht[:, (c0_ + j) * 128:(c0_ + j + 1) * 128], ident[:]),
                                 reads=[hb, identb], writes=[psb], last=(j == 3))
                        ps3 = ps[:, :].rearrange("p (a b) -> p a b", a=4)
                        c.op("act", lambda e: e.activation(out=hh_[:, c0_:c0_ + 4, tsl], in_=ps3, func=AF.Copy),
                             reads=[psb], writes=[hhb_])
                        c.op("dve", lambda e: e.tensor_tensor(out=hl_[:, c0_:c0_ + 4, tsl], in0=ps3,
                                                              in1=hh_[:, c0_:c0_ + 4, tsl].bitcast(F32), op=ALU.subtract),
                             reads=[psb, hhb_], writes=[hlb_])

                dmaw(0)
                loadw(0)
                dmaw(1)
                pro_load(0)
                for ti in range(blocks[0][1]):
                    pro_tile(0, ti)
                n = 0
                for bi_, (t0, nt) in enumerate(blocks):
                    ntok = nt * 128
                    xt, xtb, hh_, hhb_, hl_, hlb_ = BS[bi_]
                    nxt = bi_ + 1 < len(blocks)
                    pend_ff2 = None

                    def ff2(fc, c2, hbuf, wts):
                        a1, a1b, l1, l1b, a2, a2b, l2, l2b = wts
                        hh, hhb, hl, hlb = hbuf
                        for ti in range(nt):
                            for half in range(2):
                                pa_, pab_ = acc[ti * 2 + half]
                                lastone = (ti == nt - 1 and half == 1)
                                mm3(pa_[:, :], pab_,
                                    ((hh[:, ti * 128:(ti + 1) * 128], hhb), (hl[:, ti * 128:(ti + 1) * 128], hlb)),
                                    ((a2[:, c2, half * 512:(half + 1) * 512], a2b), (l2[:, c2, half * 512:(half + 1) * 512], l2b)),
                                    fc == 0, fc == 31, lastone)

                    for pj in range(16):
                        wts = wsl.pop(n)
                        n += 1
                        a1, a1b, l1, l1b = wts[0:4]
                        if nxt and pj == 0:
                            pro_load(bi_ + 1)
                        for c2 in range(2):
                            fc = pj * 2 + c2
                            ps, psb = ps67.next()
                            for kc in range(8):
                                mm3(ps[:, 0:ntok], psb,
                                    ((a1[:, kc, c2 * 128:(c2 + 1) * 128], a1b), (l1[:, kc, c2 * 128:(c2 + 1) * 128], l1b)),
                                    ((hh_[:, kc, 0:ntok], hhb_), (hl_[:, kc, 0:ntok], hlb_)), kc == 0, kc == 7, kc == 7)
                            if pend_ff2 is not None:
                                ff2(*pend_ff2)
                                pend_ff2 = None
                            if c2 == 0 and n < len(jobs):
                                loadw(n)
                                dmaw(n + 1)
                            r_, rb_ = rls.next()
                            q_, qb_ = sqs.next()
                            hh, hhb = hhs.next()
                            hl, hlb = hls.next()
                            c.op("act", lambda e: e.activation(out=r_[:, 0:ntok], in_=ps[:, 0:ntok], func=AF.Relu), reads=[psb], writes=[rb_])
                            c.op("dve", lambda e: e.tensor_tensor(out=q_[:, 0:ntok], in0=r_[:, 0:ntok], in1=r_[:, 0:ntok], op=ALU.mult),
                                 reads=[rb_], writes=[qb_])
                            c.op("act", lambda e: e.activation(out=hh[:, 0:ntok], in_=q_[:, 0:ntok], func=AF.Copy), reads=[qb_], writes=[hhb])
                            c.op("dve", lambda e: e.tensor_tensor(out=hl[:, 0:ntok], in0=q_[:, 0:ntok], in1=hh[:, 0:ntok].bitcast(F32),
                                                                   op=ALU.subtract), reads=[qb_, hhb], writes=[hlb])
                            pend_ff2 = (fc, c2, (hh, hhb, hl, hlb), wts)
                        if nxt and 9 <= pj < 9 + blocks[bi_ + 1][1]:
                            pro_tile(bi_ + 1, pj - 9)
                    ff2(*pend_ff2)
                    pend_ff2 = None
                    for ti in range(nt):
                        for half in range(2):
                            pa_, pab_ = acc[ti * 2 + half]
                            c.op("dve", lambda e: e.tensor_tensor(out=xt[:, ti, half * 512:(half + 1) * 512], in0=pa_[:, :],
                                                                  in1=xt[:, ti, half * 512:(half + 1) * 512], op=ALU.add),
                                 reads=[pab_, xtb], writes=[xtb])
                    if final:
                        for ti in range(nt):
                            ht, hb = hts.next()
                            st, stb = sts.next()
                            rms_stats(xt[:, ti, :], xtb, ht[:], hb, st, stb, D)
                            c.op("dve", lambda e: e.scalar_tensor_tensor(out=ht[:], in0=xt[:, ti, :], scalar=st[:, 2:3], in1=gfn[:],
                                                                         op0=ALU.mult, op1=ALU.mult),
                                 reads=[xtb, stb, gfnb], writes=[hb])
                            c.dma("pool", xdst[t0 + ti * 128:t0 + (ti + 1) * 128, :], ht[:], reads=[hb])
                    else:
                        c.dma("pool", xdst[t0:t0 + ntok, :].rearrange("(a p) c -> p a c", p=128), xt[:, 0:nt, :], reads=[xtb])
                    del BS[bi_]
                c.barrier()
            c.es = ges

        for l in range(depth):
            xsrc = x_in if l == 0 else S_["xa"]
            final = (l == depth - 1)
            if "A" in phases:
                phase_A(l, xsrc)
            if "B" in phases:
                phase_B(l)
            if "C" in phases:
                phase_C(l, 1)
            if "c" in phases:
                phase_C(l, 0)
            if "D" in phases:
                phase_D(l, xsrc)
            if "E" in phases:
                phase_E(l, y_out if final else S_["xa"], final)
        c.barrier(final=True)
    return nc


_PROG_CACHE = {}


def kernel(x_prompt, x_sample, norm_mix_g, w_in, gla_w_gate_fwd, gla_b_gate_fwd, gla_w_gate_bwd, gla_b_gate_bwd,
           gla_norm_g, w_branch_a, w_branch_b, w_out, norm_ffn_g, w_ff1, w_ff2, final_norm_g):
    f32 = np.float32
    xp = np.asarray(x_prompt, f32)
    xs = np.asarray(x_sample, f32)
    nb_p, sp = xp.shape[0], xp.shape[1]
    nb_s, ss = xs.shape[0], xs.shape[1]
    pp, ps_ = nb_p // N_CORES, nb_s // N_CORES
    seqs = [sp] * pp + [ss] * ps_
    key = tuple(seqs)
    if key not in _PROG_CACHE:
        _PROG_CACHE[key] = build_program(seqs)
    nc = _PROG_CACHE[key]
    shared = {
        "norm_mix_g": norm_mix_g, "w_in": w_in, "gla_w_gate_fwd": gla_w_gate_fwd, "gla_b_gate_fwd": gla_b_gate_fwd,
        "gla_w_gate_bwd": gla_w_gate_bwd, "gla_b_gate_bwd": gla_b_gate_bwd, "gla_norm_g": gla_norm_g,
        "w_branch_a": w_branch_a, "w_branch_b": w_branch_b, "w_out": w_out, "norm_ffn_g": norm_ffn_g,
        "w_ff1": w_ff1, "w_ff2": w_ff2, "final_norm_g": final_norm_g,
    }
    shared = {k: np.ascontiguousarray(np.asarray(v, f32)) for k, v in shared.items()}
    shared.update(host_consts())
    in_maps = []
    for i in range(N_CORES):
        xcore = np.concatenate([xp[i * pp:(i + 1) * pp].reshape(pp * sp, D), xs[i * ps_:(i + 1) * ps_].reshape(ps_ * ss, D)], axis=0)
        m = dict(shared)
        m["x"] = np.ascontiguousarray(xcore)
        in_maps.append(m)
    res = run_bass_kernel_spmd(nc, in_maps, core_ids=list(range(N_CORES)))
    yp = np.empty((nb_p, sp, D), f32)
    ys = np.empty((nb_s, ss, D), f32)
    for i in range(N_CORES):
        y = np.asarray(res.results[i]["y"])
        yp[i * pp:(i + 1) * pp] = y[:pp * sp].reshape(pp, sp, D)
        ys[i * ps_:(i + 1) * ps_] = y[pp * sp:].reshape(ps_, ss, D)
    return (yp, ys)
```

```python
import numpy as np
from contextlib import ExitStack
import concourse.bass as bass
import concourse.mybir as mybir
from concourse.bass_utils import run_bass_kernel_spmd

F32 = mybir.dt.float32
F32R = mybir.dt.float32r
AF = mybir.ActivationFunctionType
ALU = mybir.AluOpType
AX = mybir.AxisListType

D = 1024
DEPTH = 2
A_GROUPS = ((128, 1), (512, 4), (2048, 16))
IN_COLS = 9760
C_QA, C_KA, C_VA, C_QB, C_KB, C_VB, C_RB, C_GL, C_GA, C_GB = 0, 1536, 3072, 4608, 5120, 5632, 6656, 7680, 7712, 8736
D_FF = 4096
EPS = 1e-6
NEG = -1e30
PAD = 1024
OAW = 66
N_CORES = 8
CUT = 99


class Buf:
    __slots__ = ("name", "w", "r", "excl")

    def __init__(self, name, excl=False):
        self.name = name
        self.w = None
        self.r = {}
        self.excl = excl


class Ctx:
    WINDOW = 4
    NRING = 12

    def __init__(self, nc, es):
        self.nc = nc
        self.es = es
        self.eng = {"pe": nc.tensor, "dve": nc.vector, "act": nc.scalar,
                    "pool": nc.gpsimd, "sp": nc.sync}
        self.sem = {k: es.enter_context(nc.semaphore("s_" + k)) for k in self.eng}
        self.cnt = {k: 0 for k in self.eng}
        self.seen = {k: {} for k in self.eng}
        self.pend = {k: ([], []) for k in self.eng}
        self.ring = {q: [es.enter_context(nc.semaphore("r_%s%d" % (q, i))) for i in range(self.NRING)]
                     for q in ("sp", "pool")}
        self.ringcnt = {"sp": 0, "pool": 0}
        self.uid = 0

    def sb(self, name, shape, dtype=F32):
        self.uid += 1
        t = self.es.enter_context(self.nc.sbuf_tensor("%s_%d" % (name, self.uid), list(shape), dtype))
        return t, Buf(name)

    def _semof(self, key):
        if isinstance(key, str):
            return self.sem[key]
        return self.ring[key[1]][key[2]]

    def _wait(self, e, key, val):
        if self.seen[e].get(key, 0) >= val:
            return
        if key == e:
            if e == "pe":
                return
            if val <= self.cnt[e] - self.WINDOW:
                return
        self.eng[e].wait_ge(self._semof(key), val)
        self.seen[e][key] = val

    def _deps(self, e, reads, writes):
        for b in reads:
            if b.w is not None:
                self._wait(e, b.w[0], b.w[1])
            if b.excl:
                for k, v in b.r.items():
                    if k != e:
                        self._wait(e, k, v)
        for b in writes:
            if b.w is not None:
                self._wait(e, b.w[0], b.w[1])
            for k, v in b.r.items():
                self._wait(e, k, v)

    def _mark(self, ev, reads, writes):
        for b in reads:
            if b.r.get(ev[0], 0) < ev[1]:
                b.r[ev[0]] = ev[1]
        for b in writes:
            b.w = ev
            b.r = {}

    def op(self, e, fn, reads=(), writes=(), last=True):
        self._deps(e, reads, writes)
        ins = fn(self.eng[e])
        pr, pw = self.pend[e]
        pr.extend(reads)
        pw.extend(writes)
        if last:
            self.cnt[e] += 1
            ins.then_inc(self.sem[e], 1)
            self._mark((e, self.cnt[e]), pr, pw)
            self.pend[e] = ([], [])
        return ins

    def dma(self, q, out, in_, reads=(), writes=()):
        self._deps(q, reads, writes)
        i = self.ringcnt[q]
        slot, rnd = i % self.NRING, i // self.NRING
        key = ("q", q, slot)
        if rnd > 0:
            self._wait(q, key, 16 * rnd)
        self.eng[q].dma_start(out=out, in_=in_).then_inc(self.ring[q][slot], 16)
        self.ringcnt[q] += 1
        self._mark((key, 16 * (rnd + 1)), reads, writes)

    def barrier(self, final=False):
        for e in self.eng:
            assert not self.pend[e][0] and not self.pend[e][1]
        evs = [(k, self.cnt[k]) for k in self.eng if self.cnt[k] > 0]
        for q in ("sp", "pool"):
            n = self.ringcnt[q]
            for slot in range(min(n, self.NRING)):
                rounds = (n - slot + self.NRING - 1) // self.NRING
                evs.append((("q", q, slot), 16 * rounds))
        engs = ["sp"] if final else list(self.eng)
        for e in engs:
            for k, v in evs:
                if k == e or self.seen[e].get(k, 0) >= v:
                    continue
                self.eng[e].wait_ge(self._semof(k), v)
                self.seen[e][k] = v


def ssl(start, count, step):
    return slice(start, start + (count - 1) * step + 1, step)


class Rot:
    def __init__(self, items):
        self.items = items
        self.i = 0

    def next(self):
        it = self.items[self.i % len(self.items)]
        self.i += 1
        return it


def host_consts():
    q = np.arange(128)[:, None]
    kk = np.arange(256)[None, :]
    delta = kk - 64 - q
    ad = np.abs(delta).astype(np.float64)
    valid = ad <= 64
    nh = 24
    slopes = np.exp2(-8.0 * np.arange(1, nh + 1, dtype=np.float64) / nh)
    abias = np.zeros((nh, 128, 256), np.float32)
    for g, (win, dil) in enumerate(A_GROUPS):
        for h in range(8):
            b = -slopes[g * 8 + h] * ad * dil
            abias[g * 8 + h] = np.where(valid, b, NEG).astype(np.float32)
    edge = np.zeros((2, 128, 256), np.float32)
    edge[0][:, :64] = NEG
    edge[1][:, 192:] = NEG
    s = np.arange(128)[:, None]
    t = np.arange(128)[None, :]
    tri = np.zeros((2, 128, 128), np.float32)
    tri[0] = np.where(s <= t, -1.0 / 16.0, 0.0)
    tri[1] = np.where(s >= t, -1.0 / 16.0, 0.0)
    msk = np.zeros((2, 128, 128), np.float32)
    msk[0] = (s <= t)
    msk[1] = (s > t)
    return {"c_ident": np.eye(128, dtype=np.float32), "c_abias": abias, "c_edge": edge,
            "c_tri": tri, "c_msk": msk, "c_ones": np.ones((1, 512), np.float32)}


def build_program(seqs, depth=DEPTH, dbg=False, phases="ABCcDE"):
    nc = bass.Bass("TRN2", target_bir_lowering=False)
    TOK = sum(seqs)
    offs = [sum(seqs[:i]) for i in range(len(seqs))]
    SMAX = max(seqs)

    def din(name, shape):
        return nc.dram_tensor(name, list(shape), F32, kind="ExternalInput").ap()

    def dscr(name, shape):
        return nc.dram_tensor(name, list(shape), F32, kind=("ExternalOutput" if dbg else "Internal")).ap()

    x_in = din("x", [TOK, D])
    W = {
        "norm_mix_g": din("norm_mix_g", [depth, D]),
        "w_in": din("w_in", [depth, D, IN_COLS]),
        "gwf": din("gla_w_gate_fwd", [depth, 16, 512]),
        "gbf": din("gla_b_gate_fwd", [depth, 512]),
        "gwb": din("gla_w_gate_bwd", [depth, 16, 512]),
        "gbb": din("gla_b_gate_bwd", [depth, 512]),
        "gng": din("gla_norm_g", [depth, 256]),
        "wa": din("w_branch_a", [depth, 512, D]),
        "wb": din("w_branch_b", [depth, D, D]),
        "wo": din("w_out", [depth, D, D]),
        "nfg": din("norm_ffn_g", [depth, D]),
        "w1": din("w_ff1", [depth, D, D_FF]),
        "w2": din("w_ff2", [depth, D_FF, D]),
        "fng": din("final_norm_g", [D]),
    }
    c_ident = din("c_ident", [128, 128])
    c_abias = din("c_abias", [24, 128, 256])
    c_edge = din("c_edge", [2, 128, 256])
    c_tri = din("c_tri", [2, 128, 128])
    c_msk = din("c_msk", [2, 128, 128])
    c_ones = din("c_ones", [1, 512])
    y_out = nc.dram_tensor("y", [TOK, D], F32, kind="ExternalOutput").ap()

    S_ = {
        "qaT": dscr("s_qaT", [1536, TOK]), "kaT": dscr("s_kaT", [1536, TOK]), "va": dscr("s_va", [TOK, 1536]),
        "qbT": dscr("s_qbT", [512, TOK]), "kbT": dscr("s_kbT", [512, TOK]), "vb": dscr("s_vb", [TOK, 1024]),
        "rb": dscr("s_rb", [TOK, 1024]), "lrT": dscr("s_lrT", [32, TOK]),
        "gaT": dscr("s_gaT", [1024, TOK]), "gbT": dscr("s_gbT", [1024, TOK]),
        "oag": dscr("s_oag", [TOK, 24, OAW]), "obb": dscr("s_obb", [TOK, 1024]), "ob": dscr("s_ob", [TOK, 1024]),
        "xa": dscr("s_xa", [TOK, D]), "xb": dscr("s_xb", [TOK, D]),
    }

    with ExitStack() as ges:
        c = Ctx(nc, ges)
        ident, identb = c.sb("ident", [128, 128])
        c.dma("sp", ident[:], c_ident, writes=[identb])
        banks = []
        for i in range(8):
            t = ges.enter_context(nc.psum_tensor("psb%d" % i, [128, 512], F32))
            banks.append((t, Buf("psb%d" % i, excl=True)))
        evac_rr = [0]

        def evac(out, in_, reads, writes, eng=None):
            if eng is None:
                eng = ("act", "dve")[evac_rr[0] % 2]
                evac_rr[0] += 1
            if eng == "act":
                c.op("act", lambda e: e.activation(out=out, in_=in_, func=AF.Copy), reads=reads, writes=writes)
            else:
                c.op("dve", lambda e: e.tensor_copy(out, in_), reads=reads, writes=writes)

        def rms_stats(xt_ap, xb, junk, jb, st, stb, n):
            c.op("act", lambda e: e.activation(out=junk, in_=xt_ap, func=AF.Square, accum_out=st[:, 0:1]),
                 reads=[xb], writes=[jb, stb])
            c.op("act", lambda e: e.activation(out=st[:, 1:2], in_=st[:, 0:1], func=AF.Ln, scale=1.0 / n, bias=EPS),
                 reads=[stb], writes=[stb])
            c.op("act", lambda e: e.activation(out=st[:, 2:3], in_=st[:, 1:2], func=AF.Exp, scale=-0.5),
                 reads=[stb], writes=[stb])

        def transpose_to(src, srcb, nchunks, dstT, dstb, tsl, psrot):
            for c0 in range(0, nchunks, 4):
                nn = min(4, nchunks - c0)
                ps, psb = psrot.next()
                for j in range(nn):
                    c.op("pe", lambda e, j=j: e.transpose(ps[:, j * 128:(j + 1) * 128],
                                                          src[:, (c0 + j) * 128:(c0 + j + 1) * 128], ident[:]),
                         reads=[srcb, identb], writes=[psb], last=(j == nn - 1))
                evac(dstT[:, c0:c0 + nn, tsl], ps[:, 0:nn * 128].rearrange("p (a b) -> p a b", a=nn),
                     reads=[psb], writes=[dstb])

        def phase_A(l, xsrc):
            TA = 1024 if TOK % 1024 == 0 else 512
            panels = []
            for (c0, n, form, dst) in ((C_QA, 1536, "F", "qaT"), (C_KA, 1536, "F", "kaT"), (C_VA, 1536, "T", "va"),
                                       (C_QB, 512, "F", "qbT"), (C_KB, 512, "F", "kbT"), (C_VB, 1024, "T", "vb"),
                                       (C_RB, 1024, "T", "rb"), (C_GL, 32, "F", "lrT"), (C_GA, 1024, "F", "gaT"),
                                       (C_GB, 1024, "F", "gbT")):
                for p0 in range(0, n, 512):
                    panels.append((c0 + p0, min(512, n - p0), form, dst, p0))
            with ExitStack() as es:
                c.es = es
                gbc, gbcb = c.sb("gbc", [128, D])
                c.dma("sp", gbc[:], W["norm_mix_g"][l].partition_broadcast(128), writes=[gbcb])
                hTh, hThb = c.sb("hTh", [128, 8, TA], F32R)
                hTl, hTlb = c.sb("hTl", [128, 8, TA], F32R)
                xts = Rot([c.sb("xt", [128, D]) for _ in range(2)])
                hts = Rot([c.sb("ht", [128, D]) for _ in range(2)])
                sts = Rot([c.sb("st", [128, 3]) for _ in range(2)])
                junk, junkb = c.sb("junk", [128, D])
                wp = Rot([c.sb("wpan", [128, 8, 512]) for _ in range(2)])
                wph = Rot([c.sb("wph", [128, 8, 512], F32R) for _ in range(2)])
                wpl = Rot([c.sb("wpl", [128, 8, 512], F32R) for _ in range(2)])
                stg = Rot([c.sb("stg", [128, 512]) for _ in range(4)])
                psrot = Rot(banks)
                jobs = [(blk, p) for blk in range(TOK // TA) for p in panels]
                slots = {}

                rawsA = {}

                def dmaA(n):
                    if n >= len(jobs):
                        return
                    blk, (c0, ncols, form, dst, p0) = jobs[n]
                    sl, slb = wp.next()
                    c.dma("sp", sl[:, :, 0:ncols], W["w_in"][l][:, c0:c0 + ncols].rearrange("(kc p) c -> p kc c", p=128),
                          writes=[slb])
                    rawsA[n] = (sl, slb)

                def load(n):
                    blk, (c0, ncols, form, dst, p0) = jobs[n]
                    sl, slb = rawsA.pop(n)
                    wh, whb = wph.next()
                    wl, wlb = wpl.next()
                    c.op("act", lambda e: e.activation(out=wh[:, :, 0:ncols], in_=sl[:, :, 0:ncols], func=AF.Copy),
                         reads=[slb], writes=[whb])
                    c.op("dve", lambda e: e.tensor_tensor(out=wl[:, :, 0:ncols], in0=sl[:, :, 0:ncols],
                                                           in1=wh[:, :, 0:ncols].bitcast(F32), op=ALU.subtract),
                         reads=[slb, whb], writes=[wlb])
                    slots[n] = (wh, whb, wl, wlb)

                def mm3(ps_ap, psb, A, B, first, lastk):
                    (ah, ahb), (al, alb) = A
                    (bh, bhb), (bl_, blb_) = B
                    c.op("pe", lambda e: e.matmul(ps_ap, ah, bh, start=first, stop=False), reads=[ahb, bhb], writes=[psb], last=False)
                    c.op("pe", lambda e: e.matmul(ps_ap, ah, bl_, start=False, stop=False), reads=[ahb, blb_], writes=[psb], last=False)
                    c.op("pe", lambda e: e.matmul(ps_ap, al, bh, start=False, stop=lastk), reads=[alb, bhb], writes=[psb], last=lastk)

                dmaA(0)
                load(0)
                dmaA(1)
                for n, (blk, (c0, ncols, form, dst, p0)) in enumerate(jobs):
                    t0 = blk * TA
                    if n % len(panels) == 0:
                        for ti in range(TA // 128):
                            xt, xb = xts.next()
                            ht, hb = hts.next()
                            st, stb = sts.next()
                            c.dma("sp", xt[:], xsrc[t0 + ti * 128:t0 + (ti + 1) * 128, :], writes=[xb])
                            rms_stats(xt[:], xb, junk[:], junkb, st, stb, D)
                            c.op("dve", lambda e: e.scalar_tensor_tensor(out=ht[:], in0=xt[:], scalar=st[:, 2:3], in1=gbc[:],
                                                                         op0=ALU.mult, op1=ALU.mult),
                                 reads=[xb, stb, gbcb], writes=[hb])
                            tsl = slice(ti * 128, (ti + 1) * 128)
                            for c0_ in range(0, 8, 4):
                                ps, psb = psrot.next()
                                for j in range(4):
                                    c.op("pe", lambda e, j=j: e.transpose(ps[:, j * 128:(j + 1) * 128],
                                                                          ht[:, (c0_ + j) * 128:(c0_ + j + 1) * 128], ident[:]),
                                         reads=[hb, identb], writes=[psb], last=(j == 3))
                                ps3 = ps[:, :].rearrange("p (a b) -> p a b", a=4)
                                c.op("act", lambda e: e.activation(out=hTh[:, c0_:c0_ + 4, tsl], in_=ps3, func=AF.Copy),
                                     reads=[psb], writes=[hThb])
                                c.op("dve", lambda e: e.tensor_tensor(out=hTl[:, c0_:c0_ + 4, tsl], in0=ps3,
                                                                      in1=hTh[:, c0_:c0_ + 4, tsl].bitcast(F32), op=ALU.subtract),
                                     reads=[psb, hThb], writes=[hTlb])
                    if n + 1 < len(jobs):
                        load(n + 1)
                        dmaA(n + 2)
                    wh, whb, wl, wlb = slots.pop(n)
                    if form == "F":
                        for cc in range((ncols + 127) // 128):
                            m = min(128, ncols - cc * 128)
                            csl = slice(cc * 128, cc * 128 + m)
                            for tb in range(TA // 512):
                                ps, psb = psrot.next()
                                tsl = slice(tb * 512, (tb + 1) * 512)
                                for kc in range(8):
                                    mm3(ps[0:m, :], psb, ((wh[:, kc, csl], whb), (wl[:, kc, csl], wlb)),
                                        ((hTh[:, kc, tsl], hThb), (hTl[:, kc, tsl], hTlb)), kc == 0, kc == 7)
                                sg, sgb = stg.next()
                                evac(sg[0:m, :], ps[0:m, :], reads=[psb], writes=[sgb])
                                c.dma("pool", S_[dst][p0 + cc * 128:p0 + cc * 128 + m, t0 + tb * 512:t0 + (tb + 1) * 512],
                                      sg[0:m, :], reads=[sgb])
                    else:
                        for ti in range(TA // 128):
                            ps, psb = psrot.next()
                            tsl = slice(ti * 128, (ti + 1) * 128)
                            for kc in range(8):
                                mm3(ps[:, 0:ncols], psb, ((hTh[:, kc, tsl], hThb), (hTl[:, kc, tsl], hTlb)),
                                    ((wh[:, kc, 0:ncols], whb), (wl[:, kc, 0:ncols], wlb)), kc == 0, kc == 7)
                            sg, sgb = stg.next()
                            evac(sg[:, 0:ncols], ps[:, 0:ncols], reads=[psb], writes=[sgb])
                            c.dma("pool", S_[dst][t0 + ti * 128:t0 + (ti + 1) * 128, p0:p0 + ncols], sg[:, 0:ncols],
                                  reads=[sgb])
                c.barrier()
            c.es = ges

        def phase_B(l):
            with ExitStack() as es:
                c.es = es
                NT = max(d * (S // (d * 128) + 1) for S in seqs for (_, d) in A_GROUPS)
                QT = Rot([c.sb("QT", [128, SMAX]) for _ in range(2)])
                KT = Rot([c.sb("KT", [128, SMAX + 2 * PAD]) for _ in range(2)])
                for (qt_, qtb_) in QT.items:
                    c.op("pool", lambda e: e.memset(qt_[64:128, :], 0.0), writes=[qtb_])
                for (kt_, ktb_) in KT.items:
                    c.op("pool", lambda e: e.memset(kt_[64:128, :], 0.0), writes=[ktb_])
                VP = Rot([c.sb("VP", [128, NT, 128]) for _ in range(2)])
                BI = Rot([c.sb("bias", [128, 256]) for _ in range(2)])
                edge, edgeb = c.sb("edge", [128, 2, 256])
                c.dma("sp", edge[:], c_edge.rearrange("a p k -> p a k"), writes=[edgeb])
                for (kt, ktb) in KT.items:
                    c.op("pool", lambda e: e.memset(kt[0:64, 0:PAD], 0.0), writes=[ktb])
                Tt = Rot([c.sb("T", [128, 256]) for _ in range(2)])
                Pt = Rot([c.sb("P", [128, 256]) for _ in range(3)])
                PTt = Rot([c.sb("PT", [128, 256]) for _ in range(3)])
                sm = Rot([c.sb("sm", [128, 4]) for _ in range(5)])
                ot = Rot([c.sb("ot", [128, OAW]) for _ in range(3)])
                psS = Rot(banks[0:2])
                psT = Rot(banks[2:4])
                psO = Rot(banks[4:6])
                jobs = []
                for si, S in enumerate(seqs):
                    for g, (win, d) in enumerate(A_GROUPS):
                        for h in range(8):
                            jobs.append((si, S, g, d, h))
                st = {}

                def load(n):
                    si, S, g, d, h = jobs[n]
                    o = offs[si]
                    sub = S // d
                    nt = sub // 128
                    if h % 2 == 0:
                        vp, vpb = VP.next()
                        st["vp"] = (vp, vpb)
                        vcol = C_QA + g * 512 + (h // 2) * 128
                        vv = vp[:, 0:d * (nt + 1), :].rearrange("p (r a) c -> p r a c", r=d)
                        c.op("pool", lambda e: e.memset(vv[0:64, :, 0, :], 0.0), writes=[vpb])
                        c.op("pool", lambda e: e.memset(vv[64:128, :, nt, :], 0.0), writes=[vpb])
                        va = S_["va"]
                        c.dma("sp", vv[64:128, :, 0, :],
                              va[o:o + 64 * d, vcol:vcol + 128].rearrange("(k r) c -> k r c", r=d), writes=[vpb])
                        c.dma("sp", vv[0:64, :, nt, :],
                              va[o + S - 64 * d:o + S, vcol:vcol + 128].rearrange("(k r) c -> k r c", r=d), writes=[vpb])
                        if nt > 1:
                            for r in range(d):
                                c.dma("sp", vv[:, r, 1:nt, :],
                                      va[ssl(o + 64 * d + r, (nt - 1) * 128, d), vcol:vcol + 128]
                                      .rearrange("(a p) c -> p a c", p=128), writes=[vpb])
                    qt, qtb = QT.next()
                    kt, ktb = KT.next()
                    bi, bib = BI.next()
                    col = g * 512 + h * 64
                    c.dma("sp", qt[0:64, 0:S], S_["qaT"][col:col + 64, o:o + S], writes=[qtb])
                    c.op("pool", lambda e: e.memset(kt[0:64, PAD + S:PAD + S + PAD], 0.0), writes=[ktb])
                    c.dma("sp", kt[0:64, PAD:PAD + S], S_["kaT"][col:col + 64, o:o + S], writes=[ktb])
                    c.dma("sp", bi[:], c_abias[g * 8 + h], writes=[bib])
                    st[n] = (qt, qtb, kt, ktb, bi, bib) + st["vp"]

                tiles = []
                for n, (si, S, g, d, h) in enumerate(jobs):
                    nt = (S // d) // 128
                    first = True
                    for r in range(d):
                        for j in range(nt):
                            tiles.append((n, r, j, first))
                            first = False
                NTL = len(tiles)
                res = {}
                TS = {}

                def stage1a(k):
                    n, r, j, first = tiles[k]
                    if first:
                        if n == 0:
                            load(0)
                        if n + 1 < len(jobs):
                            load(n + 1)
                        res[n] = st.pop(n)
                        res.pop(n - 2, None)
                    si, S, g, d, h = jobs[n]
                    qt, qtb, kt, ktb, bi, bib, vp, vpb = res[n]
                    q0 = r + d * 128 * j
                    k0 = PAD + r + d * (128 * j - 64)
                    ps, psb = psS.next()
                    c.op("pe", lambda e: e.matmul(ps[:, 0:256], qt[:, ssl(q0, 128, d)], kt[:, ssl(k0, 256, d)],
                                                  start=True, stop=True), reads=[qtb, ktb], writes=[psb])
                    T, Tb = Tt.next()
                    c.op("dve", lambda e: e.scalar_tensor_tensor(out=T[:], in0=ps[:, 0:256], scalar=0.125, in1=bi[:],
                                                                 op0=ALU.mult, op1=ALU.add),
                         reads=[psb, bib], writes=[Tb])
                    nt = (S // d) // 128
                    if j == 0:
                        c.op("dve", lambda e: e.tensor_tensor(out=T[:], in0=T[:], in1=edge[:, 0, :], op=ALU.add),
                             reads=[Tb, edgeb], writes=[Tb])
                    if j == nt - 1:
                        c.op("dve", lambda e: e.tensor_tensor(out=T[:], in0=T[:], in1=edge[:, 1, :], op=ALU.add),
                             reads=[Tb, edgeb], writes=[Tb])
                    TS[k] = {"T": (T, Tb), "s4": sm.next(), "P": Pt.next()}

                def stage1b(k):
                    T, Tb = TS[k]["T"]
                    s4, s4b = TS[k]["s4"]
                    P, Pb = TS[k]["P"]
                    c.op("dve", lambda e: e.reduce_max(out=s4[:, 0:1], in_=T[:], axis=AX.X, negate=True),
                         reads=[Tb], writes=[s4b])
                    c.op("act", lambda e: e.activation(out=P[:], in_=T[:], func=AF.Exp, bias=s4[:, 0:1], scale=1.0,
                                                       accum_out=s4[:, 1:2]),
                         reads=[Tb, s4b], writes=[Pb, s4b])

                def stage2(k):
                    P, Pb = TS[k]["P"]
                    pt, ptb = psT.next()
                    for b in range(2):
                        c.op("pe", lambda e, b=b: e.transpose(pt[:, b * 128:(b + 1) * 128], P[:, b * 128:(b + 1) * 128], ident[:]),
                             reads=[Pb, identb], writes=[ptb], last=(b == 1))
                    PT, PTb = PTt.next()
                    c.op("act", lambda e: e.activation(out=PT[:], in_=pt[:, 0:256], func=AF.Copy), reads=[ptb], writes=[PTb])
                    TS[k]["PT"] = (PT, PTb)

                def stage3a(k):
                    n, r, j, first = tiles[k]
                    si, S, g, d, h = jobs[n]
                    nt = (S // d) // 128
                    hh = h % 2
                    vp, vpb = res[n][6], res[n][7]
                    PT, PTb = TS[k]["PT"]
                    s4, s4b = TS[k]["s4"]
                    po, pob = psO.next()
                    for b in range(2):
                        c.op("pe", lambda e, b=b: e.matmul(po[:, 0:64], PT[:, b * 128:(b + 1) * 128],
                                                           vp[:, r * (nt + 1) + j + b, hh * 64:(hh + 1) * 64],
                                                           start=(b == 0), stop=(b == 1)),
                             reads=[PTb, vpb], writes=[pob], last=(b == 1))
                    c.op("dve", lambda e: e.reciprocal(s4[:, 2:3], s4[:, 1:2]), reads=[s4b], writes=[s4b])
                    c.op("act", lambda e: e.activation(out=s4[:, 3:4], in_=s4[:, 1:2], func=AF.Ln), reads=[s4b], writes=[s4b])
                    TS[k]["po"] = (po, pob)

                def stage3b(k):
                    n, r, j, first = tiles[k]
                    si, S, g, d, h = jobs[n]
                    o = offs[si]
                    s4, s4b = TS[k]["s4"]
                    po, pob = TS[k]["po"]
                    o_t, o_tb = ot.next()
                    c.op("dve", lambda e: e.tensor_scalar(o_t[:, 0:64], po[:, 0:64], s4[:, 2:3], None, ALU.mult),
                         reads=[pob, s4b], writes=[o_tb])
                    c.op("dve", lambda e: e.tensor_tensor(out=o_t[:, 64:65], in0=s4[:, 3:4], in1=s4[:, 0:1], op=ALU.subtract),
                         reads=[s4b], writes=[o_tb])
                    tq = o + r + d * 128 * j
                    c.dma("pool", S_["oag"][ssl(tq, 128, d), g * 8 + h, 0:65], o_t[:, 0:65], reads=[o_tb])
                    del TS[k]

                for k in range(NTL + 2):
                    if k < NTL:
                        stage1a(k)
                    if 0 <= k - 2 < NTL:
                        stage3a(k - 2)
                    if k < NTL:
                        stage1b(k)
                    if 0 <= k - 1 < NTL:
                        stage2(k - 1)
                    if 0 <= k - 2 < NTL:
                        stage3b(k - 2)
                c.barrier()
            c.es = ges

        def phase_C(l, direction):
            fwd = (direction == 0)
            LNS = float(np.log(128.0 ** -0.5))
            with ExitStack() as es:
                c.es = es
                waug, waugb = c.sb("waug", [32, 512])
                c.op("pool", lambda e: e.memset(waug[:], 0.0), writes=[waugb])
                c.dma("sp", waug[0:16, :], W["gwf" if fwd else "gwb"][l], writes=[waugb])
                c.dma("sp", waug[16:17, :], W["gbf" if fwd else "gbb"][l].unsqueeze(0), writes=[waugb])
                tri, trib = c.sb("tri", [128, 128])
                msk, mskb = c.sb("msk", [128, 128])
                c.dma("sp", tri[:], c_tri[direction], writes=[trib])
                c.dma("sp", msk[:], c_msk[direction], writes=[mskb])
                gnb, gnbb = c.sb("gnb", [128, 256])
                if fwd:
                    c.dma("sp", gnb[:], W["gng"][l].partition_broadcast(128), writes=[gnbb])
                BT = 512
                qbl = Rot([c.sb("qbl", [128, 4, BT]) for _ in range(2)])
                kbl = Rot([c.sb("kbl", [128, 4, BT]) for _ in range(2)])
                lra = Rot([c.sb("lra", [32, BT]) for _ in range(2)])
                for (t_, b_) in lra.items:
                    c.op("pool", lambda e: e.memset(t_[:], 0.0), writes=[b_])
                    c.dma("sp", t_[16:17, :], c_ones, writes=[b_])
                vts = Rot([c.sb("vt", [128, 1024]) for _ in range(3)])
                rbs = Rot([c.sb("rbt", [128, 1024]) for _ in range(3)]) if fwd else None
                obs = Rot([c.sb("obt", [128, 1024]) for _ in range(3)]) if fwd else None
                e1s = Rot([c.sb("e1", [128, 512]) for _ in range(2)])
                Ls = Rot([c.sb("L", [128, 512]) for _ in range(3)])
                eqs = Rot([c.sb("eq", [128, 512]) for _ in range(2)])
                eks = Rot([c.sb("ek", [128, 512]) for _ in range(2)])
                bls = Rot([c.sb("bl", [128, 8]) for _ in range(5)])
                qes = Rot([c.sb("qe", [128, 512]) for _ in range(4)])
                kes = Rot([c.sb("ke", [128, 512]) for _ in range(4)])
                kdTs = Rot([c.sb("kdT", [128, 512]) for _ in range(3)])
                kds = Rot([c.sb("kd", [128, 512]) for _ in range(3)])
                atts = Rot([c.sb("att", [128, 512]) for _ in range(2)])
                osb = Rot([c.sb("osb", [128, 1024]) for _ in range(2)])
                sil = Rot([c.sb("sil", [128, 1024]) for _ in range(2)]) if fwd else None
                junk, junkb = c.sb("junkc", [128, 256])
                nst = Rot([c.sb("nst", [128, 12]) for _ in range(2)])
                pend_e2 = [None]
                states = [c.sb("state", [128, 4, 256]) for _ in range(2)]
                cur = [0]
                psZ, psB_, psKD, psA = banks[0], banks[1], banks[2], banks[3]
                psO = (banks[4], banks[5])
                psSt = (banks[6], banks[7])
                lastcol = 127 if fwd else 0

                for si, S in enumerate(seqs):
                    o = offs[si]
                    nblk = S // BT
                    tiles = []
                    for bi_ in (range(nblk) if fwd else range(nblk - 1, -1, -1)):
                        tl = range(BT // 128)
                        for ti in (tl if fwd else reversed(tl)):
                            tiles.append((bi_, ti))
                    c.op("dve", lambda e: e.memset(states[cur[0]][0][:], 0.0), writes=[states[cur[0]][1]])
                    G = {}
                    blkbuf = {}

                    GS = {}

                    def g1(idx):
                        bi_, ti = tiles[idx]
                        t0 = o + bi_ * BT
                        if bi_ not in blkbuf:
                            qb_, qbb = qbl.next()
                            kb_, kbb = kbl.next()
                            lr_, lrb = lra.next()
                            c.dma("sp", qb_[:], S_["qbT"][:, t0:t0 + BT].rearrange("(h p) t -> p h t", p=128), writes=[qbb])
                            c.dma("sp", kb_[:], S_["kbT"][:, t0:t0 + BT].rearrange("(h p) t -> p h t", p=128), writes=[kbb])
                            c.dma("sp", lr_[0:16, :], S_["lrT"][direction * 16:direction * 16 + 16, t0:t0 + BT], writes=[lrb])
                            blkbuf.clear()
                            blkbuf[bi_] = (qb_, qbb, kb_, kbb, lr_, lrb)
                        qb_, qbb, kb_, kbb, lr_, lrb = blkbuf[bi_]
                        tsl = slice(ti * 128, (ti + 1) * 128)
                        pz, pzb = psZ
                        c.op("pe", lambda e: e.matmul(pz[:, :], lr_[:, tsl], waug[:], start=True, stop=True),
                             reads=[lrb, waugb], writes=[pzb])
                        e1, e1b = e1s.next()
                        Lt, Lb = Ls.next()
                        c.op("act", lambda e: e.activation(out=e1[:], in_=pz[:, :], func=AF.Exp, scale=-1.0),
                             reads=[pzb], writes=[e1b])
                        c.op("act", lambda e: e.activation(out=Lt[:], in_=e1[:], func=AF.Ln, bias=1.0),
                             reads=[e1b], writes=[Lb])
                        GS[idx] = {"blk": (qb_, qbb, kb_, kbb), "tsl": tsl, "L": (Lt, Lb), "tok": t0 + ti * 128}

                    def g2(idx):
                        d_ = GS[idx]
                        qb_, qbb, kb_, kbb = d_["blk"]
                        tsl = d_["tsl"]
                        Lt, Lb = d_["L"]
                        pb, pbb = psB_
                        for h in range(4):
                            c.op("pe", lambda e, h=h: e.matmul(pb[:, h * 128:(h + 1) * 128], Lt[:, h * 128:(h + 1) * 128], tri[:],
                                                               start=True, stop=True),
                                 reads=[Lb, trib], writes=[pbb], last=(h == 3))
                        bl, blb = bls.next()
                        pb3 = pb[:, :].rearrange("p (h t) -> p h t", h=4)
                        c.op("dve", lambda e: e.tensor_copy(bl[:, 0:4], pb3[:, :, lastcol]), reads=[pbb], writes=[blb])
                        eq, eqb = eqs.next()
                        ek, ekb = eks.next()
                        c.op("act", lambda e: e.activation(out=eq[:], in_=pb[:, :], func=AF.Exp, bias=LNS, scale=1.0),
                             reads=[pbb], writes=[eqb])
                        c.op("act", lambda e: e.activation(out=ek[:], in_=pb[:, :], func=AF.Exp, scale=-1.0),
                             reads=[pbb], writes=[ekb])
                        c.op("act", lambda e: e.activation(out=bl[:, 4:8], in_=bl[:, 0:4], func=AF.Exp),
                             reads=[blb], writes=[blb])
                        qe, qeb = qes.next()
                        ke, keb = kes.next()
                        kdT, kdTb = kdTs.next()
                        c.op("dve", lambda e: e.tensor_tensor(out=qe[:].rearrange("p (h t) -> p h t", h=4), in0=qb_[:, :, tsl],
                                                              in1=eq[:].rearrange("p (h t) -> p h t", h=4), op=ALU.mult),
                             reads=[qbb, eqb], writes=[qeb])
                        c.op("dve", lambda e: e.tensor_tensor(out=ke[:].rearrange("p (h t) -> p h t", h=4), in0=kb_[:, :, tsl],
                                                              in1=ek[:].rearrange("p (h t) -> p h t", h=4), op=ALU.mult),
                             reads=[kbb, ekb], writes=[keb])
                        for h in range(4):
                            c.op("dve", lambda e, h=h: e.tensor_scalar(kdT[:, h * 128:(h + 1) * 128], ke[:, h * 128:(h + 1) * 128],
                                                                       bl[:, 4 + h:5 + h], None, ALU.mult),
                                 reads=[keb, blb], writes=[kdTb])
                        d_.update({"bl": (bl, blb), "qe": (qe, qeb), "ke": (ke, keb), "kdT": (kdT, kdTb)})

                    def g3(idx):
                        d_ = GS.pop(idx)
                        tok = d_["tok"]
                        vt, vtb = vts.next()
                        c.dma("sp", vt[:], S_["vb"][tok:tok + 128, :], writes=[vtb])
                        ext = None
                        if fwd:
                            rbt, rbb = rbs.next()
                            obt, obb_ = obs.next()
                            c.dma("sp", rbt[:], S_["rb"][tok:tok + 128, :], writes=[rbb])
                            c.dma("sp", obt[:], S_["obb"][tok:tok + 128, :], writes=[obb_])
                            ext = (rbt, rbb, obt, obb_)
                        kdT, kdTb = d_["kdT"]
                        pk, pkb = psKD
                        for h in range(4):
                            c.op("pe", lambda e, h=h: e.transpose(pk[:, h * 128:(h + 1) * 128], kdT[:, h * 128:(h + 1) * 128], ident[:]),
                                 reads=[kdTb, identb], writes=[pkb], last=(h == 3))
                        kd, kdb = kds.next()
                        c.op("act", lambda e: e.activation(out=kd[:], in_=pk[:, :], func=AF.Copy), reads=[pkb], writes=[kdb])
                        bl, blb = d_["bl"]
                        qe, qeb = d_["qe"]
                        ke, keb = d_["ke"]
                        G[idx] = (tok, vt, vtb, ext, bl, blb, qe, qeb, ke, keb, kd, kdb)

                    def statepart(idx):
                        if CUT < 5:
                            return
                        tok, vt, vtb, ext, bl, blb, qe, qeb, ke, keb, kd, kdb = G.pop(idx)
                        state, stateb = states[cur[0]]
                        snew, snewb = states[1 - cur[0]]
                        cur[0] = 1 - cur[0]
                        pa, pab = psA
                        for h in range(4):
                            c.op("pe", lambda e, h=h: e.matmul(pa[:, h * 128:(h + 1) * 128], ke[:, h * 128:(h + 1) * 128],
                                                               qe[:, h * 128:(h + 1) * 128], start=True, stop=True),
                                 reads=[keb, qeb], writes=[pab], last=(h == 3))
                        for h in range(4):
                            pst, pstb = psSt[h // 2]
                            osl = slice((h % 2) * 256, (h % 2 + 1) * 256)
                            c.op("pe", lambda e, h=h: e.matmul(pst[:, osl], kd[:, h * 128:(h + 1) * 128], vt[:, h * 256:(h + 1) * 256],
                                                               start=True, stop=True),
                                 reads=[kdb, vtb], writes=[pstb], last=(h % 2 == 1))
                        att, attb = atts.next()
                        c.op("dve", lambda e: e.tensor_tensor(out=att[:].rearrange("p (h t) -> p h t", h=4),
                                                              in0=pa[:, :].rearrange("p (h t) -> p h t", h=4),
                                                              in1=msk[:].unsqueeze(1).to_broadcast([128, 4, 128]), op=ALU.mult),
                             reads=[pab, mskb], writes=[attb])
                        for h in range(4):
                            pst, pstb = psSt[h // 2]
                            osl = slice((h % 2) * 256, (h % 2 + 1) * 256)
                            c.op("dve", lambda e, h=h: e.scalar_tensor_tensor(out=snew[:, h, :], in0=state[:, h, :],
                                                                              scalar=bl[:, 4 + h:5 + h], in1=pst[:, osl],
                                                                              op0=ALU.mult, op1=ALU.add),
                                 reads=[stateb, blb, pstb], writes=[snewb])
                        if pend_e2[0] is not None:
                            pend_e2[0]()
                            pend_e2[0] = None
                        for h in range(4):
                            po, pob = psO[h // 2]
                            osl = slice((h % 2) * 256, (h % 2 + 1) * 256)
                            c.op("pe", lambda e, h=h: e.matmul(po[:, osl], att[:, h * 128:(h + 1) * 128], vt[:, h * 256:(h + 1) * 256],
                                                               start=True, stop=False),
                                 reads=[attb, vtb], writes=[pob], last=False)
                            c.op("pe", lambda e, h=h: e.matmul(po[:, osl], qe[:, h * 128:(h + 1) * 128], state[:, h, :],
                                                               start=False, stop=True),
                                 reads=[qeb, stateb], writes=[pob], last=(h % 2 == 1))
                        if CUT < 8:
                            return
                        ob_, obb2 = osb.next()
                        if not fwd:
                            evac(ob_[:, 0:512], psO[0][0][:, :], reads=[psO[0][1]], writes=[obb2], eng="act")
                            evac(ob_[:, 512:1024], psO[1][0][:, :], reads=[psO[1][1]], writes=[obb2], eng="dve")
                            c.dma("pool", S_["obb"][tok:tok + 128, :], ob_[:], reads=[obb2])
                        else:
                            rbt, rbb, obt, obb_ = ext
                            for k in range(2):
                                c.op("dve", lambda e, k=k: e.tensor_tensor(out=ob_[:, k * 512:(k + 1) * 512], in0=psO[k][0][:, :],
                                                                           in1=obt[:, k * 512:(k + 1) * 512], op=ALU.add),
                                     reads=[psO[k][1], obb_], writes=[obb2])
                            pend_e2[0] = lambda: epi2(tok, ob_, obb2, rbt, rbb)

                    def epi2(tok, ob_, obb2, rbt, rbb):
                        if True:
                            ns, nsb = nst.next()
                            for h in range(4):
                                c.op("act", lambda e, h=h: e.activation(out=junk[:], in_=ob_[:, h * 256:(h + 1) * 256], func=AF.Square,
                                                                        accum_out=ns[:, h:h + 1]),
                                     reads=[obb2], writes=[junkb, nsb])
                            c.op("act", lambda e: e.activation(out=ns[:, 4:8], in_=ns[:, 0:4], func=AF.Ln, scale=1.0 / 256, bias=EPS),
                                 reads=[nsb], writes=[nsb])
                            c.op("act", lambda e: e.activation(out=ns[:, 8:12], in_=ns[:, 4:8], func=AF.Exp, scale=-0.5),
                                 reads=[nsb], writes=[nsb])
                            sl_, slb_ = sil.next()
                            c.op("act", lambda e: e.activation(out=sl_[:], in_=rbt[:], func=AF.Silu), reads=[rbb], writes=[slb_])
                            c.op("pool", lambda e: e.tensor_tensor(out=sl_[:].rearrange("p (h v) -> p h v", h=4),
                                                                   in0=sl_[:].rearrange("p (h v) -> p h v", h=4),
                                                                   in1=gnb[:].unsqueeze(1).to_broadcast([128, 4, 256]), op=ALU.mult),
                                 reads=[slb_, gnbb], writes=[slb_])
                            c.op("dve", lambda e: e.tensor_tensor(out=ob_[:].rearrange("p (h v) -> p h v", h=4),
                                                                  in0=ob_[:].rearrange("p (h v) -> p h v", h=4),
                                                                  in1=ns[:, 8:12].unsqueeze(2).to_broadcast([128, 4, 256]), op=ALU.mult),
                                 reads=[obb2, nsb], writes=[obb2])
                            c.op("dve", lambda e: e.tensor_tensor(out=ob_[:], in0=ob_[:], in1=sl_[:], op=ALU.mult),
                                 reads=[obb2, slb_], writes=[obb2])
                            c.dma("pool", S_["ob"][tok:tok + 128, :], ob_[:], reads=[obb2])

                    NTI = len(tiles)
                    pend_e2[0] = None
                    for it in range(-3, NTI):
                        if 0 <= it + 3 < NTI:
                            g1(it + 3)
                        if 0 <= it + 2 < NTI:
                            g2(it + 2)
                        if 0 <= it + 1 < NTI:
                            g3(it + 1)
                        if 0 <= it < NTI:
                            statepart(it)
                    if pend_e2[0] is not None:
                        pend_e2[0]()
                        pend_e2[0] = None
                c.barrier()
            c.es = ges

        def phase_D(l, xsrc):
            BT = 512
            with ExitStack() as es:
                c.es = es
                wo, wob = c.sb("wo", [128, 8, D])
                c.dma("sp", wo[:], W["wo"][l].rearrange("(kc p) c -> p kc c", p=128), writes=[wob])
                oags = Rot([c.sb("oag", [128, 24, OAW]) for _ in range(2)])
                obts = Rot([c.sb("obt", [128, 1024]) for _ in range(2)])
                oas = Rot([c.sb("oa", [128, 512]) for _ in range(2)])
                tm1 = Rot([c.sb("tm1", [128, 512]) for _ in range(2)])
                tm2 = Rot([c.sb("tm2", [128, 512]) for _ in range(2)])
                cs = Rot([c.sb("cs", [128, 64]) for _ in range(2)])
                oaT, oaTb = c.sb("oaT", [128, 4, BT])
                obT, obTb = c.sb("obT", [128, 8, BT])
                gaT, gaTb = c.sb("gaT", [128, 8, BT])
                gbT, gbTb = c.sb("gbT", [128, 8, BT])
                mT, mTb = c.sb("mT", [128, 8, BT])
                xt, xtb = c.sb("xtD", [128, 4, D])
                wap = Rot([c.sb("wap", [128, 4, 256]) for _ in range(2)])
                wbp = Rot([c.sb("wbp", [128, 8, 256]) for _ in range(2)])
                tmpm = Rot([c.sb("tmpm", [128, 512]) for _ in range(2)])
                psrot = Rot(banks)
                nblk = TOK // BT
                jobs = [(blk, pc) for blk in range(nblk) for pc in range(4)]
                slots = {}

                def loadw(n):
                    blk, pc = jobs[n]
                    a_, ab = wap.next()
                    b_, bb = wbp.next()
                    c.dma("sp", a_[:], W["wa"][l][:, pc * 256:(pc + 1) * 256].rearrange("(kc p) c -> p kc c", p=128), writes=[ab])
                    c.dma("sp", b_[:], W["wb"][l][:, pc * 256:(pc + 1) * 256].rearrange("(kc p) c -> p kc c", p=128), writes=[bb])
                    slots[n] = (a_, ab, b_, bb)

                def gates(blk):
                    t0 = blk * BT
                    c.dma("sp", gaT[:], S_["gaT"][:, t0:t0 + BT].rearrange("(kc p) t -> p kc t", p=128), writes=[gaTb])
                    c.dma("sp", gbT[:], S_["gbT"][:, t0:t0 + BT].rearrange("(kc p) t -> p kc t", p=128), writes=[gbTb])
                    c.op("act", lambda e: e.activation(out=gaT[:], in_=gaT[:], func=AF.Sigmoid), reads=[gaTb], writes=[gaTb])
                    c.op("act", lambda e: e.activation(out=gbT[:], in_=gbT[:], func=AF.Sigmoid), reads=[gbTb], writes=[gbTb])

                def combine(blk, ti):
                    tok = blk * BT + ti * 128
                    og, ogb = oags.next()
                    ob_, obb_ = obts.next()
                    c.dma("sp", og[:], S_["oag"][tok:tok + 128, :, :], writes=[ogb])
                    c.dma("sp", ob_[:], S_["ob"][tok:tok + 128, :], writes=[obb_])
                    cs_, csb = cs.next()
                    lse = og[:, :, 64].rearrange("p (g h) -> p g h", g=3)
                    c.op("dve", lambda e: e.tensor_tensor(out=cs_[:, 0:8], in0=lse[:, 0, :], in1=lse[:, 1, :], op=ALU.max),
                         reads=[ogb], writes=[csb])
                    c.op("dve", lambda e: e.tensor_tensor(out=cs_[:, 0:8], in0=cs_[:, 0:8], in1=lse[:, 2, :], op=ALU.max),
                         reads=[ogb, csb], writes=[csb])
                    e3 = cs_[:, 8:32].rearrange("p (g h) -> p g h", g=3)
                    c.op("dve", lambda e: e.tensor_tensor(out=e3, in0=lse, in1=cs_[:, 0:8].unsqueeze(1).to_broadcast([128, 3, 8]),
                                                          op=ALU.subtract), reads=[ogb, csb], writes=[csb])
                    c.op("act", lambda e: e.activation(out=cs_[:, 8:32], in_=cs_[:, 8:32], func=AF.Exp), reads=[csb], writes=[csb])
                    c.op("dve", lambda e: e.tensor_tensor(out=cs_[:, 32:40], in0=cs_[:, 8:16], in1=cs_[:, 16:24], op=ALU.add),
                         reads=[csb], writes=[csb])
                    c.op("dve", lambda e: e.tensor_tensor(out=cs_[:, 32:40], in0=cs_[:, 32:40], in1=cs_[:, 24:32], op=ALU.add),
                         reads=[csb], writes=[csb])
                    c.op("dve", lambda e: e.reciprocal(cs_[:, 40:48], cs_[:, 32:40]), reads=[csb], writes=[csb])
                    c.op("dve", lambda e: e.tensor_tensor(out=e3, in0=e3, in1=cs_[:, 40:48].unsqueeze(1).to_broadcast([128, 3, 8]),
                                                          op=ALU.mult), reads=[csb], writes=[csb])
                    oa, oab = oas.next()
                    t1, t1b = tm1.next()
                    t2, t2b = tm2.next()

                    def wmul(eng, dst, dstb, g):
                        c.op(eng, lambda e: e.tensor_tensor(out=dst[:].rearrange("p (h x) -> p h x", h=8),
                                                            in0=og[:, g * 8:(g + 1) * 8, 0:64],
                                                            in1=cs_[:, 8 + g * 8:16 + g * 8].unsqueeze(2).to_broadcast([128, 8, 64]),
                                                            op=ALU.mult), reads=[ogb, csb], writes=[dstb])
                    wmul("dve", oa, oab, 0)
                    wmul("pool", t1, t1b, 1)
                    wmul("pool", t2, t2b, 2)
                    c.op("dve", lambda e: e.tensor_tensor(out=oa[:], in0=oa[:], in1=t1[:], op=ALU.add), reads=[oab, t1b], writes=[oab])
                    c.op("dve", lambda e: e.tensor_tensor(out=oa[:], in0=oa[:], in1=t2[:], op=ALU.add), reads=[oab, t2b], writes=[oab])
                    return (oa, oab, ob_, obb_)

                def trans(ti, bufs):
                    oa, oab, ob_, obb_ = bufs
                    tsl = slice(ti * 128, (ti + 1) * 128)
                    transpose_to(oa, oab, 4, oaT, oaTb, tsl, psrot)
                    transpose_to(ob_, obb_, 8, obT, obTb, tsl, psrot)

                def branch(blk):
                    for pc in range(4):
                        n = blk * 4 + pc
                        if n + 1 < len(jobs):
                            loadw(n + 1)
                        a_, ab, b_, bb = slots.pop(n)
                        for c2 in range(2):
                            cc = pc * 2 + c2
                            pA, pAb = psrot.next()
                            for kc in range(4):
                                c.op("pe", lambda e, kc=kc: e.matmul(pA[:, :], a_[:, kc, c2 * 128:(c2 + 1) * 128], oaT[:, kc, :],
                                                                     start=(kc == 0), stop=(kc == 3)),
                                     reads=[ab, oaTb], writes=[pAb], last=(kc == 3))
                            pB, pBb = psrot.next()
                            for kc in range(8):
                                c.op("pe", lambda e, kc=kc: e.matmul(pB[:, :], b_[:, kc, c2 * 128:(c2 + 1) * 128], obT[:, kc, :],
                                                                     start=(kc == 0), stop=(kc == 7)),
                                     reads=[bb, obTb], writes=[pBb], last=(kc == 7))
                            tm, tmb = tmpm.next()
                            c.op("dve", lambda e: e.tensor_tensor(out=tm[:], in0=pA[:, :], in1=gaT[:, cc, :], op=ALU.mult),
                                 reads=[pAb, gaTb], writes=[tmb])
                            c.op("dve", lambda e: e.tensor_tensor(out=mT[:, cc, :], in0=pB[:, :], in1=gbT[:, cc, :], op=ALU.mult),
                                 reads=[pBb, gbTb], writes=[mTb])
                            c.op("pool", lambda e: e.tensor_tensor(out=mT[:, cc, :], in0=mT[:, cc, :], in1=tm[:], op=ALU.add),
                                 reads=[mTb, tmb], writes=[mTb])

                def outproj(ti):
                    for half in range(2):
                        pX, pXb = psrot.next()
                        for cc in range(8):
                            c.op("pe", lambda e, cc=cc: e.matmul(pX[:, :], mT[:, cc, ti * 128:(ti + 1) * 128],
                                                                 wo[:, cc, half * 512:(half + 1) * 512],
                                                                 start=(cc == 0), stop=(cc == 7)),
                                 reads=[mTb, wob], writes=[pXb], last=(cc == 7))
                        c.op("dve", lambda e: e.tensor_tensor(out=xt[:, ti, half * 512:(half + 1) * 512], in0=pX[:, :],
                                                              in1=xt[:, ti, half * 512:(half + 1) * 512], op=ALU.add),
                             reads=[pXb, xtb], writes=[xtb])

                loadw(0)
                gates(0)
                for ti in range(BT // 128):
                    trans(ti, combine(0, ti))
                for blk in range(nblk):
                    t0 = blk * BT
                    c.dma("sp", xt[:], xsrc[t0:t0 + BT, :].rearrange("(a p) c -> p a c", p=128), writes=[xtb])
                    branch(blk)
                    nxt = blk + 1 < nblk
                    if nxt:
                        gates(blk + 1)
                    for ti in range(BT // 128):
                        bufs = combine(blk + 1, ti) if nxt else None
                        outproj(ti)
                        if nxt:
                            trans(ti, bufs)
                    c.dma("pool", S_["xb"][t0:t0 + BT, :].rearrange("(a p) c -> p a c", p=128), xt[:], reads=[xtb])
                c.barrier()
            c.es = ges

        def phase_E(l, xdst, final):
            BT = 384
            blocks = []
            t = 0
            while t < TOK:
                n_ = min(BT, TOK - t)
                blocks.append((t, n_ // 128))
                t += n_
            with ExitStack() as es:
                c.es = es
                gbc, gbcb = c.sb("gbcE", [128, D])
                c.dma("sp", gbc[:], W["nfg"][l].partition_broadcast(128), writes=[gbcb])
                gfn, gfnb = c.sb("gfn", [128, D])
                if final:
                    c.dma("sp", gfn[:], W["fng"].partition_broadcast(128), writes=[gfnb])
                xts = Rot([c.sb("xtE", [128, 3, D]) for _ in range(2)])
                h2h = Rot([c.sb("h2h", [128, 8, BT], F32R) for _ in range(2)])
                h2l = Rot([c.sb("h2l", [128, 8, BT], F32R) for _ in range(2)])
                hts = Rot([c.sb("htE", [128, D]) for _ in range(2)])
                sts = Rot([c.sb("stE", [128, 3]) for _ in range(2)])
                w1r = Rot([c.sb("w1r", [128, 8, 256]) for _ in range(2)])
                w1h = Rot([c.sb("w1h", [128, 8, 256], F32R) for _ in range(2)])
                w1l = Rot([c.sb("w1l", [128, 8, 256], F32R) for _ in range(2)])
                w2r = Rot([c.sb("w2r", [128, 2, D]) for _ in range(2)])
                w2h = Rot([c.sb("w2h", [128, 2, D], F32R) for _ in range(2)])
                w2l = Rot([c.sb("w2l", [128, 2, D], F32R) for _ in range(2)])
                rls = Rot([c.sb("rl", [128, BT]) for _ in range(2)])
                sqs = Rot([c.sb("sq", [128, BT]) for _ in range(2)])
                hhs = Rot([c.sb("hh", [128, BT], F32R) for _ in range(3)])
                hls = Rot([c.sb("hl", [128, BT], F32R) for _ in range(3)])
                acc = banks[0:6]
                ps67 = Rot(banks[6:8])

                def mm3(ps_ap, psb, A, B, first, lastk, inc):
                    (ah, ahb), (al, alb) = A
                    (bh, bhb), (bl_, blb_) = B
                    c.op("pe", lambda e: e.matmul(ps_ap, ah, bh, start=first, stop=False), reads=[ahb, bhb], writes=[psb], last=False)
                    c.op("pe", lambda e: e.matmul(ps_ap, ah, bl_, start=False, stop=False), reads=[ahb, blb_], writes=[psb], last=False)
                    c.op("pe", lambda e: e.matmul(ps_ap, al, bh, start=False, stop=lastk), reads=[alb, bhb], writes=[psb], last=inc)

                jobs = [(bi_, pj) for bi_ in range(len(blocks)) for pj in range(16)]
                wsl = {}

                raws = {}

                def dmaw(n):
                    if n >= len(jobs):
                        return
                    bi_, pj = jobs[n]
                    r1, r1b = w1r.next()
                    r2, r2b = w2r.next()
                    c.dma("sp", r1[:], W["w1"][l][:, pj * 256:(pj + 1) * 256].rearrange("(kc p) c -> p kc c", p=128), writes=[r1b])
                    c.dma("sp", r2[:], W["w2"][l][pj * 256:(pj + 1) * 256, :].rearrange("(fc p) c -> p fc c", p=128), writes=[r2b])
                    raws[n] = (r1, r1b, r2, r2b)

                def loadw(n):
                    r1, r1b, r2, r2b = raws.pop(n)
                    a1, a1b = w1h.next()
                    l1, l1b = w1l.next()
                    a2, a2b = w2h.next()
                    l2, l2b = w2l.next()
                    c.op("act", lambda e: e.activation(out=a1[:], in_=r1[:], func=AF.Copy), reads=[r1b], writes=[a1b])
                    c.op("dve", lambda e: e.tensor_tensor(out=l1[:], in0=r1[:], in1=a1[:].bitcast(F32), op=ALU.subtract),
                         reads=[r1b, a1b], writes=[l1b])
                    c.op("act", lambda e: e.activation(out=a2[:], in_=r2[:], func=AF.Copy), reads=[r2b], writes=[a2b])
                    c.op("dve", lambda e: e.tensor_tensor(out=l2[:], in0=r2[:], in1=a2[:].bitcast(F32), op=ALU.subtract),
                         reads=[r2b, a2b], writes=[l2b])
                    wsl[n] = (a1, a1b, l1, l1b, a2, a2b, l2, l2b)

                BS = {}

                def pro_load(bi_):
                    t0, nt = blocks[bi_]
                    xt, xtb = xts.next()
                    hh_, hhb_ = h2h.next()
                    hl_, hlb_ = h2l.next()
                    c.dma("sp", xt[:, 0:nt, :], S_["xb"][t0:t0 + nt * 128, :].rearrange("(a p) c -> p a c", p=128), writes=[xtb])
                    BS[bi_] = (xt, xtb, hh_, hhb_, hl_, hlb_)

                def pro_tile(bi_, ti):
                    xt, xtb, hh_, hhb_, hl_, hlb_ = BS[bi_]
                    ht, hb = hts.next()
                    st, stb = sts.next()
                    rms_stats(xt[:, ti, :], xtb, ht[:], hb, st, stb, D)
                    c.op("dve", lambda e: e.scalar_tensor_tensor(out=ht[:], in0=xt[:, ti, :], scalar=st[:, 2:3], in1=gbc[:],
                                                                 op0=ALU.mult, op1=ALU.mult),
                         reads=[xtb, stb, gbcb], writes=[hb])
                    tsl = slice(ti * 128, (ti + 1) * 128)
                    for c0_ in range(0, 8, 4):
                        ps, psb = ps67.next()
                        for j in range(4):
                            c.op("pe", lambda e, j=j: e.transpose(ps[:, j * 128:(j + 1) * 128],
                                                                  ht[:, (c0_ + j) * 128:(c0_ + j + 1) * 128], ident[:]),
                                 reads=[hb, identb], writes=[psb], last=(j == 3))
                        ps3 = ps[:, :].rearrange("p (a b) -> p a b", a=4)
                        c.op("act", lambda e: e.activation(out=hh_[:, c0_:c0_ + 4, tsl], in_=ps3, func=AF.Copy),
                             reads=[psb], writes=[hhb_])
                        c.op("dve", lambda e: e.tensor_tensor(out=hl_[:, c0_:c0_ + 4, tsl], in0=ps3,
                                                              in1=hh_[:, c0_:c0_ + 4, tsl].bitcast(F32), op=ALU.subtract),
                             reads=[psb, hhb_], writes=[hlb_])

                dmaw(0)
                loadw(0)
                dmaw(1)
                pro_load(0)
                for ti in range(blocks[0][1]):
                    pro_tile(0, ti)
                n = 0
                for bi_, (t0, nt) in enumerate(blocks):
                    ntok = nt * 128
                    xt, xtb, hh_, hhb_, hl_, hlb_ = BS[bi_]
                    nxt = bi_ + 1 < len(blocks)
                    pend_ff2 = None

                    def ff2(fc, c2, hbuf, wts):
                        a1, a1b, l1, l1b, a2, a2b, l2, l2b = wts
                        hh, hhb, hl, hlb = hbuf
                        for ti in range(nt):
                            for half in range(2):
                                pa_, pab_ = acc[ti * 2 + half]
                                lastone = (ti == nt - 1 and half == 1)
                                mm3(pa_[:, :], pab_,
                                    ((hh[:, ti * 128:(ti + 1) * 128], hhb), (hl[:, ti * 128:(ti + 1) * 128], hlb)),
                                    ((a2[:, c2, half * 512:(half + 1) * 512], a2b), (l2[:, c2, half * 512:(half + 1) * 512], l2b)),
                                    fc == 0, fc == 31, lastone)

                    for pj in range(16):
                        wts = wsl.pop(n)
                        n += 1
                        a1, a1b, l1, l1b = wts[0:4]
                        if nxt and pj == 0:
                            pro_load(bi_ + 1)
                        for c2 in range(2):
                            fc = pj * 2 + c2
                            ps, psb = ps67.next()
                            for kc in range(8):
                                mm3(ps[:, 0:ntok], psb,
                                    ((a1[:, kc, c2 * 128:(c2 + 1) * 128], a1b), (l1[:, kc, c2 * 128:(c2 + 1) * 128], l1b)),
                                    ((hh_[:, kc, 0:ntok], hhb_), (hl_[:, kc, 0:ntok], hlb_)), kc == 0, kc == 7, kc == 7)
                            if pend_ff2 is not None:
                                ff2(*pend_ff2)
                                pend_ff2 = None
                            if c2 == 0 and n < len(jobs):
                                loadw(n)
                                dmaw(n + 1)
                            r_, rb_ = rls.next()
                            q_, qb_ = sqs.next()
                            hh, hhb = hhs.next()
                            hl, hlb = hls.next()
                            c.op("act", lambda e: e.activation(out=r_[:, 0:ntok], in_=ps[:, 0:ntok], func=AF.Relu), reads=[psb], writes=[rb_])
                            c.op("dve", lambda e: e.tensor_tensor(out=q_[:, 0:ntok], in0=r_[:, 0:ntok], in1=r_[:, 0:ntok], op=ALU.mult),
                                 reads=[rb_], writes=[qb_])
                            c.op("act", lambda e: e.activation(out=hh[:, 0:ntok], in_=q_[:, 0:ntok], func=AF.Copy), reads=[qb_], writes=[hhb])
                            c.op("dve", lambda e: e.tensor_tensor(out=hl[:, 0:ntok], in0=q_[:, 0:ntok], in1=hh[:, 0:ntok].bitcast(F32),
                                                                   op=ALU.subtract), reads=[qb_, hhb], writes=[hlb])
                            pend_ff2 = (fc, c2, (hh, hhb, hl, hlb), wts)
                        if nxt and 9 <= pj < 9 + blocks[bi_ + 1][1]:
                            pro_tile(bi_ + 1, pj - 9)
                    ff2(*pend_ff2)
                    pend_ff2 = None
                    for ti in range(nt):
                        for half in range(2):
                            pa_, pab_ = acc[ti * 2 + half]
                            c.op("dve", lambda e: e.tensor_tensor(out=xt[:, ti, half * 512:(half + 1) * 512], in0=pa_[:, :],
                                                                  in1=xt[:, ti, half * 512:(half + 1) * 512], op=ALU.add),
                                 reads=[pab_, xtb], writes=[xtb])
                    if final:
                        for ti in range(nt):
                            ht, hb = hts.next()
                            st, stb = sts.next()
                            rms_stats(xt[:, ti, :], xtb, ht[:], hb, st, stb, D)
                            c.op("dve", lambda e: e.scalar_tensor_tensor(out=ht[:], in0=xt[:, ti, :], scalar=st[:, 2:3], in1=gfn[:],
                                                                         op0=ALU.mult, op1=ALU.mult),
                                 reads=[xtb, stb, gfnb], writes=[hb])
                            c.dma("pool", xdst[t0 + ti * 128:t0 + (ti + 1) * 128, :], ht[:], reads=[hb])
                    else:
                        c.dma("pool", xdst[t0:t0 + ntok, :].rearrange("(a p) c -> p a c", p=128), xt[:, 0:nt, :], reads=[xtb])
                    del BS[bi_]
                c.barrier()
            c.es = ges

        for l in range(depth):
            xsrc = x_in if l == 0 else S_["xa"]
            final = (l == depth - 1)
            if "A" in phases:
                phase_A(l, xsrc)
            if "B" in phases:
                phase_B(l)
            if "C" in phases:
                phase_C(l, 1)
            if "c" in phases:
                phase_C(l, 0)
            if "D" in phases:
                phase_D(l, xsrc)
            if "E" in phases:
                phase_E(l, y_out if final else S_["xa"], final)
        c.barrier(final=True)
    return nc


_PROG_CACHE = {}


def kernel(x_prompt, x_sample, norm_mix_g, w_in, gla_w_gate_fwd, gla_b_gate_fwd, gla_w_gate_bwd, gla_b_gate_bwd,
           gla_norm_g, w_branch_a, w_branch_b, w_out, norm_ffn_g, w_ff1, w_ff2, final_norm_g):
    f32 = np.float32
    xp = np.asarray(x_prompt, f32)
    xs = np.asarray(x_sample, f32)
    nb_p, sp = xp.shape[0], xp.shape[1]
    nb_s, ss = xs.shape[0], xs.shape[1]
    pp, ps_ = nb_p // N_CORES, nb_s // N_CORES
    seqs = [sp] * pp + [ss] * ps_
    key = tuple(seqs)
    if key not in _PROG_CACHE:
        _PROG_CACHE[key] = build_program(seqs)
    nc = _PROG_CACHE[key]
    shared = {
        "norm_mix_g": norm_mix_g, "w_in": w_in, "gla_w_gate_fwd": gla_w_gate_fwd, "gla_b_gate_fwd": gla_b_gate_fwd,
        "gla_w_gate_bwd": gla_w_gate_bwd, "gla_b_gate_bwd": gla_b_gate_bwd, "gla_norm_g": gla_norm_g,
        "w_branch_a": w_branch_a, "w_branch_b": w_branch_b, "w_out": w_out, "norm_ffn_g": norm_ffn_g,
        "w_ff1": w_ff1, "w_ff2": w_ff2, "final_norm_g": final_norm_g,
    }
    shared = {k: np.ascontiguousarray(np.asarray(v, f32)) for k, v in shared.items()}
    shared.update(host_consts())
    in_maps = []
    for i in range(N_CORES):
        xcore = np.concatenate([xp[i * pp:(i + 1) * pp].reshape(pp * sp, D), xs[i * ps_:(i + 1) * ps_].reshape(ps_ * ss, D)], axis=0)
        m = dict(shared)
        m["x"] = np.ascontiguousarray(xcore)
        in_maps.append(m)
    res = run_bass_kernel_spmd(nc, in_maps, core_ids=list(range(N_CORES)))
    yp = np.empty((nb_p, sp, D), f32)
    ys = np.empty((nb_s, ss, D), f32)
    for i in range(N_CORES):
        y = np.asarray(res.results[i]["y"])
        yp[i * pp:(i + 1) * pp] = y[:pp * sp].reshape(pp, sp, D)
        ys[i * ps_:(i + 1) * ps_] = y[pp * sp:].reshape(ps_, ss, D)
    return (yp, ys)
```

```python
import numpy as np
from contextlib import ExitStack
import concourse.bass as bass
import concourse.mybir as mybir
from concourse.bass_utils import run_bass_kernel_spmd

F32 = mybir.dt.float32
F32R = mybir.dt.float32r
AF = mybir.ActivationFunctionType
ALU = mybir.AluOpType
AX = mybir.AxisListType

D = 1024
DEPTH = 2
A_GROUPS = ((128, 1), (512, 4), (2048, 16))
IN_COLS = 9760
C_QA, C_KA, C_VA, C_QB, C_KB, C_VB, C_RB, C_GL, C_GA, C_GB = 0, 1536, 3072, 4608, 5120, 5632, 6656, 7680, 7712, 8736
D_FF = 4096
EPS = 1e-6
NEG = -1e30
PAD = 1024
OAW = 66
N_CORES = 8
CUT = 99


class Buf:
    __slots__ = ("name", "w", "r", "excl")

    def __init__(self, name, excl=False):
        self.name = name
        self.w = None
        self.r = {}
        self.excl = excl


class Ctx:
    WINDOW = 4
    NRING = 12

    def __init__(self, nc, es):
        self.nc = nc
        self.es = es
        self.eng = {"pe": nc.tensor, "dve": nc.vector, "act": nc.scalar,
                    "pool": nc.gpsimd, "sp": nc.sync}
        self.sem = {k: es.enter_context(nc.semaphore("s_" + k)) for k in self.eng}
        self.cnt = {k: 0 for k in self.eng}
        self.seen = {k: {} for k in self.eng}
        self.pend = {k: ([], []) for k in self.eng}
        self.ring = {q: [es.enter_context(nc.semaphore("r_%s%d" % (q, i))) for i in range(self.NRING)]
                     for q in ("sp", "pool")}
        self.ringcnt = {"sp": 0, "pool": 0}
        self.uid = 0

    def sb(self, name, shape, dtype=F32):
        self.uid += 1
        t = self.es.enter_context(self.nc.sbuf_tensor("%s_%d" % (name, self.uid), list(shape), dtype))
        return t, Buf(name)

    def _semof(self, key):
        if isinstance(key, str):
            return self.sem[key]
        return self.ring[key[1]][key[2]]

    def _wait(self, e, key, val):
        if self.seen[e].get(key, 0) >= val:
            return
        if key == e:
            if e == "pe":
                return
            if val <= self.cnt[e] - self.WINDOW:
                return
        self.eng[e].wait_ge(self._semof(key), val)
        self.seen[e][key] = val

    def _deps(self, e, reads, writes):
        for b in reads:
            if b.w is not None:
                self._wait(e, b.w[0], b.w[1])
            if b.excl:
                for k, v in b.r.items():
                    if k != e:
                        self._wait(e, k, v)
        for b in writes:
            if b.w is not None:
                self._wait(e, b.w[0], b.w[1])
            for k, v in b.r.items():
                self._wait(e, k, v)

    def _mark(self, ev, reads, writes):
        for b in reads:
            if b.r.get(ev[0], 0) < ev[1]:
                b.r[ev[0]] = ev[1]
        for b in writes:
            b.w = ev
            b.r = {}

    def op(self, e, fn, reads=(), writes=(), last=True):
        self._deps(e, reads, writes)
        ins = fn(self.eng[e])
        pr, pw = self.pend[e]
        pr.extend(reads)
        pw.extend(writes)
        if last:
            self.cnt[e] += 1
            ins.then_inc(self.sem[e], 1)
            self._mark((e, self.cnt[e]), pr, pw)
            self.pend[e] = ([], [])
        return ins

    def dma(self, q, out, in_, reads=(), writes=()):
        self._deps(q, reads, writes)
        i = self.ringcnt[q]
        slot, rnd = i % self.NRING, i // self.NRING
        key = ("q", q, slot)
        if rnd > 0:
            self._wait(q, key, 16 * rnd)
        self.eng[q].dma_start(out=out, in_=in_).then_inc(self.ring[q][slot], 16)
        self.ringcnt[q] += 1
        self._mark((key, 16 * (rnd + 1)), reads, writes)

    def barrier(self, final=False):
        for e in self.eng:
            assert not self.pend[e][0] and not self.pend[e][1]
        evs = [(k, self.cnt[k]) for k in self.eng if self.cnt[k] > 0]
        for q in ("sp", "pool"):
            n = self.ringcnt[q]
            for slot in range(min(n, self.NRING)):
                rounds = (n - slot + self.NRING - 1) // self.NRING
                evs.append((("q", q, slot), 16 * rounds))
        engs = ["sp"] if final else list(self.eng)
        for e in engs:
            for k, v in evs:
                if k == e or self.seen[e].get(k, 0) >= v:
                    continue
                self.eng[e].wait_ge(self._semof(k), v)
                self.seen[e][k] = v


def ssl(start, count, step):
    return slice(start, start + (count - 1) * step + 1, step)


class Rot:
    def __init__(self, items):
        self.items = items
        self.i = 0

    def next(self):
        it = self.items[self.i % len(self.items)]
        self.i += 1
        return it


def host_consts():
    q = np.arange(128)[:, None]
    kk = np.arange(256)[None, :]
    delta = kk - 64 - q
    ad = np.abs(delta).astype(np.float64)
    valid = ad <= 64
    nh = 24
    slopes = np.exp2(-8.0 * np.arange(1, nh + 1, dtype=np.float64) / nh)
    abias = np.zeros((nh, 128, 256), np.float32)
    for g, (win, dil) in enumerate(A_GROUPS):
        for h in range(8):
            b = -slopes[g * 8 + h] * ad * dil
            abias[g * 8 + h] = np.where(valid, b, NEG).astype(np.float32)
    edge = np.zeros((2, 128, 256), np.float32)
    edge[0][:, :64] = NEG
    edge[1][:, 192:] = NEG
    s = np.arange(128)[:, None]
    t = np.arange(128)[None, :]
    tri = np.zeros((2, 128, 128), np.float32)
    tri[0] = np.where(s <= t, -1.0 / 16.0, 0.0)
    tri[1] = np.where(s >= t, -1.0 / 16.0, 0.0)
    msk = np.zeros((2, 128, 128), np.float32)
    msk[0] = (s <= t)
    msk[1] = (s > t)
    return {"c_ident": np.eye(128, dtype=np.float32), "c_abias": abias, "c_edge": edge,
            "c_tri": tri, "c_msk": msk, "c_ones": np.ones((1, 512), np.float32)}


def build_program(seqs, depth=DEPTH, dbg=False, phases="ABCcDE"):
    nc = bass.Bass("TRN2", target_bir_lowering=False)
    TOK = sum(seqs)
    offs = [sum(seqs[:i]) for i in range(len(seqs))]
    SMAX = max(seqs)

    def din(name, shape):
        return nc.dram_tensor(name, list(shape), F32, kind="ExternalInput").ap()

    def dscr(name, shape):
        return nc.dram_tensor(name, list(shape), F32, kind=("ExternalOutput" if dbg else "Internal")).ap()

    x_in = din("x", [TOK, D])
    W = {
        "norm_mix_g": din("norm_mix_g", [depth, D]),
        "w_in": din("w_in", [depth, D, IN_COLS]),
        "gwf": din("gla_w_gate_fwd", [depth, 16, 512]),
        "gbf": din("gla_b_gate_fwd", [depth, 512]),
        "gwb": din("gla_w_gate_bwd", [depth, 16, 512]),
        "gbb": din("gla_b_gate_bwd", [depth, 512]),
        "gng": din("gla_norm_g", [depth, 256]),
        "wa": din("w_branch_a", [depth, 512, D]),
        "wb": din("w_branch_b", [depth, D, D]),
        "wo": din("w_out", [depth, D, D]),
        "nfg": din("norm_ffn_g", [depth, D]),
        "w1": din("w_ff1", [depth, D, D_FF]),
        "w2": din("w_ff2", [depth, D_FF, D]),
        "fng": din("final_norm_g", [D]),
    }
    c_ident = din("c_ident", [128, 128])
    c_abias = din("c_abias", [24, 128, 256])
    c_edge = din("c_edge", [2, 128, 256])
    c_tri = din("c_tri", [2, 128, 128])
    c_msk = din("c_msk", [2, 128, 128])
    c_ones = din("c_ones", [1, 512])
    y_out = nc.dram_tensor("y", [TOK, D], F32, kind="ExternalOutput").ap()

    S_ = {
        "qaT": dscr("s_qaT", [1536, TOK]), "kaT": dscr("s_kaT", [1536, TOK]), "va": dscr("s_va", [TOK, 1536]),
        "qbT": dscr("s_qbT", [512, TOK]), "kbT": dscr("s_kbT", [512, TOK]), "vb": dscr("s_vb", [TOK, 1024]),
        "rb": dscr("s_rb", [TOK, 1024]), "lrT": dscr("s_lrT", [32, TOK]),
        "gaT": dscr("s_gaT", [1024, TOK]), "gbT": dscr("s_gbT", [1024, TOK]),
        "oag": dscr("s_oag", [TOK, 24, OAW]), "obb": dscr("s_obb", [TOK, 1024]), "ob": dscr("s_ob", [TOK, 1024]),
        "xa": dscr("s_xa", [TOK, D]), "xb": dscr("s_xb", [TOK, D]),
    }

    with ExitStack() as ges:
        c = Ctx(nc, ges)
        ident, identb = c.sb("ident", [128, 128])
        c.dma("sp", ident[:], c_ident, writes=[identb])
        banks = []
        for i in range(8):
            t = ges.enter_context(nc.psum_tensor("psb%d" % i, [128, 512], F32))
            banks.append((t, Buf("psb%d" % i, excl=True)))
        evac_rr = [0]

        def evac(out, in_, reads, writes, eng=None):
            if eng is None:
                eng = ("act", "dve")[evac_rr[0] % 2]
                evac_rr[0] += 1
            if eng == "act":
                c.op("act", lambda e: e.activation(out=out, in_=in_, func=AF.Copy), reads=reads, writes=writes)
            else:
                c.op("dve", lambda e: e.tensor_copy(out, in_), reads=reads, writes=writes)

        def rms_stats(xt_ap, xb, junk, jb, st, stb, n):
            c.op("act", lambda e: e.activation(out=junk, in_=xt_ap, func=AF.Square, accum_out=st[:, 0:1]),
                 reads=[xb], writes=[jb, stb])
            c.op("act", lambda e: e.activation(out=st[:, 1:2], in_=st[:, 0:1], func=AF.Ln, scale=1.0 / n, bias=EPS),
                 reads=[stb], writes=[stb])
            c.op("act", lambda e: e.activation(out=st[:, 2:3], in_=st[:, 1:2], func=AF.Exp, scale=-0.5),
                 reads=[stb], writes=[stb])

        def transpose_to(src, srcb, nchunks, dstT, dstb, tsl, psrot):
            for c0 in range(0, nchunks, 4):
                nn = min(4, nchunks - c0)
                ps, psb = psrot.next()
                for j in range(nn):
                    c.op("pe", lambda e, j=j: e.transpose(ps[:, j * 128:(j + 1) * 128],
                                                          src[:, (c0 + j) * 128:(c0 + j + 1) * 128], ident[:]),
                         reads=[srcb, identb], writes=[psb], last=(j == nn - 1))
                evac(dstT[:, c0:c0 + nn, tsl], ps[:, 0:nn * 128].rearrange("p (a b) -> p a b", a=nn),
                     reads=[psb], writes=[dstb])

        def phase_A(l, xsrc):
            TA = 1024 if TOK % 1024 == 0 else 512
            panels = []
            for (c0, n, form, dst) in ((C_QA, 1536, "F", "qaT"), (C_KA, 1536, "F", "kaT"), (C_VA, 1536, "T", "va"),
                                       (C_QB, 512, "F", "qbT"), (C_KB, 512, "F", "kbT"), (C_VB, 1024, "T", "vb"),
                                       (C_RB, 1024, "T", "rb"), (C_GL, 32, "F", "lrT"), (C_GA, 1024, "F", "gaT"),
                                       (C_GB, 1024, "F", "gbT")):
                for p0 in range(0, n, 512):
                    panels.append((c0 + p0, min(512, n - p0), form, dst, p0))
            with ExitStack() as es:
                c.es = es
                gbc, gbcb = c.sb("gbc", [128, D])
                c.dma("sp", gbc[:], W["norm_mix_g"][l].partition_broadcast(128), writes=[gbcb])
                hTh, hThb = c.sb("hTh", [128, 8, TA], F32R)
                hTl, hTlb = c.sb("hTl", [128, 8, TA], F32R)
                xts = Rot([c.sb("xt", [128, D]) for _ in range(2)])
                hts = Rot([c.sb("ht", [128, D]) for _ in range(2)])
                sts = Rot([c.sb("st", [128, 3]) for _ in range(2)])
                junk, junkb = c.sb("junk", [128, D])
                wp = Rot([c.sb("wpan", [128, 8, 512]) for _ in range(2)])
                wph = Rot([c.sb("wph", [128, 8, 512], F32R) for _ in range(2)])
                wpl = Rot([c.sb("wpl", [128, 8, 512], F32R) for _ in range(2)])
                stg = Rot([c.sb("stg", [128, 512]) for _ in range(4)])
                psrot = Rot(banks)
                jobs = [(blk, p) for blk in range(TOK // TA) for p in panels]
                slots = {}

                rawsA = {}

                def dmaA(n):
                    if n >= len(jobs):
                        return
                    blk, (c0, ncols, form, dst, p0) = jobs[n]
                    sl, slb = wp.next()
                    c.dma("sp", sl[:, :, 0:ncols], W["w_in"][l][:, c0:c0 + ncols].rearrange("(kc p) c -> p kc c", p=128),
                          writes=[slb])
                    rawsA[n] = (sl, slb)

                def load(n):
                    blk, (c0, ncols, form, dst, p0) = jobs[n]
                    sl, slb = rawsA.pop(n)
                    wh, whb = wph.next()
                    wl, wlb = wpl.next()
                    c.op("act", lambda e: e.activation(out=wh[:, :, 0:ncols], in_=sl[:, :, 0:ncols], func=AF.Copy),
                         reads=[slb], writes=[whb])
                    c.op("dve", lambda e: e.tensor_tensor(out=wl[:, :, 0:ncols], in0=sl[:, :, 0:ncols],
                                                           in1=wh[:, :, 0:ncols].bitcast(F32), op=ALU.subtract),
                         reads=[slb, whb], writes=[wlb])
                    slots[n] = (wh, whb, wl, wlb)

                def mm3(ps_ap, psb, A, B, first, lastk):
                    (ah, ahb), (al, alb) = A
                    (bh, bhb), (bl_, blb_) = B
                    c.op("pe", lambda e: e.matmul(ps_ap, ah, bh, start=first, stop=False), reads=[ahb, bhb], writes=[psb], last=False)
                    c.op("pe", lambda e: e.matmul(ps_ap, ah, bl_, start=False, stop=False), reads=[ahb, blb_], writes=[psb], last=False)
                    c.op("pe", lambda e: e.matmul(ps_ap, al, bh, start=False, stop=lastk), reads=[alb, bhb], writes=[psb], last=lastk)

                dmaA(0)
                load(0)
                dmaA(1)
                for n, (blk, (c0, ncols, form, dst, p0)) in enumerate(jobs):
                    t0 = blk * TA
                    if n % len(panels) == 0:
                        for ti in range(TA // 128):
                            xt, xb = xts.next()
                            ht, hb = hts.next()
                            st, stb = sts.next()
                            c.dma("sp", xt[:], xsrc[t0 + ti * 128:t0 + (ti + 1) * 128, :], writes=[xb])
                            rms_stats(xt[:], xb, junk[:], junkb, st, stb, D)
                            c.op("dve", lambda e: e.scalar_tensor_tensor(out=ht[:], in0=xt[:], scalar=st[:, 2:3], in1=gbc[:],
                                                                         op0=ALU.mult, op1=ALU.mult),
                                 reads=[xb, stb, gbcb], writes=[hb])
                            tsl = slice(ti * 128, (ti + 1) * 128)
                            for c0_ in range(0, 8, 4):
                                ps, psb = psrot.next()
                                for j in range(4):
                                    c.op("pe", lambda e, j=j: e.transpose(ps[:, j * 128:(j + 1) * 128],
                                                                          ht[:, (c0_ + j) * 128:(c0_ + j + 1) * 128], ident[:]),
                                         reads=[hb, identb], writes=[psb], last=(j == 3))
                                ps3 = ps[:, :].rearrange("p (a b) -> p a b", a=4)
                                c.op("act", lambda e: e.activation(out=hTh[:, c0_:c0_ + 4, tsl], in_=ps3, func=AF.Copy),
                                     reads=[psb], writes=[hThb])
                                c.op("dve", lambda e: e.tensor_tensor(out=hTl[:, c0_:c0_ + 4, tsl], in0=ps3,
                                                                      in1=hTh[:, c0_:c0_ + 4, tsl].bitcast(F32), op=ALU.subtract),
                                     reads=[psb, hThb], writes=[hTlb])
                    if n + 1 < len(jobs):
                        load(n + 1)
                        dmaA(n + 2)
                    wh, whb, wl, wlb = slots.pop(n)
                    if form == "F":
                        for cc in range((ncols + 127) // 128):
                            m = min(128, ncols - cc * 128)
                            csl = slice(cc * 128, cc * 128 + m)
                            for tb in range(TA // 512):
                                ps, psb = psrot.next()
                                tsl = slice(tb * 512, (tb + 1) * 512)
                                for kc in range(8):
                                    mm3(ps[0:m, :], psb, ((wh[:, kc, csl], whb), (wl[:, kc, csl], wlb)),
                                        ((hTh[:, kc, tsl], hThb), (hTl[:, kc, tsl], hTlb)), kc == 0, kc == 7)
                                sg, sgb = stg.next()
                                evac(sg[0:m, :], ps[0:m, :], reads=[psb], writes=[sgb])
                                c.dma("pool", S_[dst][p0 + cc * 128:p0 + cc * 128 + m, t0 + tb * 512:t0 + (tb + 1) * 512],
                                      sg[0:m, :], reads=[sgb])
                    else:
                        for ti in range(TA // 128):
                            ps, psb = psrot.next()
                            tsl = slice(ti * 128, (ti + 1) * 128)
                            for kc in range(8):
                                mm3(ps[:, 0:ncols], psb, ((hTh[:, kc, tsl], hThb), (hTl[:, kc, tsl], hTlb)),
                                    ((wh[:, kc, 0:ncols], whb), (wl[:, kc, 0:ncols], wlb)), kc == 0, kc == 7)
                            sg, sgb = stg.next()
                            evac(sg[:, 0:ncols], ps[:, 0:ncols], reads=[psb], writes=[sgb])
                            c.dma("pool", S_[dst][t0 + ti * 128:t0 + (ti + 1) * 128, p0:p0 + ncols], sg[:, 0:ncols],
                                  reads=[sgb])
                c.barrier()
            c.es = ges

        def phase_B(l):
            with ExitStack() as es:
                c.es = es
                NT = max(d * (S // (d * 128) + 1) for S in seqs for (_, d) in A_GROUPS)
                QT = Rot([c.sb("QT", [128, SMAX]) for _ in range(2)])
                KT = Rot([c.sb("KT", [128, SMAX + 2 * PAD]) for _ in range(2)])
                for (qt_, qtb_) in QT.items:
                    c.op("pool", lambda e: e.memset(qt_[64:128, :], 0.0), writes=[qtb_])
                for (kt_, ktb_) in KT.items:
                    c.op("pool", lambda e: e.memset(kt_[64:128, :], 0.0), writes=[ktb_])
                VP = Rot([c.sb("VP", [128, NT, 128]) for _ in range(2)])
                BI = Rot([c.sb("bias", [128, 256]) for _ in range(2)])
                edge, edgeb = c.sb("edge", [128, 2, 256])
                c.dma("sp", edge[:], c_edge.rearrange("a p k -> p a k"), writes=[edgeb])
                for (kt, ktb) in KT.items:
                    c.op("pool", lambda e: e.memset(kt[0:64, 0:PAD], 0.0), writes=[ktb])
                Tt = Rot([c.sb("T", [128, 256]) for _ in range(2)])
                Pt = Rot([c.sb("P", [128, 256]) for _ in range(3)])
                PTt = Rot([c.sb("PT", [128, 256]) for _ in range(3)])
                sm = Rot([c.sb("sm", [128, 4]) for _ in range(5)])
                ot = Rot([c.sb("ot", [128, OAW]) for _ in range(3)])
                psS = Rot(banks[0:2])
                psT = Rot(banks[2:4])
                psO = Rot(banks[4:6])
                jobs = []
                for si, S in enumerate(seqs):
                    for g, (win, d) in enumerate(A_GROUPS):
                        for h in range(8):
                            jobs.append((si, S, g, d, h))
                st = {}

                def load(n):
                    si, S, g, d, h = jobs[n]
                    o = offs[si]
                    sub = S // d
                    nt = sub // 128
                    if h % 2 == 0:
                        vp, vpb = VP.next()
                        st["vp"] = (vp, vpb)
                        vcol = C_QA + g * 512 + (h // 2) * 128
                        vv = vp[:, 0:d * (nt + 1), :].rearrange("p (r a) c -> p r a c", r=d)
                        c.op("pool", lambda e: e.memset(vv[0:64, :, 0, :], 0.0), writes=[vpb])
                        c.op("pool", lambda e: e.memset(vv[64:128, :, nt, :], 0.0), writes=[vpb])
                        va = S_["va"]
                        c.dma("sp", vv[64:128, :, 0, :],
                              va[o:o + 64 * d, vcol:vcol + 128].rearrange("(k r) c -> k r c", r=d), writes=[vpb])
                        c.dma("sp", vv[0:64, :, nt, :],
                              va[o + S - 64 * d:o + S, vcol:vcol + 128].rearrange("(k r) c -> k r c", r=d), writes=[vpb])
                        if nt > 1:
                            for r in range(d):
                                c.dma("sp", vv[:, r, 1:nt, :],
                                      va[ssl(o + 64 * d + r, (nt - 1) * 128, d), vcol:vcol + 128]
                                      .rearrange("(a p) c -> p a c", p=128), writes=[vpb])
                    qt, qtb = QT.next()
                    kt, ktb = KT.next()
                    bi, bib = BI.next()
                    col = g * 512 + h * 64
                    c.dma("sp", qt[0:64, 0:S], S_["qaT"][col:col + 64, o:o + S], writes=[qtb])
                    c.op("pool", lambda e: e.memset(kt[0:64, PAD + S:PAD + S + PAD], 0.0), writes=[ktb])
                    c.dma("sp", kt[0:64, PAD:PAD + S], S_["kaT"][col:col + 64, o:o + S], writes=[ktb])
                    c.dma("sp", bi[:], c_abias[g * 8 + h], writes=[bib])
                    st[n] = (qt, qtb, kt, ktb, bi, bib) + st["vp"]

                tiles = []
                for n, (si, S, g, d, h) in enumerate(jobs):
                    nt = (S // d) // 128
                    first = True
                    for r in range(d):
                        for j in range(nt):
                            tiles.append((n, r, j, first))
                            first = False
                NTL = len(tiles)
                res = {}
                TS = {}

                def stage1a(k):
                    n, r, j, first = tiles[k]
                    if first:
                        if n == 0:
                            load(0)
                        if n + 1 < len(jobs):
                            load(n + 1)
                        res[n] = st.pop(n)
                        res.pop(n - 2, None)
                    si, S, g, d, h = jobs[n]
                    qt, qtb, kt, ktb, bi, bib, vp, vpb = res[n]
                    q0 = r + d * 128 * j
                    k0 = PAD + r + d * (128 * j - 64)
                    ps, psb = psS.next()
                    c.op("pe", lambda e: e.matmul(ps[:, 0:256], qt[:, ssl(q0, 128, d)], kt[:, ssl(k0, 256, d)],
                                                  start=True, stop=True), reads=[qtb, ktb], writes=[psb])
                    T, Tb = Tt.next()
                    c.op("dve", lambda e: e.scalar_tensor_tensor(out=T[:], in0=ps[:, 0:256], scalar=0.125, in1=bi[:],
                                                                 op0=ALU.mult, op1=ALU.add),
                         reads=[psb, bib], writes=[Tb])
                    nt = (S // d) // 128
                    if j == 0:
                        c.op("dve", lambda e: e.tensor_tensor(out=T[:], in0=T[:], in1=edge[:, 0, :], op=ALU.add),
                             reads=[Tb, edgeb], writes=[Tb])
                    if j == nt - 1:
                        c.op("dve", lambda e: e.tensor_tensor(out=T[:], in0=T[:], in1=edge[:, 1, :], op=ALU.add),
                             reads=[Tb, edgeb], writes=[Tb])
                    TS[k] = {"T": (T, Tb), "s4": sm.next(), "P": Pt.next()}

                def stage1b(k):
                    T, Tb = TS[k]["T"]
                    s4, s4b = TS[k]["s4"]
                    P, Pb = TS[k]["P"]
                    c.op("dve", lambda e: e.reduce_max(out=s4[:, 0:1], in_=T[:], axis=AX.X, negate=True),
                         reads=[Tb], writes=[s4b])
                    c.op("act", lambda e: e.activation(out=P[:], in_=T[:], func=AF.Exp, bias=s4[:, 0:1], scale=1.0,
                                                       accum_out=s4[:, 1:2]),
                         reads=[Tb, s4b], writes=[Pb, s4b])

                def stage2(k):
                    P, Pb = TS[k]["P"]
                    pt, ptb = psT.next()
                    for b in range(2):
                        c.op("pe", lambda e, b=b: e.transpose(pt[:, b * 128:(b + 1) * 128], P[:, b * 128:(b + 1) * 128], ident[:]),
                             reads=[Pb, identb], writes=[ptb], last=(b == 1))
                    PT, PTb = PTt.next()
                    c.op("act", lambda e: e.activation(out=PT[:], in_=pt[:, 0:256], func=AF.Copy), reads=[ptb], writes=[PTb])
                    TS[k]["PT"] = (PT, PTb)

                def stage3a(k):
                    n, r, j, first = tiles[k]
                    si, S, g, d, h = jobs[n]
                    nt = (S // d) // 128
                    hh = h % 2
                    vp, vpb = res[n][6], res[n][7]
                    PT, PTb = TS[k]["PT"]
                    s4, s4b = TS[k]["s4"]
                    po, pob = psO.next()
                    for b in range(2):
                        c.op("pe", lambda e, b=b: e.matmul(po[:, 0:64], PT[:, b * 128:(b + 1) * 128],
                                                           vp[:, r * (nt + 1) + j + b, hh * 64:(hh + 1) * 64],
                                                           start=(b == 0), stop=(b == 1)),
                             reads=[PTb, vpb], writes=[pob], last=(b == 1))
                    c.op("dve", lambda e: e.reciprocal(s4[:, 2:3], s4[:, 1:2]), reads=[s4b], writes=[s4b])
                    c.op("act", lambda e: e.activation(out=s4[:, 3:4], in_=s4[:, 1:2], func=AF.Ln), reads=[s4b], writes=[s4b])
                    TS[k]["po"] = (po, pob)

                def stage3b(k):
                    n, r, j, first = tiles[k]
                    si, S, g, d, h = jobs[n]
                    o = offs[si]
                    s4, s4b = TS[k]["s4"]
                    po, pob = TS[k]["po"]
                    o_t, o_tb = ot.next()
                    c.op("dve", lambda e: e.tensor_scalar(o_t[:, 0:64], po[:, 0:64], s4[:, 2:3], None, ALU.mult),
                         reads=[pob, s4b], writes=[o_tb])
                    c.op("dve", lambda e: e.tensor_tensor(out=o_t[:, 64:65], in0=s4[:, 3:4], in1=s4[:, 0:1], op=ALU.subtract),
                         reads=[s4b], writes=[o_tb])
                    tq = o + r + d * 128 * j
                    c.dma("pool", S_["oag"][ssl(tq, 128, d), g * 8 + h, 0:65], o_t[:, 0:65], reads=[o_tb])
                    del TS[k]

                for k in range(NTL + 2):
                    if k < NTL:
                        stage1a(k)
                    if 0 <= k - 2 < NTL:
                        stage3a(k - 2)
                    if k < NTL:
                        stage1b(k)
                    if 0 <= k - 1 < NTL:
                        stage2(k - 1)
                    if 0 <= k - 2 < NTL:
                        stage3b(k - 2)
                c.barrier()
            c.es = ges

        def phase_C(l, direction):
            fwd = (direction == 0)
            LNS = float(np.log(128.0 ** -0.5))
            with ExitStack() as es:
                c.es = es
                waug, waugb = c.sb("waug", [32, 512])
                c.op("pool", lambda e: e.memset(waug[:], 0.0), writes=[waugb])
                c.dma("sp", waug[0:16, :], W["gwf" if fwd else "gwb"][l], writes=[waugb])
                c.dma("sp", waug[16:17, :], W["gbf" if fwd else "gbb"][l].unsqueeze(0), writes=[waugb])
                tri, trib = c.sb("tri", [128, 128])
                msk, mskb = c.sb("msk", [128, 128])
                c.dma("sp", tri[:], c_tri[direction], writes=[trib])
                c.dma("sp", msk[:], c_msk[direction], writes=[mskb])
                gnb, gnbb = c.sb("gnb", [128, 256])
                if fwd:
                    c.dma("sp", gnb[:], W["gng"][l].partition_broadcast(128), writes=[gnbb])
                BT = 512
                qbl = Rot([c.sb("qbl", [128, 4, BT]) for _ in range(2)])
                kbl = Rot([c.sb("kbl", [128, 4, BT]) for _ in range(2)])
                lra = Rot([c.sb("lra", [32, BT]) for _ in range(2)])
                for (t_, b_) in lra.items:
                    c.op("pool", lambda e: e.memset(t_[:], 0.0), writes=[b_])
                    c.dma("sp", t_[16:17, :], c_ones, writes=[b_])
                vts = Rot([c.sb("vt", [128, 1024]) for _ in range(3)])
                rbs = Rot([c.sb("rbt", [128, 1024]) for _ in range(3)]) if fwd else None
                obs = Rot([c.sb("obt", [128, 1024]) for _ in range(3)]) if fwd else None
                e1s = Rot([c.sb("e1", [128, 512]) for _ in range(2)])
                Ls = Rot([c.sb("L", [128, 512]) for _ in range(3)])
                eqs = Rot([c.sb("eq", [128, 512]) for _ in range(2)])
                eks = Rot([c.sb("ek", [128, 512]) for _ in range(2)])
                bls = Rot([c.sb("bl", [128, 8]) for _ in range(5)])
                qes = Rot([c.sb("qe", [128, 512]) for _ in range(4)])
                kes = Rot([c.sb("ke", [128, 512]) for _ in range(4)])
                kdTs = Rot([c.sb("kdT", [128, 512]) for _ in range(3)])
                kds = Rot([c.sb("kd", [128, 512]) for _ in range(3)])
                atts = Rot([c.sb("att", [128, 512]) for _ in range(2)])
                osb = Rot([c.sb("osb", [128, 1024]) for _ in range(2)])
                sil = Rot([c.sb("sil", [128, 1024]) for _ in range(2)]) if fwd else None
                junk, junkb = c.sb("junkc", [128, 256])
                nst = Rot([c.sb("nst", [128, 12]) for _ in range(2)])
                pend_e2 = [None]
                states = [c.sb("state", [128, 4, 256]) for _ in range(2)]
                cur = [0]
                psZ, psB_, psKD, psA = banks[0], banks[1], banks[2], banks[3]
                psO = (banks[4], banks[5])
                psSt = (banks[6], banks[7])
                lastcol = 127 if fwd else 0

                for si, S in enumerate(seqs):
                    o = offs[si]
                    nblk = S // BT
                    tiles = []
                    for bi_ in (range(nblk) if fwd else range(nblk - 1, -1, -1)):
                        tl = range(BT // 128)
                        for ti in (tl if fwd else reversed(tl)):
                            tiles.append((bi_, ti))
                    c.op("dve", lambda e: e.memset(states[cur[0]][0][:], 0.0), writes=[states[cur[0]][1]])
                    G = {}
                    blkbuf = {}

                    GS = {}

                    def g1(idx):
                        bi_, ti = tiles[idx]
                        t0 = o + bi_ * BT
                        if bi_ not in blkbuf:
                            qb_, qbb = qbl.next()
                            kb_, kbb = kbl.next()
                            lr_, lrb = lra.next()
                            c.dma("sp", qb_[:], S_["qbT"][:, t0:t0 + BT].rearrange("(h p) t -> p h t", p=128), writes=[qbb])
                            c.dma("sp", kb_[:], S_["kbT"][:, t0:t0 + BT].rearrange("(h p) t -> p h t", p=128), writes=[kbb])
                            c.dma("sp", lr_[0:16, :], S_["lrT"][direction * 16:direction * 16 + 16, t0:t0 + BT], writes=[lrb])
                            blkbuf.clear()
                            blkbuf[bi_] = (qb_, qbb, kb_, kbb, lr_, lrb)
                        qb_, qbb, kb_, kbb, lr_, lrb = blkbuf[bi_]
                        tsl = slice(ti * 128, (ti + 1) * 128)
                        pz, pzb = psZ
                        c.op("pe", lambda e: e.matmul(pz[:, :], lr_[:, tsl], waug[:], start=True, stop=True),
                             reads=[lrb, waugb], writes=[pzb])
                        e1, e1b = e1s.next()
                        Lt, Lb = Ls.next()
                        c.op("act", lambda e: e.activation(out=e1[:], in_=pz[:, :], func=AF.Exp, scale=-1.0),
                             reads=[pzb], writes=[e1b])
                        c.op("act", lambda e: e.activation(out=Lt[:], in_=e1[:], func=AF.Ln, bias=1.0),
                             reads=[e1b], writes=[Lb])
                        GS[idx] = {"blk": (qb_, qbb, kb_, kbb), "tsl": tsl, "L": (Lt, Lb), "tok": t0 + ti * 128}

                    def g2(idx):
                        d_ = GS[idx]
                        qb_, qbb, kb_, kbb = d_["blk"]
                        tsl = d_["tsl"]
                        Lt, Lb = d_["L"]
                        pb, pbb = psB_
                        for h in range(4):
                            c.op("pe", lambda e, h=h: e.matmul(pb[:, h * 128:(h + 1) * 128], Lt[:, h * 128:(h + 1) * 128], tri[:],
                                                               start=True, stop=True),
                                 reads=[Lb, trib], writes=[pbb], last=(h == 3))
                        bl, blb = bls.next()
                        pb3 = pb[:, :].rearrange("p (h t) -> p h t", h=4)
                        c.op("dve", lambda e: e.tensor_copy(bl[:, 0:4], pb3[:, :, lastcol]), reads=[pbb], writes=[blb])
                        eq, eqb = eqs.next()
                        ek, ekb = eks.next()
                        c.op("act", lambda e: e.activation(out=eq[:], in_=pb[:, :], func=AF.Exp, bias=LNS, scale=1.0),
                             reads=[pbb], writes=[eqb])
                        c.op("act", lambda e: e.activation(out=ek[:], in_=pb[:, :], func=AF.Exp, scale=-1.0),
                             reads=[pbb], writes=[ekb])
                        c.op("act", lambda e: e.activation(out=bl[:, 4:8], in_=bl[:, 0:4], func=AF.Exp),
                             reads=[blb], writes=[blb])
                        qe, qeb = qes.next()
                        ke, keb = kes.next()
                        kdT, kdTb = kdTs.next()
                        c.op("dve", lambda e: e.tensor_tensor(out=qe[:].rearrange("p (h t) -> p h t", h=4), in0=qb_[:, :, tsl],
                                                              in1=eq[:].rearrange("p (h t) -> p h t", h=4), op=ALU.mult),
                             reads=[qbb, eqb], writes=[qeb])
                        c.op("dve", lambda e: e.tensor_tensor(out=ke[:].rearrange("p (h t) -> p h t", h=4), in0=kb_[:, :, tsl],
                                                              in1=ek[:].rearrange("p (h t) -> p h t", h=4), op=ALU.mult),
                             reads=[kbb, ekb], writes=[keb])
                        for h in range(4):
                            c.op("dve", lambda e, h=h: e.tensor_scalar(kdT[:, h * 128:(h + 1) * 128], ke[:, h * 128:(h + 1) * 128],
                                                                       bl[:, 4 + h:5 + h], None, ALU.mult),
                                 reads=[keb, blb], writes=[kdTb])
                        d_.update({"bl": (bl, blb), "qe": (qe, qeb), "ke": (ke, keb), "kdT": (kdT, kdTb)})

                    def g3(idx):
                        d_ = GS.pop(idx)
                        tok = d_["tok"]
                        vt, vtb = vts.next()
                        c.dma("sp", vt[:], S_["vb"][tok:tok + 128, :], writes=[vtb])
                        ext = None
                        if fwd:
                            rbt, rbb = rbs.next()
                            obt, obb_ = obs.next()
                            c.dma("sp", rbt[:], S_["rb"][tok:tok + 128, :], writes=[rbb])
                            c.dma("sp", obt[:], S_["obb"][tok:tok + 128, :], writes=[obb_])
                            ext = (rbt, rbb, obt, obb_)
                        kdT, kdTb = d_["kdT"]
                        pk, pkb = psKD
                        for h in range(4):
                            c.op("pe", lambda e, h=h: e.transpose(pk[:, h * 128:(h + 1) * 128], kdT[:, h * 128:(h + 1) * 128], ident[:]),
                                 reads=[kdTb, identb], writes=[pkb], last=(h == 3))
                        kd, kdb = kds.next()
                        c.op("act", lambda e: e.activation(out=kd[:], in_=pk[:, :], func=AF.Copy), reads=[pkb], writes=[kdb])
                        bl, blb = d_["bl"]
                        qe, qeb = d_["qe"]
                        ke, keb = d_["ke"]
                        G[idx] = (tok, vt, vtb, ext, bl, blb, qe, qeb, ke, keb, kd, kdb)

                    def statepart(idx):
                        if CUT < 5:
                            return
                        tok, vt, vtb, ext, bl, blb, qe, qeb, ke, keb, kd, kdb = G.pop(idx)
                        state, stateb = states[cur[0]]
                        snew, snewb = states[1 - cur[0]]
                        cur[0] = 1 - cur[0]
                        pa, pab = psA
                        for h in range(4):
                            c.op("pe", lambda e, h=h: e.matmul(pa[:, h * 128:(h + 1) * 128], ke[:, h * 128:(h + 1) * 128],
                                                               qe[:, h * 128:(h + 1) * 128], start=True, stop=True),
                                 reads=[keb, qeb], writes=[pab], last=(h == 3))
                        for h in range(4):
                            pst, pstb = psSt[h // 2]
                            osl = slice((h % 2) * 256, (h % 2 + 1) * 256)
                            c.op("pe", lambda e, h=h: e.matmul(pst[:, osl], kd[:, h * 128:(h + 1) * 128], vt[:, h * 256:(h + 1) * 256],
                                                               start=True, stop=True),
                                 reads=[kdb, vtb], writes=[pstb], last=(h % 2 == 1))
                        att, attb = atts.next()
                        c.op("dve", lambda e: e.tensor_tensor(out=att[:].rearrange("p (h t) -> p h t", h=4),
                                                              in0=pa[:, :].rearrange("p (h t) -> p h t", h=4),
                                                              in1=msk[:].unsqueeze(1).to_broadcast([128, 4, 128]), op=ALU.mult),
                             reads=[pab, mskb], writes=[attb])
                        for h in range(4):
                            pst, pstb = psSt[h // 2]
                            osl = slice((h % 2) * 256, (h % 2 + 1) * 256)
                            c.op("dve", lambda e, h=h: e.scalar_tensor_tensor(out=snew[:, h, :], in0=state[:, h, :],
                                                                              scalar=bl[:, 4 + h:5 + h], in1=pst[:, osl],
                                                                              op0=ALU.mult, op1=ALU.add),
                                 reads=[stateb, blb, pstb], writes=[snewb])
                        if pend_e2[0] is not None:
                            pend_e2[0]()
                            pend_e2[0] = None
                        for h in range(4):
                            po, pob = psO[h // 2]
                            osl = slice((h % 2) * 256, (h % 2 + 1) * 256)
                            c.op("pe", lambda e, h=h: e.matmul(po[:, osl], att[:, h * 128:(h + 1) * 128], vt[:, h * 256:(h + 1) * 256],
                                                               start=True, stop=False),
                                 reads=[attb, vtb], writes=[pob], last=False)
                            c.op("pe", lambda e, h=h: e.matmul(po[:, osl], qe[:, h * 128:(h + 1) * 128], state[:, h, :],
                                                               start=False, stop=True),
                                 reads=[qeb, stateb], writes=[pob], last=(h % 2 == 1))
                        if CUT < 8:
                            return
                        ob_, obb2 = osb.next()
                        if not fwd:
                            evac(ob_[:, 0:512], psO[0][0][:, :], reads=[psO[0][1]], writes=[obb2], eng="act")
                            evac(ob_[:, 512:1024], psO[1][0][:, :], reads=[psO[1][1]], writes=[obb2], eng="dve")
                            c.dma("pool", S_["obb"][tok:tok + 128, :], ob_[:], reads=[obb2])
                        else:
                            rbt, rbb, obt, obb_ = ext
                            for k in range(2):
                                c.op("dve", lambda e, k=k: e.tensor_tensor(out=ob_[:, k * 512:(k + 1) * 512], in0=psO[k][0][:, :],
                                                                           in1=obt[:, k * 512:(k + 1) * 512], op=ALU.add),
                                     reads=[psO[k][1], obb_], writes=[obb2])
                            pend_e2[0] = lambda: epi2(tok, ob_, obb2, rbt, rbb)

                    def epi2(tok, ob_, obb2, rbt, rbb):
                        if True:
                            ns, nsb = nst.next()
                            for h in range(4):
                                c.op("act", lambda e, h=h: e.activation(out=junk[:], in_=ob_[:, h * 256:(h + 1) * 256], func=AF.Square,
                                                                        accum_out=ns[:, h:h + 1]),
                                     reads=[obb2], writes=[junkb, nsb])
                            c.op("act", lambda e: e.activation(out=ns[:, 4:8], in_=ns[:, 0:4], func=AF.Ln, scale=1.0 / 256, bias=EPS),
                                 reads=[nsb], writes=[nsb])
                            c.op("act", lambda e: e.activation(out=ns[:, 8:12], in_=ns[:, 4:8], func=AF.Exp, scale=-0.5),
                                 reads=[nsb], writes=[nsb])
                            sl_, slb_ = sil.next()
                            c.op("act", lambda e: e.activation(out=sl_[:], in_=rbt[:], func=AF.Silu), reads=[rbb], writes=[slb_])
                            c.op("pool", lambda e: e.tensor_tensor(out=sl_[:].rearrange("p (h v) -> p h v", h=4),
                                                                   in0=sl_[:].rearrange("p (h v) -> p h v", h=4),
                                                                   in1=gnb[:].unsqueeze(1).to_broadcast([128, 4, 256]), op=ALU.mult),
                                 reads=[slb_, gnbb], writes=[slb_])
                            c.op("dve", lambda e: e.tensor_tensor(out=ob_[:].rearrange("p (h v) -> p h v", h=4),
                                                                  in0=ob_[:].rearrange("p (h v) -> p h v", h=4),
                                                                  in1=ns[:, 8:12].unsqueeze(2).to_broadcast([128, 4, 256]), op=ALU.mult),
                                 reads=[obb2, nsb], writes=[obb2])
                            c.op("dve", lambda e: e.tensor_tensor(out=ob_[:], in0=ob_[:], in1=sl_[:], op=ALU.mult),
                                 reads=[obb2, slb_], writes=[obb2])
                            c.dma("pool", S_["ob"][tok:tok + 128, :], ob_[:], reads=[obb2])

                    NTI = len(tiles)
                    pend_e2[0] = None
                    for it in range(-3, NTI):
                        if 0 <= it + 3 < NTI:
                            g1(it + 3)
                        if 0 <= it + 2 < NTI:
                            g2(it + 2)
                        if 0 <= it + 1 < NTI:
                            g3(it + 1)
                        if 0 <= it < NTI:
                            statepart(it)
                    if pend_e2[0] is not None:
                        pend_e2[0]()
                        pend_e2[0] = None
                c.barrier()
            c.es = ges

        def phase_D(l, xsrc):
            BT = 512
            with ExitStack() as es:
                c.es = es
                wo, wob = c.sb("wo", [128, 8, D])
                c.dma("sp", wo[:], W["wo"][l].rearrange("(kc p) c -> p kc c", p=128), writes=[wob])
                oags = Rot([c.sb("oag", [128, 24, OAW]) for _ in range(2)])
                obts = Rot([c.sb("obt", [128, 1024]) for _ in range(2)])
                oas = Rot([c.sb("oa", [128, 512]) for _ in range(2)])
                tm1 = Rot([c.sb("tm1", [128, 512]) for _ in range(2)])
                tm2 = Rot([c.sb("tm2", [128, 512]) for _ in range(2)])
                cs = Rot([c.sb("cs", [128, 64]) for _ in range(2)])
                oaT, oaTb = c.sb("oaT", [128, 4, BT])
                obT, obTb = c.sb("obT", [128, 8, BT])
                gaT, gaTb = c.sb("gaT", [128, 8, BT])
                gbT, gbTb = c.sb("gbT", [128, 8, BT])
                mT, mTb = c.sb("mT", [128, 8, BT])
                xt, xtb = c.sb("xtD", [128, 4, D])
                wap = Rot([c.sb("wap", [128, 4, 256]) for _ in range(2)])
                wbp = Rot([c.sb("wbp", [128, 8, 256]) for _ in range(2)])
                tmpm = Rot([c.sb("tmpm", [128, 512]) for _ in range(2)])
                psrot = Rot(banks)
                nblk = TOK // BT
                jobs = [(blk, pc) for blk in range(nblk) for pc in range(4)]
                slots = {}

                def loadw(n):
                    blk, pc = jobs[n]
                    a_, ab = wap.next()
                    b_, bb = wbp.next()
                    c.dma("sp", a_[:], W["wa"][l][:, pc * 256:(pc + 1) * 256].rearrange("(kc p) c -> p kc c", p=128), writes=[ab])
                    c.dma("sp", b_[:], W["wb"][l][:, pc * 256:(pc + 1) * 256].rearrange("(kc p) c -> p kc c", p=128), writes=[bb])
                    slots[n] = (a_, ab, b_, bb)

                def gates(blk):
                    t0 = blk * BT
                    c.dma("sp", gaT[:], S_["gaT"][:, t0:t0 + BT].rearrange("(kc p) t -> p kc t", p=128), writes=[gaTb])
                    c.dma("sp", gbT[:], S_["gbT"][:, t0:t0 + BT].rearrange("(kc p) t -> p kc t", p=128), writes=[gbTb])
                    c.op("act", lambda e: e.activation(out=gaT[:], in_=gaT[:], func=AF.Sigmoid), reads=[gaTb], writes=[gaTb])
                    c.op("act", lambda e: e.activation(out=gbT[:], in_=gbT[:], func=AF.Sigmoid), reads=[gbTb], writes=[gbTb])

                def combine(blk, ti):
                    tok = blk * BT + ti * 128
                    og, ogb = oags.next()
                    ob_, obb_ = obts.next()
                    c.dma("sp", og[:], S_["oag"][tok:tok + 128, :, :], writes=[ogb])
                    c.dma("sp", ob_[:], S_["ob"][tok:tok + 128, :], writes=[obb_])
                    cs_, csb = cs.next()
                    lse = og[:, :, 64].rearrange("p (g h) -> p g h", g=3)
                    c.op("dve", lambda e: e.tensor_tensor(out=cs_[:, 0:8], in0=lse[:, 0, :], in1=lse[:, 1, :], op=ALU.max),
                         reads=[ogb], writes=[csb])
                    c.op("dve", lambda e: e.tensor_tensor(out=cs_[:, 0:8], in0=cs_[:, 0:8], in1=lse[:, 2, :], op=ALU.max),
                         reads=[ogb, csb], writes=[csb])
                    e3 = cs_[:, 8:32].rearrange("p (g h) -> p g h", g=3)
                    c.op("dve", lambda e: e.tensor_tensor(out=e3, in0=lse, in1=cs_[:, 0:8].unsqueeze(1).to_broadcast([128, 3, 8]),
                                                          op=ALU.subtract), reads=[ogb, csb], writes=[csb])
                    c.op("act", lambda e: e.activation(out=cs_[:, 8:32], in_=cs_[:, 8:32], func=AF.Exp), reads=[csb], writes=[csb])
                    c.op("dve", lambda e: e.tensor_tensor(out=cs_[:, 32:40], in0=cs_[:, 8:16], in1=cs_[:, 16:24], op=ALU.add),
                         reads=[csb], writes=[csb])
                    c.op("dve", lambda e: e.tensor_tensor(out=cs_[:, 32:40], in0=cs_[:, 32:40], in1=cs_[:, 24:32], op=ALU.add),
                         reads=[csb], writes=[csb])
                    c.op("dve", lambda e: e.reciprocal(cs_[:, 40:48], cs_[:, 32:40]), reads=[csb], writes=[csb])
                    c.op("dve", lambda e: e.tensor_tensor(out=e3, in0=e3, in1=cs_[:, 40:48].unsqueeze(1).to_broadcast([128, 3, 8]),
                                                          op=ALU.mult), reads=[csb], writes=[csb])
                    oa, oab = oas.next()
                    t1, t1b = tm1.next()
                    t2, t2b = tm2.next()

                    def wmul(eng, dst, dstb, g):
                        c.op(eng, lambda e: e.tensor_tensor(out=dst[:].rearrange("p (h x) -> p h x", h=8),
                                                            in0=og[:, g * 8:(g + 1) * 8, 0:64],
                                                            in1=cs_[:, 8 + g * 8:16 + g * 8].unsqueeze(2).to_broadcast([128, 8, 64]),
                                                            op=ALU.mult), reads=[ogb, csb], writes=[dstb])
                    wmul("dve", oa, oab, 0)
                    wmul("pool", t1, t1b, 1)
                    wmul("pool", t2, t2b, 2)
                    c.op("dve", lambda e: e.tensor_tensor(out=oa[:], in0=oa[:], in1=t1[:], op=ALU.add), reads=[oab, t1b], writes=[oab])
                    c.op("dve", lambda e: e.tensor_tensor(out=oa[:], in0=oa[:], in1=t2[:], op=ALU.add), reads=[oab, t2b], writes=[oab])
                    return (oa, oab, ob_, obb_)

                def trans(ti, bufs):
                    oa, oab, ob_, obb_ = bufs
                    tsl = slice(ti * 128, (ti + 1) * 128)
                    transpose_to(oa, oab, 4, oaT, oaTb, tsl, psrot)
                    transpose_to(ob_, obb_, 8, obT, obTb, tsl, psrot)

                def branch(blk):
                    for pc in range(4):
                        n = blk * 4 + pc
                        if n + 1 < len(jobs):
                            loadw(n + 1)
                        a_, ab, b_, bb = slots.pop(n)
                        for c2 in range(2):
                            cc = pc * 2 + c2
                            pA, pAb = psrot.next()
                            for kc in range(4):
                                c.op("pe", lambda e, kc=kc: e.matmul(pA[:, :], a_[:, kc, c2 * 128:(c2 + 1) * 128], oaT[:, kc, :],
                                                                     start=(kc == 0), stop=(kc == 3)),
                                     reads=[ab, oaTb], writes=[pAb], last=(kc == 3))
                            pB, pBb = psrot.next()
                            for kc in range(8):
                                c.op("pe", lambda e, kc=kc: e.matmul(pB[:, :], b_[:, kc, c2 * 128:(c2 + 1) * 128], obT[:, kc, :],
                                                                     start=(kc == 0), stop=(kc == 7)),
                                     reads=[bb, obTb], writes=[pBb], last=(kc == 7))
                            tm, tmb = tmpm.next()
                            c.op("dve", lambda e: e.tensor_tensor(out=tm[:], in0=pA[:, :], in1=gaT[:, cc, :], op=ALU.mult),
                                 reads=[pAb, gaTb], writes=[tmb])
                            c.op("dve", lambda e: e.tensor_tensor(out=mT[:, cc, :], in0=pB[:, :], in1=gbT[:, cc, :], op=ALU.mult),
                                 reads=[pBb, gbTb], writes=[mTb])
                            c.op("pool", lambda e: e.tensor_tensor(out=mT[:, cc, :], in0=mT[:, cc, :], in1=tm[:], op=ALU.add),
                                 reads=[mTb, tmb], writes=[mTb])

                def outproj(ti):
                    for half in range(2):
                        pX, pXb = psrot.next()
                        for cc in range(8):
                            c.op("pe", lambda e, cc=cc: e.matmul(pX[:, :], mT[:, cc, ti * 128:(ti + 1) * 128],
                                                                 wo[:, cc, half * 512:(half + 1) * 512],
                                                                 start=(cc == 0), stop=(cc == 7)),
                                 reads=[mTb, wob], writes=[pXb], last=(cc == 7))
                        c.op("dve", lambda e: e.tensor_tensor(out=xt[:, ti, half * 512:(half + 1) * 512], in0=pX[:, :],
                                                              in1=xt[:, ti, half * 512:(half + 1) * 512], op=ALU.add),
                             reads=[pXb, xtb], writes=[xtb])

                loadw(0)
                gates(0)
                for ti in range(BT // 128):
                    trans(ti, combine(0, ti))
                for blk in range(nblk):
                    t0 = blk * BT
                    c.dma("sp", xt[:], xsrc[t0:t0 + BT, :].rearrange("(a p) c -> p a c", p=128), writes=[xtb])
                    branch(blk)
                    nxt = blk + 1 < nblk
                    if nxt:
                        gates(blk + 1)
                    for ti in range(BT // 128):
                        bufs = combine(blk + 1, ti) if nxt else None
                        outproj(ti)
                        if nxt:
                            trans(ti, bufs)
                    c.dma("pool", S_["xb"][t0:t0 + BT, :].rearrange("(a p) c -> p a c", p=128), xt[:], reads=[xtb])
                c.barrier()
            c.es = ges

        def phase_E(l, xdst, final):
            BT = 384
            blocks = []
            t = 0
            while t < TOK:
                n_ = min(BT, TOK - t)
                blocks.append((t, n_ // 128))
                t += n_
            with ExitStack() as es:
                c.es = es
                gbc, gbcb = c.sb("gbcE", [128, D])
                c.dma("sp", gbc[:], W["nfg"][l].partition_broadcast(128), writes=[gbcb])
                gfn, gfnb = c.sb("gfn", [128, D])
                if final:
                    c.dma("sp", gfn[:], W["fng"].partition_broadcast(128), writes=[gfnb])
                xts = Rot([c.sb("xtE", [128, 3, D]) for _ in range(2)])
                h2h = Rot([c.sb("h2h", [128, 8, BT], F32R) for _ in range(2)])
                h2l = Rot([c.sb("h2l", [128, 8, BT], F32R) for _ in range(2)])
                hts = Rot([c.sb("htE", [128, D]) for _ in range(2)])
                sts = Rot([c.sb("stE", [128, 3]) for _ in range(2)])
                w1r = Rot([c.sb("w1r", [128, 8, 256]) for _ in range(2)])
                w1h = Rot([c.sb("w1h", [128, 8, 256], F32R) for _ in range(2)])
                w1l = Rot([c.sb("w1l", [128, 8, 256], F32R) for _ in range(2)])
                w2r = Rot([c.sb("w2r", [128, 2, D]) for _ in range(2)])
                w2h = Rot([c.sb("w2h", [128, 2, D], F32R) for _ in range(2)])
                w2l = Rot([c.sb("w2l", [128, 2, D], F32R) for _ in range(2)])
                rls = Rot([c.sb("rl", [128, BT]) for _ in range(2)])
                sqs = Rot([c.sb("sq", [128, BT]) for _ in range(2)])
                hhs = Rot([c.sb("hh", [128, BT], F32R) for _ in range(3)])
                hls = Rot([c.sb("hl", [128, BT], F32R) for _ in range(3)])
                acc = banks[0:6]
                ps67 = Rot(banks[6:8])

                def mm3(ps_ap, psb, A, B, first, lastk, inc):
                    (ah, ahb), (al, alb) = A
                    (bh, bhb), (bl_, blb_) = B
                    c.op("pe", lambda e: e.matmul(ps_ap, ah, bh, start=first, stop=False), reads=[ahb, bhb], writes=[psb], last=False)
                    c.op("pe", lambda e: e.matmul(ps_ap, ah, bl_, start=False, stop=False), reads=[ahb, blb_], writes=[psb], last=False)
                    c.op("pe", lambda e: e.matmul(ps_ap, al, bh, start=False, stop=lastk), reads=[alb, bhb], writes=[psb], last=inc)

                jobs = [(bi_, pj) for bi_ in range(len(blocks)) for pj in range(16)]
                wsl = {}

                raws = {}

                def dmaw(n):
                    if n >= len(jobs):
                        return
                    bi_, pj = jobs[n]
                    r1, r1b = w1r.next()
                    r2, r2b = w2r.next()
                    c.dma("sp", r1[:], W["w1"][l][:, pj * 256:(pj + 1) * 256].rearrange("(kc p) c -> p kc c", p=128), writes=[r1b])
                    c.dma("sp", r2[:], W["w2"][l][pj * 256:(pj + 1) * 256, :].rearrange("(fc p) c -> p fc c", p=128), writes=[r2b])
                    raws[n] = (r1, r1b, r2, r2b)

                def loadw(n):
                    r1, r1b, r2, r2b = raws.pop(n)
                    a1, a1b = w1h.next()
                    l1, l1b = w1l.next()
                    a2, a2b = w2h.next()
                    l2, l2b = w2l.next()
                    c.op("act", lambda e: e.activation(out=a1[:], in_=r1[:], func=AF.Copy), reads=[r1b], writes=[a1b])
                    c.op("dve", lambda e: e.tensor_tensor(out=l1[:], in0=r1[:], in1=a1[:].bitcast(F32), op=ALU.subtract),
                         reads=[r1b, a1b], writes=[l1b])
                    c.op("act", lambda e: e.activation(out=a2[:], in_=r2[:], func=AF.Copy), reads=[r2b], writes=[a2b])
                    c.op("dve", lambda e: e.tensor_tensor(out=l2[:], in0=r2[:], in1=a2[:].bitcast(F32), op=ALU.subtract),
                         reads=[r2b, a2b], writes=[l2b])
                    wsl[n] = (a1, a1b, l1, l1b, a2, a2b, l2, l2b)

                BS = {}

                def pro_load(bi_):
                    t0, nt = blocks[bi_]
                    xt, xtb = xts.next()
                    hh_, hhb_ = h2h.next()
                    hl_, hlb_ = h2l.next()
                    c.dma("sp", xt[:, 0:nt, :], S_["xb"][t0:t0 + nt * 128, :].rearrange("(a p) c -> p a c", p=128), writes=[xtb])
                    BS[bi_] = (xt, xtb, hh_, hhb_, hl_, hlb_)

                def pro_tile(bi_, ti):
                    xt, xtb, hh_, hhb_, hl_, hlb_ = BS[bi_]
                    ht, hb = hts.next()
                    st, stb = sts.next()
                    rms_stats(xt[:, ti, :], xtb, ht[:], hb, st, stb, D)
                    c.op("dve", lambda e: e.scalar_tensor_tensor(out=ht[:], in0=xt[:, ti, :], scalar=st[:, 2:3], in1=gbc[:],
                                                                 op0=ALU.mult, op1=ALU.mult),
                         reads=[xtb, stb, gbcb], writes=[hb])
                    tsl = slice(ti * 128, (ti + 1) * 128)
                    for c0_ in range(0, 8, 4):
                        ps, psb = ps67.next()
                        for j in range(4):
                            c.op("pe", lambda e, j=j: e.transpose(ps[:, j * 128:(j + 1) * 128],
                                                                  ht[:, (c0_ + j) * 128:(c0_ + j + 1) * 128], ident[:]),
                                 reads=[hb, identb], writes=[psb], last=(j == 3))
                        ps3 = ps[:, :].rearrange("p (a b) -> p a b", a=4)
                        c.op("act", lambda e: e.activation(out=hh_[:, c0_:c0_ + 4, tsl], in_=ps3, func=AF.Copy),
                             reads=[psb], writes=[hhb_])
                        c.op("dve", lambda e: e.tensor_tensor(out=hl_[:, c0_:c0_ + 4, tsl], in0=ps3,
                                                              in1=hh_[:, c0_:c0_ + 4, tsl].bitcast(F32), op=ALU.subtract),
                             reads=[psb, hhb_], writes=[hlb_])

                dmaw(0)
                loadw(0)
                dmaw(1)
                pro_load(0)
                for ti in range(blocks[0][1]):
                    pro_tile(0, ti)
                n = 0
                for bi_, (t0, nt) in enumerate(blocks):
                    ntok = nt * 128
                    xt, xtb, hh_, hhb_, hl_, hlb_ = BS[bi_]
                    nxt = bi_ + 1 < len(blocks)
                    pend_ff2 = None

                    def ff2(fc, c2, hbuf, wts):
                        a1, a1b, l1, l1b, a2, a2b, l2, l2b = wts
                        hh, hhb, hl, hlb = hbuf
                        for ti in range(nt):
                            p0, p0b = acc[ti * 2]
                            p1, p1b = acc[ti * 2 + 1]
                            sh = hh[:, ti * 128:(ti + 1) * 128]
                            sl_ = hl[:, ti * 128:(ti + 1) * 128]
                            m0h, m0l = a2[:, c2, 0:512], l2[:, c2, 0:512]
                            m1h, m1l = a2[:, c2, 512:1024], l2[:, c2, 512:1024]
                            first, lastk = (fc == 0), (fc == 31)
                            lastone = (ti == nt - 1)
                            c.op("pe", lambda e: e.matmul(p0[:, :], sh, m0h, start=first, stop=False), reads=[hhb, a2b], writes=[p0b], last=False)
                            c.op("pe", lambda e: e.matmul(p0[:, :], sh, m0l, start=False, stop=False), reads=[hhb, l2b], writes=[p0b], last=False)
                            c.op("pe", lambda e: e.matmul(p1[:, :], sh, m1h, start=first, stop=False), reads=[hhb, a2b], writes=[p1b], last=False)
                            c.op("pe", lambda e: e.matmul(p1[:, :], sh, m1l, start=False, stop=False), reads=[hhb, l2b], writes=[p1b], last=False)
                            c.op("pe", lambda e: e.matmul(p0[:, :], sl_, m0h, start=False, stop=lastk), reads=[hlb, a2b], writes=[p0b], last=False)
                            c.op("pe", lambda e: e.matmul(p1[:, :], sl_, m1h, start=False, stop=lastk), reads=[hlb, a2b], writes=[p1b], last=lastone)

                    for pj in range(16):
                        wts = wsl.pop(n)
                        n += 1
                        a1, a1b, l1, l1b = wts[0:4]
                        if nxt and pj == 0:
                            pro_load(bi_ + 1)
                        for c2 in range(2):
                            fc = pj * 2 + c2
                            ps, psb = ps67.next()
                            for kc in range(8):
                                mm3(ps[:, 0:ntok], psb,
                                    ((a1[:, kc, c2 * 128:(c2 + 1) * 128], a1b), (l1[:, kc, c2 * 128:(c2 + 1) * 128], l1b)),
                                    ((hh_[:, kc, 0:ntok], hhb_), (hl_[:, kc, 0:ntok], hlb_)), kc == 0, kc == 7, kc == 7)
                            if pend_ff2 is not None:
                                ff2(*pend_ff2)
                                pend_ff2 = None
                            if c2 == 0 and n < len(jobs):
                                loadw(n)
                                dmaw(n + 1)
                            r_, rb_ = rls.next()
                            q_, qb_ = sqs.next()
                            hh, hhb = hhs.next()
                            hl, hlb = hls.next()
                            c.op("act", lambda e: e.activation(out=r_[:, 0:ntok], in_=ps[:, 0:ntok], func=AF.Relu), reads=[psb], writes=[rb_])
                            c.op("dve", lambda e: e.tensor_tensor(out=q_[:, 0:ntok], in0=r_[:, 0:ntok], in1=r_[:, 0:ntok], op=ALU.mult),
                                 reads=[rb_], writes=[qb_])
                            c.op("act", lambda e: e.activation(out=hh[:, 0:ntok], in_=q_[:, 0:ntok], func=AF.Copy), reads=[qb_], writes=[hhb])
                            c.op("dve", lambda e: e.tensor_tensor(out=hl[:, 0:ntok], in0=q_[:, 0:ntok], in1=hh[:, 0:ntok].bitcast(F32),
                                                                   op=ALU.subtract), reads=[qb_, hhb], writes=[hlb])
                            pend_ff2 = (fc, c2, (hh, hhb, hl, hlb), wts)
                        if nxt and 9 <= pj < 9 + blocks[bi_ + 1][1]:
                            pro_tile(bi_ + 1, pj - 9)
                    ff2(*pend_ff2)
                    pend_ff2 = None
                    for ti in range(nt):
                        for half in range(2):
                            pa_, pab_ = acc[ti * 2 + half]
                            c.op("dve", lambda e: e.tensor_tensor(out=xt[:, ti, half * 512:(half + 1) * 512], in0=pa_[:, :],
                                                                  in1=xt[:, ti, half * 512:(half + 1) * 512], op=ALU.add),
                                 reads=[pab_, xtb], writes=[xtb])
                    if final:
                        for ti in range(nt):
                            ht, hb = hts.next()
                            st, stb = sts.next()
                            rms_stats(xt[:, ti, :], xtb, ht[:], hb, st, stb, D)
                            c.op("dve", lambda e: e.scalar_tensor_tensor(out=ht[:], in0=xt[:, ti, :], scalar=st[:, 2:3], in1=gfn[:],
                                                                         op0=ALU.mult, op1=ALU.mult),
                                 reads=[xtb, stb, gfnb], writes=[hb])
                            c.dma("pool", xdst[t0 + ti * 128:t0 + (ti + 1) * 128, :], ht[:], reads=[hb])
                    else:
                        c.dma("pool", xdst[t0:t0 + ntok, :].rearrange("(a p) c -> p a c", p=128), xt[:, 0:nt, :], reads=[xtb])
                    del BS[bi_]
                c.barrier()
            c.es = ges

        for l in range(depth):
            xsrc = x_in if l == 0 else S_["xa"]
            final = (l == depth - 1)
            if "A" in phases:
                phase_A(l, xsrc)
            if "B" in phases:
                phase_B(l)
            if "C" in phases:
                phase_C(l, 1)
            if "c" in phases:
                phase_C(l, 0)
            if "D" in phases:
                phase_D(l, xsrc)
            if "E" in phases:
                phase_E(l, y_out if final else S_["xa"], final)
        c.barrier(final=True)
    return nc


_PROG_CACHE = {}


def kernel(x_prompt, x_sample, norm_mix_g, w_in, gla_w_gate_fwd, gla_b_gate_fwd, gla_w_gate_bwd, gla_b_gate_bwd,
           gla_norm_g, w_branch_a, w_branch_b, w_out, norm_ffn_g, w_ff1, w_ff2, final_norm_g):
    f32 = np.float32
    xp = np.asarray(x_prompt, f32)
    xs = np.asarray(x_sample, f32)
    nb_p, sp = xp.shape[0], xp.shape[1]
    nb_s, ss = xs.shape[0], xs.shape[1]
    pp, ps_ = nb_p // N_CORES, nb_s // N_CORES
    seqs = [sp] * pp + [ss] * ps_
    key = tuple(seqs)
    if key not in _PROG_CACHE:
        _PROG_CACHE[key] = build_program(seqs)
    nc = _PROG_CACHE[key]
    shared = {
        "norm_mix_g": norm_mix_g, "w_in": w_in, "gla_w_gate_fwd": gla_w_gate_fwd, "gla_b_gate_fwd": gla_b_gate_fwd,
        "gla_w_gate_bwd": gla_w_gate_bwd, "gla_b_gate_bwd": gla_b_gate_bwd, "gla_norm_g": gla_norm_g,
        "w_branch_a": w_branch_a, "w_branch_b": w_branch_b, "w_out": w_out, "norm_ffn_g": norm_ffn_g,
        "w_ff1": w_ff1, "w_ff2": w_ff2, "final_norm_g": final_norm_g,
    }
    shared = {k: np.ascontiguousarray(np.asarray(v, f32)) for k, v in shared.items()}
    shared.update(host_consts())
    in_maps = []
    for i in range(N_CORES):
        xcore = np.concatenate([xp[i * pp:(i + 1) * pp].reshape(pp * sp, D), xs[i * ps_:(i + 1) * ps_].reshape(ps_ * ss, D)], axis=0)
        m = dict(shared)
        m["x"] = np.ascontiguousarray(xcore)
        in_maps.append(m)
    res = run_bass_kernel_spmd(nc, in_maps, core_ids=list(range(N_CORES)))
    yp = np.empty((nb_p, sp, D), f32)
    ys = np.empty((nb_s, ss, D), f32)
    for i in range(N_CORES):
        y = np.asarray(res.results[i]["y"])
        yp[i * pp:(i + 1) * pp] = y[:pp * sp].reshape(pp, sp, D)
        ys[i * ps_:(i + 1) * ps_] = y[pp * sp:].reshape(ps_, ss, D)
    return (yp, ys)
```
